# Optimizing a Trainium2 kernel written in Bass

```python
import jax, jax.numpy as jnp
from jax import lax
import numpy as np

D_MODEL = 1024
BATCH = 8
SEQ = 2048
DEPTH = 2

CTX_LEN = 256
GRID_W = 64
MLA_HEADS = 4
QK_NOPE = 128
QK_ROPE = 64
V_DIM = 128
Q_LORA = 384
KV_LORA = 256
MLA_W = MLA_HEADS * V_DIM
FNET_GROUPS = 4
FNET_W = 256
CONV_GROUPS = 4
CONV_W = 256
D_MIX = MLA_W + FNET_W + CONV_W
MLA_IN = Q_LORA + KV_LORA + QK_ROPE
D_IN = MLA_IN + FNET_W + 3 * CONV_W
D_FF = 2816
ROPE_THETA = 10000.0
Q_BLOCK = 128
EPS = 1e-6
SM_SCALE = (QK_NOPE + QK_ROPE) ** -0.5

kernel_name = "hymba_style_mla_fnet_shortconv_dit_block"


def rmsnorm(x, g):
    xf = x.astype(jnp.float32)
    y = xf * lax.rsqrt(jnp.mean(xf * xf, axis=-1, keepdims=True) + EPS)
    return (y * g.astype(jnp.float32)).astype(x.dtype)


def modulate(h, shift, scale):
    return h * (1 + scale) + shift


def dwconv3(x, w, b):
    xp = jnp.pad(x, ((0, 0), (1, 1), (0, 0)))
    return xp[:, :-2] * w[0] + xp[:, 1:-1] * w[1] + xp[:, 2:] * w[2] + b


def axial_rope_tables(n_tokens):
    rows = n_tokens // GRID_W
    r = jnp.repeat(jnp.arange(rows), GRID_W).astype(jnp.float32)
    cidx = jnp.tile(jnp.arange(GRID_W), rows).astype(jnp.float32)
    half = QK_ROPE // 2
    inv = ROPE_THETA ** (-jnp.arange(0, half, 2, dtype=jnp.float32) / half)
    ang = jnp.stack([r[:, None] * inv, cidx[:, None] * inv], axis=1)
    return jnp.cos(ang), jnp.sin(ang)


def apply_axial_rope(x, cos, sin):
    shp = x.shape
    xf = x.astype(jnp.float32).reshape(shp[:-1] + (2, 2, QK_ROPE // 4))
    x1, x2 = xf[..., 0, :], xf[..., 1, :]
    out = jnp.stack([x1 * cos - x2 * sin, x1 * sin + x2 * cos], axis=-2)
    return out.reshape(shp).astype(x.dtype)


def mla_project(p, w_uq, q_norm_g, w_ukv, kv_norm_g, rope):
    cq = rmsnorm(p[..., :Q_LORA], q_norm_g)
    ckv = rmsnorm(p[..., Q_LORA:Q_LORA + KV_LORA], kv_norm_g)
    k_rope = p[..., Q_LORA + KV_LORA:]
    q = jnp.einsum('blr,rhd->blhd', cq, w_uq)
    kv = jnp.einsum('blr,rhd->blhd', ckv, w_ukv)
    q_nope, q_rope = q[..., :QK_NOPE], q[..., QK_NOPE:]
    k_nope, v = kv[..., :QK_NOPE], kv[..., QK_NOPE:]
    if rope is not None:
        cos, sin = rope
        q_rope = apply_axial_rope(q_rope, cos[:, None], sin[:, None])
        k_rope = apply_axial_rope(k_rope, cos, sin)
    return q_nope, q_rope, k_nope, k_rope, v


def mla_attend(q_n, q_r, k_n, k_r, v):
    s = (jnp.einsum('bqhd,bkhd->bhqk', q_n, k_n)
         + jnp.einsum('bqhr,bkr->bhqk', q_r, k_r)).astype(jnp.float32) * SM_SCALE
    p = jax.nn.softmax(s, axis=-1).astype(v.dtype)
    return jnp.einsum('bhqk,bkhd->bqhd', p, v)


def mla_attend_blocked(q_n, q_r, k_n, k_r, v):
    B, L, H, _ = q_n.shape
    nb = L // Q_BLOCK
    qn_b = jnp.moveaxis(q_n.reshape(B, nb, Q_BLOCK, H, QK_NOPE), 1, 0)
    qr_b = jnp.moveaxis(q_r.reshape(B, nb, Q_BLOCK, H, QK_ROPE), 1, 0)
    out = lax.map(lambda qs: mla_attend(qs[0], qs[1], k_n, k_r, v), (qn_b, qr_b))
    return jnp.moveaxis(out, 0, 1).reshape(B, L, H * V_DIM)


def fourier_mix(u):
    B, L, _ = u.shape
    ug = u.astype(jnp.float32).reshape(B, L, FNET_GROUPS, FNET_W // FNET_GROUPS)
    y = jnp.fft.fft2(ug, axes=(1, 3), norm='ortho').real
    return y.reshape(B, L, FNET_W).astype(u.dtype)


def short_gated_conv(p, w, b):
    bg, cg, xv = p[..., :CONV_W], p[..., CONV_W:2 * CONV_W], p[..., 2 * CONV_W:]
    return bg * dwconv3(cg * xv, w, b)


def merge_groups(att, p_rest, sconv_w, sconv_b, out_norm_g, w_out):
    y_f = fourier_mix(p_rest[..., :FNET_W])
    y_c = short_gated_conv(p_rest[..., FNET_W:], sconv_w, sconv_b)
    y = jnp.concatenate([rmsnorm(att, out_norm_g[:MLA_W]),
                         rmsnorm(y_f, out_norm_g[MLA_W:MLA_W + FNET_W]),
                         rmsnorm(y_c, out_norm_g[MLA_W + FNET_W:])], axis=-1)
    return y @ w_out


def conv_ffn(x, shift, scale, g, w_up, cw, cb, w_down):
    h = modulate(rmsnorm(x, g), shift, scale)
    u = h @ w_up
    gate = dwconv3(u[..., :D_FF], cw, cb)
    return (jax.nn.silu(gate) * u[..., D_FF:]) @ w_down


def setup_inputs(seed: int = 0) -> dict:
    key = jax.random.key(seed)
    ks = jax.random.split(key, 24)
    f32 = jnp.float32
    nrm = lambda k, shp, s: jax.random.normal(k, shp, f32) * s
    L = DEPTH
    return {
        "x": nrm(ks[0], (BATCH, SEQ, D_MODEL), 1.0),
        "c": nrm(ks[1], (BATCH, D_MODEL), 1.0),
        "ctx": nrm(ks[2], (BATCH, CTX_LEN, D_MODEL), 1.0),
        "c_ctx": nrm(ks[3], (D_MODEL,), 1.0),
        "ada_w": nrm(ks[4], (L, D_MODEL, 6 * D_MODEL), D_MODEL ** -0.5),
        "ada_b": nrm(ks[5], (L, 6 * D_MODEL), 0.02),
        "norm1_g": 1.0 + nrm(ks[6], (L, D_MODEL), 0.02),
        "w_in": nrm(ks[7], (L, D_MODEL, D_IN), D_MODEL ** -0.5),
        "q_norm_g": 1.0 + nrm(ks[8], (L, Q_LORA), 0.02),
        "kv_norm_g": 1.0 + nrm(ks[9], (L, KV_LORA), 0.02),
        "w_uq": nrm(ks[10], (L, Q_LORA, MLA_HEADS, QK_NOPE + QK_ROPE), Q_LORA ** -0.5),
        "w_ukv": nrm(ks[11], (L, KV_LORA, MLA_HEADS, QK_NOPE + V_DIM), KV_LORA ** -0.5),
        "sconv_w": nrm(ks[12], (L, 3, CONV_W), 3 ** -0.5),
        "sconv_b": nrm(ks[13], (L, CONV_W), 0.02),
        "out_norm_g": 1.0 + nrm(ks[14], (L, D_MIX), 0.02),
        "w_out": nrm(ks[15], (L, D_MIX, D_MODEL), D_MIX ** -0.5),
        "norm2_g": 1.0 + nrm(ks[16], (L, D_MODEL), 0.02),
        "w_up": nrm(ks[17], (L, D_MODEL, 2 * D_FF), D_MODEL ** -0.5),
        "ffconv_w": nrm(ks[18], (L, 3, D_FF), 3 ** -0.5),
        "ffconv_b": nrm(ks[19], (L, D_FF), 0.02),
        "w_down": nrm(ks[20], (L, D_FF, D_MODEL), D_FF ** -0.5),
        "final_g": 1.0 + nrm(ks[21], (D_MODEL,), 0.02),
    }


def reference(x, c, ctx, c_ctx, ada_w, ada_b, norm1_g, w_in, q_norm_g, kv_norm_g, w_uq, w_ukv,
              sconv_w, sconv_b, out_norm_g, w_out, norm2_g, w_up, ffconv_w, ffconv_b, w_down,
              final_g):
    B, S, _ = x.shape
    rope = axial_rope_tables(S)
    x_lat, x_ctx = x, ctx
    for l in range(DEPTH):
        last = l == DEPTH - 1
        mod_l = (jax.nn.silu(c) @ ada_w[l] + ada_b[l])[:, None, :]
        mod_c = (jax.nn.silu(c_ctx) @ ada_w[l] + ada_b[l])[None, None, :]
        sh1, sc1, ga1, sh2, sc2, ga2 = jnp.split(mod_l, 6, axis=-1)
        csh1, csc1, cga1, csh2, csc2, cga2 = jnp.split(mod_c, 6, axis=-1)

        p_lat = modulate(rmsnorm(x_lat, norm1_g[l]), sh1, sc1) @ w_in[l]
        p_ctx = modulate(rmsnorm(x_ctx, norm1_g[l]), csh1, csc1) @ w_in[l]
        ql_n, ql_r, kl_n, kl_r, vl = mla_project(p_lat[..., :MLA_IN], w_uq[l], q_norm_g[l],
                                                 w_ukv[l], kv_norm_g[l], rope)
        qc_n, qc_r, kc_n, kc_r, vc = mla_project(p_ctx[..., :MLA_IN], w_uq[l], q_norm_g[l],
                                                 w_ukv[l], kv_norm_g[l], None)
        k_n = jnp.concatenate([kc_n, kl_n], axis=1)
        k_r = jnp.concatenate([kc_r, kl_r], axis=1)
        v = jnp.concatenate([vc, vl], axis=1)
        att_lat = mla_attend_blocked(ql_n, ql_r, k_n, k_r, v)
        y_lat = merge_groups(att_lat, p_lat[..., MLA_IN:], sconv_w[l], sconv_b[l],
                             out_norm_g[l], w_out[l])
        x_lat_new = x_lat + ga1 * y_lat
        if not last:
            att_ctx = mla_attend(qc_n, qc_r, kc_n, kc_r, vc).reshape(B, -1, MLA_W)
            y_ctx = merge_groups(att_ctx, p_ctx[..., MLA_IN:], sconv_w[l], sconv_b[l],
                                 out_norm_g[l], w_out[l])
            x_ctx = x_ctx + cga1 * y_ctx
        x_lat = x_lat_new

        x_lat = x_lat + ga2 * conv_ffn(x_lat, sh2, sc2, norm2_g[l], w_up[l], ffconv_w[l],
                                       ffconv_b[l], w_down[l])
        if not last:
            x_ctx = x_ctx + cga2 * conv_ffn(x_ctx, csh2, csc2, norm2_g[l], w_up[l], ffconv_w[l],
                                            ffconv_b[l], w_down[l])
    return rmsnorm(x_lat, final_g)
```

```python
import contextlib
import itertools
import numpy as np
import ml_dtypes
import concourse.bass as bass
import concourse.mybir as mybir
from concourse.bass_utils import run_bass_kernel_spmd

F32 = mybir.dt.float32
BF16 = mybir.dt.bfloat16
AF = mybir.ActivationFunctionType
ALU = mybir.AluOpType

P = 128
D = 1024
S = 2048
CT = 256
T = S + CT
DFF = 2816
NCF = 22
DEPTH = 2
EPS = 1e-6
SM_SCALE = 192 ** -0.5
NWIN = 1920
O_G1, O_QG, O_KVG, O_SW, O_SB, O_OG, O_G2, O_FW, O_FB, O_AB = 0, 8, 11, 13, 19, 21, 29, 37, 103, 125
NSM = 173
TILES = [("c", 0, 256)] + [("l", 256 + 512 * i, 512) for i in range(4)]
H2C = {0: 1, 1: 258, 2: 258 + 512, 3: 258 + 1024, 4: 258 + 1536}
H2W = 2307


class Buf:
    __slots__ = ("lw", "rd", "name")

    def __init__(self, name=""):
        self.lw = {}
        self.rd = {}
        self.name = name


class Eng:
    def __init__(self, obj, sem, inorder=False, kind="dve"):
        self.obj = obj
        self.sem = sem
        self.cnt = 0
        self.waited = {}
        self.inorder = inorder
        self.kind = kind
        self.free = 0.0


class DmaQ:
    def __init__(self, issuer, sems):
        self.issuer = issuer
        self.sems = sems
        self.vals = [0] * len(sems)
        self.n = 0
        self.free = 0.0


class Task:
    def __init__(self, gen):
        self.gen = gen
        self.vt = 0.0


class KB:
    SEM_LAT = 120.0

    def __init__(self, nc):
        self.nc = nc
        self.engs = []
        self.qs = []
        self.fin = {}
        self.cur = None
        self.now = 0.0
        self.rr = 0
        self.warm = None
        self.round_robin = False
        self.force_ww = False

    def wait(self, eng, tok):
        sem, val = tok
        if eng.waited.get(sem.num, 0) >= val:
            return
        eng.obj.wait_ge(sem, val)
        eng.waited[sem.num] = val

    def _deps(self, eng, reads, writes, ww):
        t = 0.0
        toks = []
        for b in reads:
            toks.extend(b.lw.values())
        for b in writes:
            if ww:
                toks.extend(b.lw.values())
            toks.extend(b.rd.values())
        for tok in toks:
            same = tok[0] is eng.sem
            t = max(t, self.fin.get((tok[0].num, tok[1]), 0.0) + (0.0 if same else self.SEM_LAT))
            if not (eng.inorder and same):
                self.wait(eng, tok)
        return t

    def _mark(self, tok, reads, writes, ww=True):
        for b in reads:
            b.rd[tok[0].num] = tok
        for b in writes:
            if ww:
                b.lw = {tok[0].num: tok}
                b.rd = {}
            else:
                b.lw[tok[0].num] = tok

    def _peek(self, eng, reads, writes, ww):
        t = 0.0
        for b in reads:
            for tok in b.lw.values():
                t = max(t, self.fin.get((tok[0].num, tok[1]), 0.0) + (0.0 if tok[0] is eng.sem else self.SEM_LAT))
        for b in writes:
            for tok in (list(b.lw.values()) if ww else []) + list(b.rd.values()):
                t = max(t, self.fin.get((tok[0].num, tok[1]), 0.0) + (0.0 if tok[0] is eng.sem else self.SEM_LAT))
        return t

    def op(self, eng, fn, reads=(), writes=(), n=512, ww=True):
        ww = True if self.force_ww else ww
        if eng.inorder and self.warm is not None:
            base = max(eng.free, self.cur.vt if self.cur is not None else 0.0)
            gap = self._peek(eng, reads, writes, ww) - base
            if gap > 1500.0:
                for _ in range(min(int(0.5 * gap / 225.0), 10)):
                    self.warm()
                    eng.free = max(eng.free, base) + 225.0
                    base = eng.free
        tdep = self._deps(eng, reads, writes, ww)
        ins = fn()
        eng.cnt += 1
        ins.then_inc(eng.sem, 1)
        tok = (eng.sem, eng.cnt)
        if eng.inorder:
            cost, lat = n / 2.4 + 12.0, 120.0
        elif eng.kind == "act":
            cost, lat = 200.0 + n * 1.0, 60.0
        elif eng.kind == "pool":
            cost, lat = 150.0 + n * 2.0, 60.0
        else:
            cost, lat = 90.0 + n * 1.0, 60.0
        start = max(eng.free, tdep, self.cur.vt if self.cur is not None else 0.0)
        eng.free = start + cost
        self.fin[(tok[0].num, tok[1])] = start + cost + lat
        if self.cur is not None:
            self.cur.vt = start
        self._mark(tok, reads, writes, ww)

    def dma(self, q, out, in_, reads=(), writes=(), slow=False, nbytes=1 << 20, ww=True):
        eng = q.issuer
        tdep = self._deps(eng, reads, writes, ww)
        k = q.n % len(q.sems)
        q.n += 1
        if q.vals[k] > 0:
            self.wait(eng, (q.sems[k], q.vals[k]))
            tdep = max(tdep, self.fin.get((q.sems[k].num, q.vals[k]), 0.0))
        ins = eng.obj.dma_start(out=out, in_=in_, allow_slow_non_contiguous=True) if slow else eng.obj.dma_start(out=out, in_=in_)
        q.vals[k] += 16
        ins.then_inc(q.sems[k], 16)
        tok = (q.sems[k], q.vals[k])
        start = max(eng.free, tdep, self.cur.vt if self.cur is not None else 0.0)
        eng.free = start + 60.0
        q.free = max(q.free, start) + nbytes / 150.0
        self.fin[(tok[0].num, tok[1])] = max(q.free, start + 2200.0)
        self._mark(tok, reads, writes, ww)

    def barrier(self):
        toks = [(e.sem, e.cnt) for e in self.engs if e.cnt > 0]
        for q in self.qs:
            toks += [(s, v) for s, v in zip(q.sems, q.vals) if v > 0]
        tmax = max([e.free for e in self.engs] + [self.fin.get((t[0].num, t[1]), 0.0) for t in toks])
        for e in self.engs:
            e.free = tmax
            for tok in toks:
                if tok[0] is e.sem:
                    continue
                self.wait(e, tok)

    def run_tasks(self, gens):
        tasks = [Task(g) for g in gens if g is not None]
        t0 = max(e.free for e in self.engs) if False else min(e.free for e in self.engs)
        for t in tasks:
            t.vt = t0
        while tasks:
            t = tasks[self.rr % len(tasks)] if self.round_robin else min(tasks, key=lambda x: x.vt)
            self.rr += 1
            self.cur = t
            try:
                next(t.gen)
            except StopIteration:
                tasks.remove(t)
            self.cur = None


def run_tasks(tasks):
    act = [t for t in tasks if t is not None]
    while act:
        for t in list(act):
            try:
                next(t)
            except StopIteration:
                act.remove(t)


def build_nc(n_layers=DEPTH):
    nc = bass.Bass("TRN2", target_bir_lowering=False)
    dt = nc.dram_tensor
    x_in = dt("x", [S, D], F32, kind="ExternalInput").ap()
    ctx_in = dt("ctx", [CT, D], F32, kind="ExternalInput").ap()
    cvec_in = dt("cvec", [P, 16], F32, kind="ExternalInput").ap()
    smalls_in = dt("smalls", [DEPTH, P, NSM], F32, kind="ExternalInput").ap()
    ada_in = dt("ada", [DEPTH, 12, P, 8 * 512], F32, kind="ExternalInput").ap()
    win_in = dt("win", [DEPTH, P, 8 * NWIN], F32, kind="ExternalInput").ap()
    wuq_in = dt("wuq", [DEPTH, P, 3 * 1024], F32, kind="ExternalInput").ap()
    wukv_in = dt("wukv", [DEPTH, P, 2 * 1024], F32, kind="ExternalInput").ap()
    wout_in = dt("wout", [DEPTH, P, 8 * 1024], F32, kind="ExternalInput").ap()
    wup_in = dt("wup", [DEPTH, 11, P, 8 * 512], F32, kind="ExternalInput").ap()
    wdn_in = dt("wdn", [DEPTH, 2, P, NCF * 512], F32, kind="ExternalInput").ap()
    fg_in = dt("fg", [P, D], F32, kind="ExternalInput").ap()
    rope_in = dt("rope", [2, P, S], F32, kind="ExternalInput").ap()
    dftl_in = dt("dftl", [8, 8, P, 2 * 2 * 256], BF16, kind="ExternalInput").ap()
    dftc_in = dt("dftc", [P, 2 * 2 * 256], BF16, kind="ExternalInput").ap()
    bcs_in = dt("bcs", [P, 256], BF16, kind="ExternalInput").ap()
    idb_in = dt("idb", [P, P], BF16, kind="ExternalInput").ap()
    idf_in = dt("idf", [P, P], F32, kind="ExternalInput").ap()
    out = dt("out", [S, D], F32, kind="ExternalOutput").ap()
    xc_s = dt("xc_s", [CT, D], F32, kind="Internal").ap()
    h1_s = dt("h1_s", [P, 8, T], BF16, kind="Internal").ap()
    h2_s = dt("h2_s", [P, 8, H2W], BF16, kind="Internal").ap()
    v_s = dt("v_s", [P, 2, H2W], F32, kind="Internal").ap()
    bg_s = dt("bg_s", [P, 2, T], F32, kind="Internal").ap()
    cqn_s = dt("cqn_s", [P, 3, T], BF16, kind="Internal").ap()

    es = contextlib.ExitStack()
    with es:
        E = es.enter_context
        kb = KB(nc)
        marks = []
        nc_marks = marks
        PE = Eng(nc.tensor, E(nc.semaphore("s_pe")), inorder=True)
        ACT = Eng(nc.scalar, E(nc.semaphore("s_act")), kind="act")
        DVE = Eng(nc.vector, E(nc.semaphore("s_dve")))
        POOL = Eng(nc.gpsimd, E(nc.semaphore("s_pool")), kind="pool")
        SP = Eng(nc.sync, E(nc.semaphore("s_sp")), kind="sp")
        kb.engs = [PE, ACT, DVE, POOL, SP]
        QS = DmaQ(SP, [E(nc.semaphore(f"qs{i}")) for i in range(12)])
        QP = DmaQ(POOL, [E(nc.semaphore(f"qp{i}")) for i in range(6)])
        QA = DmaQ(ACT, [E(nc.semaphore(f"qa{i}")) for i in range(4)])
        kb.qs = [QS, QP, QA]

        nmc = [0]

        def sb(name, shape, dtype, scope=None):
            nmc[0] += 1
            return (scope or E)(nc.sbuf_tensor(f"sb{nmc[0]}_{name}", shape, dtype))

        PS = [E(nc.psum_tensor(f"ps{i}", [P, 512], F32)) for i in range(8)]
        PSB = [Buf(f"ps{i}") for i in range(8)]
        psrot = [5]

        def nextps():
            i = psrot[0]
            psrot[0] = 5 + (i - 5 + 1) % 3
            return PS[i], PSB[i]

        idb = sb("idb", [P, P], BF16)
        idf = sb("idf", [P, P], F32)
        ones = sb("ones", [P, P], BF16)
        bcs = sb("bcs", [P, 256], BF16)
        dftc = sb("dftc", [P, 2, 2, 256], BF16)
        fg = sb("fg", [P, D], F32)
        smalls = sb("smalls", [P, DEPTH, NSM], F32)
        cvec = sb("cvec", [P, 16], F32)
        csil = sb("csil", [P, 16], BF16)
        MOD = sb("mod", [P, DEPTH, 48, 2], F32)
        GS = sb("gs", [P, DEPTH, 2, 8, 2], F32)
        B_const = Buf("const")
        B_mod = Buf("mod")
        MODB = [[Buf(f"mod{l}_{g}") for g in range(12)] for l in range(DEPTH)]
        GSB = [[Buf(), Buf()] for l in range(DEPTH)]
        NSS = 16
        ss = [sb(f"ss{i}", [P, 4], F32) for i in range(NSS)]
        ssB = [Buf() for _ in range(NSS)]
        ssrot = [0]

        for dst, src in ((idb[:], idb_in), (idf[:], idf_in), (bcs[:], bcs_in), (fg[:], fg_in),
                         (cvec[:], cvec_in), (dftc[:].rearrange("p a b c -> p (a b c)"), dftc_in)):
            kb.dma(QS, dst, src, writes=[B_const], ww=False)
        for l in range(DEPTH):
            kb.dma(QS, smalls[:, l, :], smalls_in[l], writes=[B_const], ww=False)
        kb.op(POOL, lambda: nc.gpsimd.memset(ones[:], 1.0), writes=[B_const], ww=False)
        gfill = [sb(f"gfill{i}", [P, P], F32) for i in range(2)]
        gfillB = [Buf(), Buf()]
        epsc = sb("epsc", [P, 1], F32)
        kb.op(POOL, lambda: nc.gpsimd.memset(epsc[:], EPS), writes=[B_const], ww=False)
        kb.op(ACT, lambda: nc.scalar.activation(out=csil[:], in_=cvec[:], func=AF.Silu), reads=[B_const], writes=[B_mod])
        PADB = Buf("pads")
        zpad = sb("zpad", [P, 8, 2], F32)
        kb.op(POOL, lambda: nc.gpsimd.memset(zpad[:], 0.0), writes=[B_const], ww=False)
        for pc in (0, 257, 2306):
            kb.dma(QS, h2_s[:, :, pc:pc + 1], zpad[:].bitcast(BF16)[:, :, 0:1], reads=[B_const], writes=[PADB], slow=True)
            kb.dma(QS, v_s[:, :, pc:pc + 1], zpad[:, 0:2, 0:1], reads=[B_const], writes=[PADB], slow=True)
        RES = [Buf(f"res{i}") for i in range(18)]

        def res_ap(sti):
            if sti < 2:
                return xc_s[sti * 128:(sti + 1) * 128, :]
            return out[(sti - 2) * 128:(sti - 1) * 128, :]

        xt_t = [sb(f"xt{i}", [P, D], F32) for i in range(2)]
        xtB = [Buf() for _ in range(2)]
        xn_t = [sb(f"xn{i}", [P, D], BF16) for i in range(2)]
        xnB = [Buf() for _ in range(2)]
        junk = sb("junk", [P, D], BF16)
        junkB = Buf()
        hT_t = [sb(f"hT{i}", [P, 8, 512], BF16) for i in range(2)]
        hTB = [[Buf(), Buf()] for _ in range(2)]
        cnt = {"xt": 0, "xn": 0, "hT": 0, "alt": 0}

        def rstd_from(ssum_ap, ssumB, n_feat):
            i = ssrot[0]
            ssrot[0] = (i + 1) % NSS
            r, rB = ss[i], ssB[i]
            kb.op(ACT, lambda: nc.scalar.activation(out=r[:, 2:3], in_=ssum_ap, func=AF.Ln, scale=1.0 / n_feat, bias=epsc[:, 0:1]),
                  reads=[ssumB, B_const], writes=[rB], n=1)
            kb.op(ACT, lambda: nc.scalar.activation(out=r[:, 3:4], in_=r[:, 2:3], func=AF.Exp, scale=-0.5), reads=[rB], writes=[rB], n=1)
            return r, rB

        def tok_sumsq(xt, xB, junk_=None, junkB_=None):
            junk_, junkB_ = (junk, junkB) if junk_ is None else (junk_, junkB_)
            i = ssrot[0]
            ssrot[0] = (i + 1) % NSS
            r, rB = ss[i], ssB[i]
            kb.op(DVE, lambda: nc.vector.memset(r[:, 0:1], 0.0), writes=[rB], n=1)
            kb.op(ACT, lambda: nc.scalar.activation(out=junk_[:], in_=xt[:], func=AF.Square, accum_out=r[:, 0:1]),
                  reads=[xB, rB], writes=[junkB_, rB], n=1024)
            return r, rB

        def norm_to_hT(xt, xB, l, w, v, hT, hB, st, bank=None, bufs=None, evac=None):
            if bufs is None:
                r0, r0B = tok_sumsq(xt, xB)
            else:
                r0, r0B = tok_sumsq(xt, xB, bufs[2], bufs[3])
            yield
            r, rB = rstd_from(r0[:, 0:1], r0B, D)
            yield
            if bufs is None:
                i = cnt["xn"] % 2
                cnt["xn"] += 1
                xn, xnb = xn_t[i], xnB[i]
            else:
                xn, xnb = bufs[0], bufs[1]
            modb = [GSB[l][w]] + (MODB[l][0:2] if w == 0 else MODB[l][6:8])
            kb.op(DVE, lambda: nc.vector.tensor_scalar(out=xn[:], in0=xt[:], scalar1=r[:, 3:4], scalar2=0.0, op0=ALU.mult, op1=ALU.add),
                  reads=[xB, rB], writes=[xnb], n=1024)
            yield
            ps, psB = nextps() if bank is None else (PS[bank], PSB[bank])
            psb = ps[:].bitcast(BF16)
            for k in range(8):
                kb.op(PE, lambda k=k: nc.tensor.transpose(psb[:, k * 128:(k + 1) * 128], xn[:, k * 128:(k + 1) * 128], idb[:]),
                      reads=[xnb, B_const], writes=[psB], n=128)
            yield
            sh0 = 0 if w == 0 else 24
            use_act = cnt["alt"] % 3 == 0
            cnt["alt"] += 1
            if evac is not None:
                use_act = evac == "act"
            for k in range(8):
                o = hT[:, k, st * 128:(st + 1) * 128]
                i_ = psb[:, k * 128:(k + 1) * 128]
                g_ = GS[:, l, w, k, v:v + 1]
                s_ = MOD[:, l, sh0 + k, v:v + 1]
                if use_act:
                    kb.op(ACT, lambda o=o, i_=i_, g_=g_, s_=s_: nc.scalar.activation(out=o, in_=i_, func=AF.Identity, bias=s_, scale=g_),
                          reads=[psB] + modb, writes=[hB[0]], n=128)
                else:
                    kb.op(DVE, lambda o=o, i_=i_, g_=g_, s_=s_: nc.vector.tensor_scalar(out=o, in0=i_, scalar1=g_, scalar2=s_,
                                                                                     op0=ALU.mult, op1=ALU.add),
                          reads=[psB] + modb, writes=[hB[1]], n=128)
                if k % 2 == 1:
                    yield

        H1B = [Buf(f"h1_{i}") for i in range(5)]
        H2B = [Buf(f"h2_{i}") for i in range(5)]
        VSB = [Buf(f"vs_{i}") for i in range(5)]
        BGB = [Buf(f"bg_{i}") for i in range(5)]

        def get_xt():
            i = cnt["xt"] % len(xt_t)
            cnt["xt"] += 1
            return xt_t[i], xtB[i]

        def get_hT():
            i = cnt["hT"] % 2
            cnt["hT"] += 1
            return hT_t[i], hTB[i]

        def ada_gen(l, groups, aslot, aB, banks):
            ns = len(aslot)
            groups = list(groups)
            issued = 0
            for gi, g in enumerate(groups):
                while issued < len(groups) and issued < gi + ns:
                    gg = groups[issued]
                    kb.dma(QP, aslot[gg % ns][:].rearrange("p k n -> p (k n)"), ada_in[l, gg], writes=[aB[gg % ns]])
                    issued += 1
                    if issued > ns:
                        break
                if gi == 0:
                    yield
                sl, slB = aslot[g % ns], aB[g % ns]
                bk = banks[g % 2]
                psm = PS[bk][:, 0:8]
                for j in range(4):
                    for kc in range(8):
                        kb.op(PE, lambda j=j, kc=kc, sl=sl, psm=psm: nc.tensor.matmul(
                            psm[:, 2 * j:2 * j + 2], sl[:, kc, j * 128:(j + 1) * 128], csil[:, 2 * kc:2 * kc + 2],
                            start=(kc == 0), stop=(kc == 7)), reads=[slB, B_mod], writes=[PSB[bk]], n=32)
                for v in range(2):
                    kb.op(DVE, lambda v=v, g=g, psm=psm: nc.vector.tensor_tensor(
                        out=MOD[:, l, 4 * g:4 * g + 4, v], in0=psm.rearrange("p (j v) -> p j v", v=2)[:, :, v],
                        in1=smalls[:, l, O_AB + 4 * g:O_AB + 4 * g + 4], op=ALU.add),
                        reads=[PSB[bk], B_const], writes=[MODB[l][g]], n=8, ww=False)
                for w, (sc0, og, glast) in enumerate(((8, O_G1, 3), (32, O_G2, 9))):
                    if g == glast:
                        for v in range(2):
                            kb.op(DVE, lambda w=w, v=v, sc0=sc0, og=og: nc.vector.scalar_tensor_tensor(
                                out=GS[:, l, w, :, v], in0=MOD[:, l, sc0:sc0 + 8, v], scalar=1.0,
                                in1=smalls[:, l, og:og + 8], op0=ALU.add, op1=ALU.mult),
                                reads=[MODB[l][glast - 1], MODB[l][glast], B_const], writes=[GSB[l][w]], n=8, ww=False)
                yield

        marks.append(('pass0', PE.cnt))
        with contextlib.ExitStack() as s0:
            A0 = s0.enter_context
            aslot = [sb(f"aslot{i}", [P, 8, 512], BF16, A0) for i in range(2)]
            aB = [Buf(), Buf()]
            p0 = []
            for ti in range(5):
                p0.append(dict(xt=sb(f"p0xt{ti}", [P, D], F32, A0), xB=Buf(), xn=sb(f"p0xn{ti}", [P, D], BF16, A0), xnB=Buf(),
                               junk=sb(f"p0j{ti}", [P, D], BF16, A0), junkB=Buf(), hT=sb(f"p0hT{ti}", [P, 8, 512], BF16, A0), hB=[Buf(), Buf()]))

            def p0_task(ti):
                kind, c0, n = TILES[ti]
                v = 1 if kind == "c" else 0
                b = p0[ti]
                for st in range(n // 128):
                    sti = c0 // 128 + st
                    src = ctx_in[st * 128:(st + 1) * 128, :] if kind == "c" else x_in[(sti - 2) * 128:(sti - 1) * 128, :]
                    kb.dma(QS, b["xt"][:], src, writes=[b["xB"]])
                    yield from norm_to_hT(b["xt"], b["xB"], 0, 0, v, b["hT"], b["hB"], st, bank=2 + ti,
                                          bufs=(b["xn"], b["xnB"], b["junk"], b["junkB"]))
                kb.dma(QA, h1_s[:, :, c0:c0 + n], b["hT"][:, :, 0:n], reads=b["hB"], writes=[H1B[ti]])

            kb.run_tasks([ada_gen(0, range(12), aslot, aB, (0, 1))] + [p0_task(ti) for ti in range(5)])
            kb.barrier()

        for l in range(n_layers):
            last = l == DEPTH - 1

            def sm(o, n=1, l=l):
                return smalls[:, l, o:o + n]

            with contextlib.ExitStack() as s1:
                A = s1.enter_context
                KN = sb("KN", [P, 4, T], BF16, A)
                KR = sb("KR", [P, T], BF16, A)
                Vt = sb("Vt", [P, 18, 512], BF16, A)
                CQS = [Buf(f"cqs{i}") for i in range(5)]
                U = sb("U", [P, 18, 256], BF16, A)
                KVB = [Buf(f"kv{i}") for i in range(5)]
                B_rope = Buf()
                B_w = Buf("wmix")
                GA1 = sb("GA1", [P, 2, D], F32, A)
                WUQ = sb("WUQ", [P, 3, 1024], BF16, A)
                B_ga = Buf()

                def make_gate_bc(dst, ch0, dstB):
                    for v in range(2):
                        for j in range(8):
                            fill, fB = gfill[j % 2], gfillB[j % 2]
                            kb.op(ACT, lambda v=v, j=j, fill=fill: nc.scalar.activation(out=fill[:], in_=idf[:], func=AF.Identity,
                                                                                        bias=MOD[:, l, ch0 + j, v:v + 1], scale=0.0),
                                  reads=MODB[l][ch0 // 4:ch0 // 4 + 2] + [B_const], writes=[fB], n=128)
                            ps, psB = nextps()
                            kb.op(PE, lambda ps=ps, fill=fill: nc.tensor.matmul(ps[:, 0:128], fill[:], idf[:], start=True, stop=True),
                                  reads=[fB, B_const], writes=[psB], n=512)
                            kb.op(DVE, lambda ps=ps, v=v, j=j: nc.vector.tensor_copy(out=dst[:, v, j * 128:(j + 1) * 128], in_=ps[:, 0:128]),
                                  reads=[psB], writes=[dstB], n=128, ww=False)


                marks.append((f'L{l}pass1', PE.cnt))
                with contextlib.ExitStack() as s2:
                    A2 = s2.enter_context
                    WIN = sb("WIN", [P, 8, NWIN], BF16, A2)
                    WIN_SEGS = ((0, 640), (640, 1920))
                    B_wins = [Buf() for _ in WIN_SEGS]
                    for (a_, b_), wb in zip(WIN_SEGS, B_wins):
                        kb.dma(QP, WIN[:, :, a_:b_], win_in[l].rearrange("p (k n) -> p k n", n=NWIN)[:, :, a_:b_], writes=[wb])

                    def winB(col0, ncols):
                        return [wb for (a_, b_), wb in zip(WIN_SEGS, B_wins) if a_ < col0 + ncols and col0 < b_]
                    WUKV = sb("WUKV", [P, 2, 1024], BF16, A2)
                    kb.dma(QP, WUKV[:].rearrange("p k n -> p (k n)"), wukv_in[l], writes=[B_w])
                    ropeC = sb("ropeC", [P, 512], F32, A2)
                    ropeS = sb("ropeS", [P, 512], F32, A2)
                    kb.dma(QP, WUQ[:].rearrange("p k n -> p (k n)"), wuq_in[l], writes=[B_w])
                    make_gate_bc(GA1, 16, B_ga)
                    cqf = sb("cqf", [P, 3, 512], F32, A2)
                    cqf2 = sb("cqf2", [P, 2, 512], F32, A2)
                    cqf2B = Buf()
                    sq2 = [sb(f"sq2_{i}", [P, 512], BF16, A2) for i in range(2)]
                    sq2B = [Buf(), Buf()]
                    rsb2 = sb("rsb2", [P, 3, 512], F32, A2)
                    rsB2 = Buf()
                    cgs, cgB = cqf2, cqf2B

                    def mkrot(banks):
                        st_ = [0]

                        def r():
                            b = banks[st_[0] % len(banks)]
                            st_[0] += 1
                            return PS[b], PSB[b]
                        return r
                    rot1, rot2 = mkrot((0, 1, 2, 3)), mkrot((4, 5, 6))
                    cqo = [sb(f"cqo{i}", [P, 3, 512], BF16, A2) for i in range(2)]
                    cqoB = [Buf(), Buf()]
                    cqfB = Buf()
                    sq = [sb(f"sq{i}", [P, 512], BF16, A2) for i in range(3)]
                    sqB = [Buf() for _ in range(3)]
                    rsb = sb("rsb", [P, 3, 512], F32, A2)
                    rsB = Buf()
                    ckvn = sb("ckvn", [P, 2, 512], BF16, A2)
                    ckvnB = Buf()
                    tA = sb("tA", [P, 512], F32, A2)
                    tB_ = sb("tB", [P, 512], F32, A2)
                    tAB, tBB = Buf(), Buf()
                    vbuf = [sb(f"vbuf{i}", [P, 2, 512], F32, A2) for i in range(1)] * 2
                    vbB = [Buf()] * 2
                    bgbuf = [sb(f"bgbuf{i}", [P, 2, 512], F32, A2) for i in range(1)] * 2
                    bgbB = [Buf()] * 2
                    sqc = [0]

                    def feat_rstd(nchunks, nfeat, getsrc, n, ps_get=None, sqs=None, sqBs=None, rs=None, rsBuf=None):
                        sq_, sqB_ = (sq, sqB) if sqs is None else (sqs, sqBs)
                        rsb_, rsB_ = (rsb, rsB) if rs is None else (rs, rsBuf)
                        nsq = len(sq_)
                        pss, pssB = (ps_get or nextps)()
                        for j in range(nchunks):
                            src_ap, srcB = getsrc(j)
                            i = sqc[0] % nsq
                            sqc[0] += 1
                            kb.op(ACT, lambda src_ap=src_ap, i=i: nc.scalar.activation(out=sq_[i][:, :n], in_=src_ap, func=AF.Square),
                                  reads=[srcB], writes=[sqB_[i]])
                            kb.op(PE, lambda j=j, i=i: nc.tensor.matmul(pss[:, :n], ones[:], sq_[i][:, :n], start=(j == 0), stop=(j == nchunks - 1)),
                                  reads=[sqB_[i], B_const], writes=[pssB])
                        kb.op(ACT, lambda: nc.scalar.activation(out=rsb_[:, 1, :n], in_=pss[:, :n], func=AF.Ln, scale=1.0 / nfeat, bias=epsc[:, 0:1]),
                              reads=[pssB, B_const], writes=[rsB_])
                        kb.op(ACT, lambda: nc.scalar.activation(out=rsb_[:, 2, :n], in_=rsb_[:, 1, :n], func=AF.Exp, scale=-0.5), reads=[rsB_], writes=[rsB_])

                    def win_mm(ps, psB, col0, ncols, hT, hB, n, c_lo=0):
                        for kc in range(8):
                            kb.op(PE, lambda kc=kc: nc.tensor.matmul(ps[0:ncols, :n], WIN[:, kc, col0:col0 + ncols], hT[:, kc, c_lo:c_lo + n],
                                                                    start=(kc == 0), stop=(kc == 7)), reads=winB(col0, ncols) + hB, writes=[psB])

                    ada_next = None
                    if l + 1 < n_layers:
                        aslot1 = [sb(f"aslotn{i}", [P, 8, 512], BF16, A2) for i in range(2)]
                        ada_next = ada_gen(l + 1, range(12), aslot1, [Buf(), Buf()], (7, 7))
                        next(ada_next, None)
                    print('SBUF remaining pass1', nc.sbuf_bytes_remaining)
                    for ti, (kind, c0, n) in enumerate(TILES):
                        lat = kind == "l"
                        kvb = KVB[ti]
                        hT, hB = get_hT()
                        kb.dma(QS, hT[:, :, 0:n], h1_s[:, :, c0:c0 + n], reads=[H1B[ti]], writes=hB)
                        def T1():
                            nextps = rot1
                            for j in range(3):
                                yield
                                ps, psB = nextps()
                                win_mm(ps, psB, j * 128, 128, hT, hB, n)
                                kb.op(ACT, lambda ps=ps, j=j: nc.scalar.activation(out=cqf[:, j, :n], in_=ps[:, :n], func=AF.Copy),
                                      reads=[psB], writes=[cqfB])
                            feat_rstd(3, 384, lambda j: (cqf[:, j, :n], cqfB), n, ps_get=nextps)
                            for j in range(3):
                                kb.op(DVE, lambda j=j: nc.vector.scalar_tensor_tensor(out=cqo[ti % 2][:, j, :n], in0=cqf[:, j, :n], scalar=sm(O_QG + j),
                                                                                     in1=rsb[:, 2, :n], op0=ALU.mult, op1=ALU.mult),
                                      reads=[cqfB, rsB, B_const], writes=[cqoB[ti % 2]])
                            kb.dma(QP, cqn_s[:, :, c0:c0 + n], cqo[ti % 2][:, :, :n], reads=[cqoB[ti % 2]], writes=[CQS[ti]])
                            yield
                            psa, psaB = nextps()
                            win_mm(psa, psaB, 640, 128, hT, hB, n)
                            if lat:
                                psb_, psbB = nextps()
                                win_mm(psb_, psbB, 768, 128, hT, hB, n)
                                r0 = c0 - 256
                                kb.dma(QS, ropeC[:, :n], rope_in[0][:, r0:r0 + n], writes=[B_rope])
                                kb.dma(QS, ropeS[:, :n], rope_in[1][:, r0:r0 + n], writes=[B_rope])
                                kb.op(DVE, lambda: nc.vector.tensor_tensor(out=tA[:, :n], in0=psa[:, :n], in1=ropeC[:, :n], op=ALU.mult),
                                      reads=[psaB, B_rope], writes=[tAB])
                                kb.op(DVE, lambda: nc.vector.tensor_tensor(out=tB_[:, :n], in0=psb_[:, :n], in1=ropeS[:, :n], op=ALU.mult),
                                      reads=[psbB, B_rope], writes=[tBB])
                                kb.op(DVE, lambda: nc.vector.tensor_tensor(out=KR[:, c0:c0 + n], in0=tA[:, :n], in1=tB_[:, :n], op=ALU.add),
                                      reads=[tAB, tBB], writes=[kvb])
                            else:
                                kb.op(ACT, lambda: nc.scalar.activation(out=KR[:, c0:c0 + n], in_=psa[:, :n], func=AF.Copy), reads=[psaB], writes=[kvb])
                            for st in range(n // 128):
                                yield
                                ps, psB = nextps()
                                for kc in range(8):
                                    kb.op(PE, lambda kc=kc, st=st, ps=ps: nc.tensor.matmul(ps[:, 0:256], hT[:, kc, st * 128:(st + 1) * 128], WIN[:, kc, 896:1152],
                                                                                          start=(kc == 0), stop=(kc == 7)), reads=winB(896, 256) + hB, writes=[psB])
                                kt = c0 // 128 + st
                                if st % 2 == 1:
                                    kb.op(ACT, lambda kt=kt, ps=ps: nc.scalar.activation(out=U[:, kt, :], in_=ps[:, 0:256], func=AF.Copy), reads=[psB], writes=[kvb], ww=False, n=256)
                                else:
                                    kb.op(DVE, lambda kt=kt, ps=ps: nc.vector.tensor_copy(out=U[:, kt, :], in_=ps[:, 0:256]), reads=[psB], writes=[kvb], ww=False, n=256)

                        def T2():
                            nextps = rot2
                            for j in range(2):
                                yield
                                ps, psB = nextps()
                                win_mm(ps, psB, 384 + j * 128, 128, hT, hB, n)
                                kb.op(ACT, lambda ps=ps, j=j: nc.scalar.activation(out=cqf2[:, j, :n], in_=ps[:, :n], func=AF.Copy),
                                      reads=[psB], writes=[cqf2B])
                            feat_rstd(2, 256, lambda j: (cqf2[:, j, :n], cqf2B), n, ps_get=nextps, sqs=sq2, sqBs=sq2B, rs=rsb2, rsBuf=rsB2)
                            for j in range(2):
                                kb.op(DVE, lambda j=j: nc.vector.scalar_tensor_tensor(out=ckvn[:, j, :n], in0=cqf2[:, j, :n], scalar=sm(O_KVG + j),
                                                                                     in1=rsb2[:, 2, :n], op0=ALU.mult, op1=ALU.mult),
                                      reads=[cqf2B, rsB2, B_const], writes=[ckvnB])
                            for h in range(4):
                                yield
                                ps, psB = nextps()
                                for kc in range(2):
                                    kb.op(PE, lambda kc=kc, h=h, ps=ps: nc.tensor.matmul(ps[:, :n], WUKV[:, kc, h * 128:(h + 1) * 128], ckvn[:, kc, :n],
                                                                                        start=(kc == 0), stop=(kc == 1)), reads=[B_w, ckvnB], writes=[psB])
                                if h % 2 == 0:
                                    kb.op(ACT, lambda h=h, ps=ps: nc.scalar.activation(out=KN[:, h, c0:c0 + n], in_=ps[:, :n], func=AF.Copy),
                                          reads=[psB], writes=[kvb], ww=False)
                                else:
                                    kb.op(DVE, lambda h=h, ps=ps: nc.vector.tensor_copy(out=KN[:, h, c0:c0 + n], in_=ps[:, :n]), reads=[psB], writes=[kvb], ww=False)
                            for st in range(n // 128):
                                yield
                                ps, psB = nextps()
                                for kc in range(2):
                                    kb.op(PE, lambda kc=kc, st=st, ps=ps: nc.tensor.matmul(ps[:, :], ckvn[:, kc, st * 128:(st + 1) * 128], WUKV[:, kc, 512:1024],
                                                                                          start=(kc == 0), stop=(kc == 1)), reads=[B_w, ckvnB], writes=[psB])
                                kt = c0 // 128 + st
                                if st % 2 == 0:
                                    kb.op(ACT, lambda kt=kt, ps=ps: nc.scalar.activation(out=Vt[:, kt, :], in_=ps[:, :], func=AF.Copy), reads=[psB], writes=[kvb], ww=False)
                                else:
                                    kb.op(DVE, lambda kt=kt, ps=ps: nc.vector.tensor_copy(out=Vt[:, kt, :], in_=ps[:, :]), reads=[psB], writes=[kvb], ww=False)
                            vb, vbb = vbuf[ti % 2], vbB[ti % 2]
                            bgb, bgbb = bgbuf[ti % 2], bgbB[ti % 2]
                            for j in range(2):
                                yield
                                ps, psB = nextps()
                                win_mm(ps, psB, 1152 + j * 128, 128, hT, hB, n)
                                kb.op(ACT, lambda ps=ps, j=j: nc.scalar.activation(out=bgb[:, j, :n], in_=ps[:, :n], func=AF.Copy), reads=[psB], writes=[bgbb])
                            for j in range(2):
                                yield
                                ps, psB = nextps()
                                win_mm(ps, psB, 1408 + j * 128, 128, hT, hB, n)
                                kb.op(ACT, lambda ps=ps, j=j: nc.scalar.activation(out=cgs[:, j, :n], in_=ps[:, :n], func=AF.Copy), reads=[psB], writes=[cgB])
                            for j in range(2):
                                yield
                                ps, psB = nextps()
                                win_mm(ps, psB, 1664 + j * 128, 128, hT, hB, n)
                                kb.op(DVE, lambda ps=ps, j=j: nc.vector.tensor_tensor(out=vb[:, j, :n], in0=ps[:, :n], in1=cgs[:, j, :n], op=ALU.mult),
                                      reads=[psB, cgB], writes=[vbb])
                            kb.dma(QA, bg_s[:, :, c0:c0 + n], bgb[:, :, :n], reads=[bgbb], writes=[BGB[ti]])
                            kb.dma(QP, v_s[:, :, H2C[ti]:H2C[ti] + n], vb[:, :, :n], reads=[vbb], writes=[VSB[ti]])

                        kb.run_tasks([T1(), T2()] + ([itertools.islice(ada_next, 3)] if ada_next is not None else []))
                    kb.barrier()

                marks.append((f'L{l}pass2', PE.cnt))
                with contextlib.ExitStack() as s2:
                    A2 = s2.enter_context
                    WOUT = sb("WOUT", [P, 8, 1024], BF16, A2)
                    B_wout = Buf()
                    kb.dma(QP, WOUT[:].rearrange("p k n -> p (k n)"), wout_in[l], writes=[B_wout])
                    ropeC = sb("ropeC2", [P, 512], F32, A2)
                    ropeS = sb("ropeS2", [P, 512], F32, A2)
                    DT = [sb(f"DT{i}", [P, 2, 2, 256], BF16, A2) for i in range(3)]
                    DTB = [Buf() for _ in range(3)]
                    dtc = [0]
                    QNl = [sb(f"QN{i}", [P, 4, 512], BF16, A2) for i in range(2)]
                    QRl = [sb(f"QR{i}", [P, 2, 512], BF16, A2) for i in range(2)]
                    qBl = [Buf(), Buf()]
                    CQT = [sb(f"CQT{i}", [P, 3, 512], BF16, A2) for i in range(2)]
                    CQTB = [Buf(), Buf()]
                    tA = sb("tA2", [P, 512], F32, A2)
                    tB_ = sb("tB2", [P, 512], F32, A2)
                    tAB, tBB = Buf(), Buf()
                    rinv, rinvB = tA, tAB
                    PT = [sb(f"PT{i}", [P, 512], BF16, A2) for i in range(3)]
                    PTB = [Buf() for _ in range(3)]
                    SB3 = (0, 1)
                    warm_rhs = sb("warm_rhs", [P, 512], BF16, A2)
                    kb.op(POOL, lambda: nc.gpsimd.memset(warm_rhs[:], 0.0), writes=[B_const], ww=False)
                    kb.warm = lambda: nc.tensor.matmul(PS[5][:, :], ones[:], warm_rhs[:], start=True, stop=True)
                    attf = sb("attf", [P, 4, 512], F32, A2)
                    attB = Buf()
                    Yt = [sb(f"Y{i}", [P, 8, 512], BF16, A2) for i in range(2)]
                    YAB = [Buf(), Buf()]
                    YBB = [Buf(), Buf()]
                    P12 = sb("P12", [P, 4, 256], BF16, A2)
                    P12B = Buf()
                    yff = sb("yff", [P, 2, 256], F32, A2)
                    yffB = Buf()
                    vwj = sb("vwj", [P, 514], F32, A2)
                    vwB = Buf()
                    bgj = sb("bgj", [P, 512], F32, A2)
                    bgwB = Buf()
                    cacc = sb("cacc", [P, 2, 512], F32, A2)
                    caccB = Buf()
                    sqA = [sb(f"sqA{i}", [P, 512], BF16, A2) for i in range(2)]
                    sqAB = [Buf() for _ in range(2)]
                    sqBt = [sb(f"sqB{i}", [P, 512], BF16, A2) for i in range(2)]
                    sqBB = [Buf() for _ in range(2)]
                    rsA = sb("rsA", [P, 2, 512], F32, A2)
                    rsAB = Buf()
                    rsBt = sb("rsB", [P, 2, 512], F32, A2)
                    rsBB = Buf()
                    tmpx = sb("tmpx", [P, 512], F32, A2)
                    tmpxB = Buf()
                    print('SBUF remaining pass2', nc.sbuf_bytes_remaining)

                    def feat_rstd2(nchunks, nfeat, getsrc, n, bank, sqs, sqBs, rs, rsBuf, cnt_):
                        pss, pssB = PS[bank], PSB[bank]
                        for j in range(nchunks):
                            src_ap, srcB = getsrc(j)
                            i = cnt_[0] % 2
                            cnt_[0] += 1
                            kb.op(DVE, lambda src_ap=src_ap, i=i: nc.vector.tensor_tensor(out=sqs[i][:, :n], in0=src_ap, in1=src_ap, op=ALU.mult),
                                  reads=[srcB], writes=[sqBs[i]], n=n)
                            kb.op(PE, lambda j=j, i=i: nc.tensor.matmul(pss[:, :n], ones[:], sqs[i][:, :n], start=(j == 0), stop=(j == nchunks - 1)),
                                  reads=[sqBs[i], B_const], writes=[pssB], n=n)
                        kb.op(ACT, lambda: nc.scalar.activation(out=rs[:, 0, :n], in_=pss[:, :n], func=AF.Ln, scale=1.0 / nfeat, bias=epsc[:, 0:1]),
                              reads=[pssB, B_const], writes=[rsBuf])
                        kb.op(ACT, lambda: nc.scalar.activation(out=rs[:, 1, :n], in_=rs[:, 0, :n], func=AF.Exp, scale=-0.5), reads=[rsBuf], writes=[rsBuf])

                    cA, cB = [0], [0]

                    def qproj(ti, yi, b0, b1):
                        kind, c0, n = TILES[ti]
                        lat = kind == "l"
                        QN, QR, qB = QNl[yi], QRl[yi], qBl[yi]
                        cq, cqB = CQT[yi], CQTB[yi]
                        kb.dma(QS, cq[:, :, 0:n], cqn_s[:, :, c0:c0 + n], reads=[CQS[ti]], writes=[cqB])
                        if lat:
                            r0 = c0 - 256
                            kb.dma(QS, ropeC[:, :n], rope_in[0][:, r0:r0 + n], writes=[B_rope])
                            kb.dma(QS, ropeS[:, :n], rope_in[1][:, r0:r0 + n], writes=[B_rope])
                        for h in range(4):
                            ps, psB = PS[(b0, b1)[h % 2]], PSB[(b0, b1)[h % 2]]
                            for kc in range(3):
                                kb.op(PE, lambda kc=kc, h=h, ps=ps: nc.tensor.matmul(ps[:, :n], WUQ[:, kc, h * 128:(h + 1) * 128], cq[:, kc, :n],
                                                                                    start=(kc == 0), stop=(kc == 2)), reads=[B_w, cqB], writes=[psB])
                            kb.op(DVE, lambda h=h, ps=ps: nc.vector.tensor_copy(out=QN[:, h, :n], in_=ps[:, :n]), reads=[psB], writes=[qB], ww=False)
                            yield
                        for pr in range(2):
                            psa, psaB = PS[b0], PSB[b0]
                            for kc in range(3):
                                kb.op(PE, lambda kc=kc, pr=pr, psa=psa: nc.tensor.matmul(psa[:, :n], WUQ[:, kc, 512 + pr * 128:640 + pr * 128], cq[:, kc, :n],
                                                                                        start=(kc == 0), stop=(kc == 2)), reads=[B_w, cqB], writes=[psaB])
                            if lat:
                                kb.op(DVE, lambda psa=psa: nc.vector.tensor_tensor(out=rsBt[:, 0, :n], in0=psa[:, :n], in1=ropeC[:, :n], op=ALU.mult),
                                      reads=[psaB, B_rope], writes=[rsBB], ww=False)
                                psb_, psbB = PS[b1], PSB[b1]
                                for kc in range(3):
                                    kb.op(PE, lambda kc=kc, pr=pr, psb_=psb_: nc.tensor.matmul(psb_[:, :n], WUQ[:, kc, 768 + pr * 128:896 + pr * 128], cq[:, kc, :n],
                                                                                              start=(kc == 0), stop=(kc == 2)), reads=[B_w, cqB], writes=[psbB])
                                kb.op(DVE, lambda psb_=psb_: nc.vector.tensor_tensor(out=rsBt[:, 1, :n], in0=psb_[:, :n], in1=ropeS[:, :n], op=ALU.mult),
                                      reads=[psbB, B_rope], writes=[rsBB], ww=False)
                                kb.op(POOL, lambda pr=pr: nc.gpsimd.tensor_tensor(out=QR[:, pr, :n], in0=rsBt[:, 0, :n], in1=rsBt[:, 1, :n], op=ALU.add),
                                      reads=[rsBB], writes=[qB], ww=False)
                            else:
                                kb.op(DVE, lambda pr=pr, psa=psa: nc.vector.tensor_copy(out=QR[:, pr, :n], in_=psa[:, :n]), reads=[psaB], writes=[qB])
                            yield

                    def taskA(ti, yi):
                        kind, c0, n = TILES[ti]
                        lat = kind == "l"
                        nk = 18 if lat else 2
                        allkv = KVB if lat else KVB[0:1]
                        QN, QR, qB = QNl[yi], QRl[yi], qBl[yi]
                        for h in range(4):
                            kp = (h % 2) * 64

                            def scores(kt, h=h, kp=kp):
                                b = SB3[kt % len(SB3)]
                                kb.op(PE, lambda: nc.tensor.matmul(PS[b][:, :n], KN[:, h, kt * 128:(kt + 1) * 128], QN[:, h, :n], start=True, stop=False),
                                      reads=allkv + [qB], writes=[PSB[b]])
                                kb.op(PE, lambda: nc.tensor.matmul(PS[b][:, :n], KR[kp:kp + 64, kt * 128:(kt + 1) * 128], QR[kp:kp + 64, h // 2, :n],
                                                                   start=False, stop=True), reads=allkv + [qB], writes=[PSB[b]])

                            def expo(kt):
                                b = SB3[kt % len(SB3)]
                                kb.op(ACT, lambda: nc.scalar.activation(out=PT[kt % 3][:, :n], in_=PS[b][:, :n], func=AF.Exp, scale=SM_SCALE),
                                      reads=[PSB[b]], writes=[PTB[kt % 3]])

                            def pv(kt, h=h):
                                b = kt % 3
                                kb.op(PE, lambda: nc.tensor.matmul(PS[2][:, :n], Vt[:, kt, h * 128:(h + 1) * 128], PT[b][:, :n], start=(kt == 0), stop=(kt == nk - 1)),
                                      reads=allkv + [PTB[b]], writes=[PSB[2]])
                                kb.op(PE, lambda: nc.tensor.matmul(PS[3][:, :n], ones[:], PT[b][:, :n], start=(kt == 0), stop=(kt == nk - 1)),
                                      reads=[B_const, PTB[b]], writes=[PSB[3]])

                            la = len(SB3) - 1
                            for k0_ in range(min(la, nk)):
                                scores(k0_)
                            for kt in range(nk):
                                expo(kt)
                                if kt + la < nk:
                                    scores(kt + la)
                                pv(kt)
                                yield
                            kb.op(ACT, lambda: nc.scalar.activation(out=tB_[:, :n], in_=PS[3][:, :n], func=AF.Ln), reads=[PSB[3]], writes=[tBB])
                            kb.op(ACT, lambda: nc.scalar.activation(out=rinv[:, :n], in_=tB_[:, :n], func=AF.Exp, scale=-1.0), reads=[tBB], writes=[rinvB])
                            kb.op(DVE, lambda h=h: nc.vector.tensor_copy(out=attf[:, h, :n], in_=PS[2][:, :n]), reads=[PSB[2]], writes=[attB])
                            kb.op(DVE, lambda h=h: nc.vector.tensor_tensor(out=attf[:, h, :n], in0=attf[:, h, :n], in1=rinv[:, :n], op=ALU.mult),
                                  reads=[attB, rinvB], writes=[attB])

                    def taskB(ti, yi, nxt=None):
                        kind, c0, n = TILES[ti]
                        lat = kind == "l"
                        allkv = KVB if lat else KVB[0:1]
                        Y, YB = Yt[yi], YBB[yi]
                        nti = 16 if lat else 2
                        scale = (nti * 128 * 64) ** -0.5
                        for kh in range(n // 256):
                            k0 = kh * 256
                            for f in range(2):
                                for ti2 in range(nti):
                                    if lat:
                                        if ti2 % 2 == 0:
                                            di = dtc[0] % 3
                                            dtc[0] += 1
                                            kb.dma(QS, DT[di][:].rearrange("p a b c -> p (a b c)"), dftl_in[(ti - 1) * 2 + kh, ti2 // 2], writes=[DTB[di]])
                                        tab = lambda di=di, ti2=ti2: DT[di][:, ti2 % 2, :, :].rearrange("p a b -> p (a b)")
                                        tabB = DTB[di]
                                        ug = 2 + ti2
                                    else:
                                        tab = lambda ti2=ti2: dftc[:, ti2, :, :].rearrange("p a b -> p (a b)")
                                        tabB = B_const
                                        ug = ti2
                                    kb.op(PE, lambda f=f, tab=tab, ug=ug, ti2=ti2: nc.tensor.matmul(
                                        PS[4][:, :], U[:, ug, f * 128:(f + 1) * 128], tab(), start=(ti2 == 0), stop=(ti2 == nti - 1)),
                                        reads=[tabB] + allkv, writes=[PSB[4]])
                                    if ti2 % 2 == 1:
                                        yield
                                dst = P12[:, 2 * f:2 * f + 2, :].rearrange("p a b -> p (a b)")
                                kb.op(DVE, lambda dst=dst: nc.vector.tensor_scalar(out=dst, in0=PS[4][:, :], scalar1=scale, scalar2=0.0, op0=ALU.mult, op1=ALU.add),
                                      reads=[PSB[4]], writes=[P12B], ww=False)
                                kb.op(PE, lambda f=f: nc.tensor.matmul(PS[4][:, 0:256], bcs[:, 0:128], P12[:, 2 * f, :], start=True, stop=False),
                                      reads=[P12B, B_const], writes=[PSB[4]], n=256)
                                kb.op(PE, lambda f=f: nc.tensor.matmul(PS[4][:, 0:256], bcs[:, 128:256], P12[:, 2 * f + 1, :], start=False, stop=True),
                                      reads=[P12B, B_const], writes=[PSB[4]], n=256)
                                kb.op(DVE, lambda f=f: nc.vector.tensor_copy(out=yff[:, f, :], in_=PS[4][:, 0:256]), reads=[PSB[4]], writes=[yffB], n=256, ww=False)
                                yield
                            feat_rstd2(2, 256, lambda j: (yff[:, j, :], yffB), 256, 4, sqBt, sqBB, rsBt, rsBB, cB)
                            for f in range(2):
                                kb.op(DVE, lambda f=f: nc.vector.scalar_tensor_tensor(out=Y[:, 4 + f, k0:k0 + 256], in0=yff[:, f, :], scalar=sm(O_OG + 4 + f),
                                                                                     in1=rsBt[:, 1, 0:256], op0=ALU.mult, op1=ALU.mult),
                                      reads=[yffB, rsBB, B_const], writes=[YB])
                            yield
                        first = ti in (0, 1)
                        lastt = ti in (0, 4)
                        nb = [VSB[ti], PADB] + ([] if first else [VSB[ti - 1]]) + ([] if lastt else [VSB[ti + 1]])
                        lo = 1 if first else 0
                        hi = n - 1 if lastt else n
                        for j in range(2):
                            kb.dma(QS, vwj[:, 0:n + 2], v_s[:, j, H2C[ti] - 1:H2C[ti] + n + 1], reads=nb, writes=[vwB])
                            kb.dma(QS, bgj[:, 0:n], bg_s[:, j, c0:c0 + n], reads=[BGB[ti]], writes=[bgwB])
                            kb.op(POOL, lambda j=j: nc.gpsimd.tensor_scalar(out=cacc[:, j, :n], in0=vwj[:, 1:n + 1], scalar1=sm(O_SW + 3 * j + 1),
                                                                            scalar2=sm(O_SB + j), op0=ALU.mult, op1=ALU.add),
                                  reads=[vwB, B_const], writes=[caccB])
                            kb.op(DVE, lambda j=j: nc.vector.scalar_tensor_tensor(out=cacc[:, j, lo:n], in0=vwj[:, lo:n], scalar=sm(O_SW + 3 * j),
                                                                                  in1=cacc[:, j, lo:n], op0=ALU.mult, op1=ALU.add),
                                  reads=[vwB, B_const, caccB], writes=[caccB])
                            kb.op(DVE, lambda j=j: nc.vector.scalar_tensor_tensor(out=cacc[:, j, 0:hi], in0=vwj[:, 2:hi + 2], scalar=sm(O_SW + 3 * j + 2),
                                                                                  in1=cacc[:, j, 0:hi], op0=ALU.mult, op1=ALU.add),
                                  reads=[vwB, B_const, caccB], writes=[caccB])
                            kb.op(POOL, lambda j=j: nc.gpsimd.tensor_tensor(out=cacc[:, j, :n], in0=cacc[:, j, :n], in1=bgj[:, :n], op=ALU.mult),
                                  reads=[caccB, bgwB], writes=[caccB])
                            yield
                        feat_rstd2(2, 256, lambda j: (cacc[:, j, :n], caccB), n, 4, sqBt, sqBB, rsBt, rsBB, cB)
                        yield
                        for j in range(2):
                            kb.op(DVE, lambda j=j: nc.vector.scalar_tensor_tensor(out=Y[:, 6 + j, :n], in0=cacc[:, j, :n], scalar=sm(O_OG + 6 + j),
                                                                                 in1=rsBt[:, 1, :n], op0=ALU.mult, op1=ALU.mult),
                                  reads=[caccB, rsBB, B_const], writes=[YB])
                        yield

                        if nxt is not None:
                            yield from qproj(nxt[0], nxt[1], 4, 4)

                    def taskC(ti, yi):
                        kind, c0, n = TILES[ti]
                        v = 0 if kind == "l" else 1
                        Y = Yt[yi]
                        yb = [YAB[yi], YBB[yi]]
                        hT, hB = get_hT()
                        feat_rstd2(4, 512, lambda j: (attf[:, j, :n], attB), n, 6, sqA, sqAB, rsA, rsAB, cA)
                        yield
                        for h in range(4):
                            kb.op(DVE, lambda h=h: nc.vector.scalar_tensor_tensor(out=Y[:, h, :n], in0=attf[:, h, :n], scalar=sm(O_OG + h),
                                                                                 in1=rsA[:, 1, :n], op0=ALU.mult, op1=ALU.mult),
                                  reads=[attB, rsAB, B_const], writes=[YAB[yi]])
                        yield

                        pendc = None
                        for st in range(n // 128):
                            sti = c0 // 128 + st
                            xt, xB = get_xt()
                            if l == 0:
                                src0 = ctx_in[sti * 128:(sti + 1) * 128, :] if sti < 2 else x_in[(sti - 2) * 128:(sti - 1) * 128, :]
                            else:
                                src0 = res_ap(sti)
                            kb.dma(QS, xt[:], src0, reads=[RES[sti]], writes=[xB])
                            for hf in range(2):
                                ps, psB = PS[6 + hf], PSB[6 + hf]
                                for kc in range(8):
                                    kb.op(PE, lambda kc=kc, st=st, hf=hf, ps=ps: nc.tensor.matmul(ps[:, :], Y[:, kc, st * 128:(st + 1) * 128], WOUT[:, kc, hf * 512:(hf + 1) * 512],
                                                                                                 start=(kc == 0), stop=(kc == 7)), reads=yb + [B_wout], writes=[psB])
                                kb.op(DVE, lambda hf=hf, ps=ps: nc.vector.tensor_tensor(out=tmpx[:], in0=ps[:, :], in1=GA1[:, v, hf * 512:(hf + 1) * 512], op=ALU.mult),
                                      reads=[psB, B_ga], writes=[tmpxB])
                                kb.op(POOL, lambda hf=hf, xt=xt: nc.gpsimd.tensor_tensor(out=xt[:, hf * 512:(hf + 1) * 512], in0=xt[:, hf * 512:(hf + 1) * 512], in1=tmpx[:], op=ALU.add),
                                      reads=[tmpxB, xB], writes=[xB])
                                yield
                            kb.dma(QP, res_ap(sti), xt[:], reads=[xB], writes=[RES[sti]])
                            newp = norm_to_hT(xt, xB, l, 1, v, hT, hB, st, bank=6 + (st % 2), evac="dve")
                            for _ in range(3):
                                next(newp)
                                yield
                            if pendc is not None:
                                yield from pendc
                            pendc = newp
                            yield
                        if pendc is not None:
                            yield from pendc
                        kb.dma(QA, h2_s[:, :, H2C[ti]:H2C[ti] + n], hT[:, :, 0:n], reads=hB, writes=[H2B[ti]])

                    p2tiles = [1, 2, 3, 4] + ([] if last else [0])
                    prev = None
                    kb.run_tasks([qproj(p2tiles[0], 0, 4, 4)])
                    for idx, ti in enumerate(p2tiles):
                        nxt = (p2tiles[idx + 1], (idx + 1) % 2) if idx + 1 < len(p2tiles) else None
                        tasks = [taskA(ti, idx % 2), taskB(ti, idx % 2, nxt)]
                        if prev is not None:
                            tasks.append(taskC(prev[0], prev[1]))
                        kb.run_tasks(tasks)
                        prev = (ti, idx % 2)
                    kb.run_tasks([taskC(prev[0], prev[1])])
                    kb.warm = None
                    kb.barrier()

            marks.append((f'L{l}ffn', PE.cnt))
            with contextlib.ExitStack() as s1:
                A = s1.enter_context
                GA2 = sb("GA2", [P, 2, D], F32, A)
                B_ga2 = Buf()
                WU = [sb(f"WU{i}", [P, 8, 512], BF16, A) for i in range(2)]
                WUB = [Buf(), Buf()]
                WD = sb("WD", [P, 2, NCF, 512], BF16, A)
                WDB = [Buf(), Buf()]
                G = sb("G", [P, NCF, 1280], BF16, A)
                GB_ = Buf()
                h2w = sb("h2w", [P, 8, 1280 + 6], BF16, A)
                h2wBs = {t: Buf() for t in range(5)}
                gb = [sb(f"gb{i}", [P, 512], F32, A) for i in range(2)]
                gbB = [Buf(), Buf()]
                sg = [sb(f"sg{i}", [P, 512], F32, A) for i in range(2)]
                sgB = [Buf(), Buf()]
                tmpx = sb("tmpx3", [P, 512], F32, A)
                tmpxB = Buf()
                xt_t.append(sb("xt_extra", [P, D], F32, A))
                xtB.append(Buf())
                print('SBUF remaining ffn', nc.sbuf_bytes_remaining)
                supers = [[t for t in (0, 1, 2) if not (last and t == 0)], [3, 4]]
                wuc = [0]
                upc = [0]
                up_defer = [None]
                def load_windows(sup, prev=()):
                    offs, goff = {}, {}
                    o_, g_ = 0, 0
                    for t in sup:
                        kind, c0, n = TILES[t]
                        offs[t], goff[t] = o_, g_
                        first = t in (0, 1)
                        lastt = t in (0, 4)
                        nb = [H2B[t], PADB] + ([] if first else [H2B[t - 1]]) + ([] if lastt else [H2B[t + 1]])
                        kb.dma(QS, h2w[:, :, o_:o_ + n + 2], h2_s[:, :, H2C[t] - 1:H2C[t] + n + 1], reads=nb, writes=[h2wBs[t]] + [h2wBs[u] for u in prev])
                        o_ += n + 2
                        g_ += n
                    return offs, goff

                win_next = load_windows(supers[0])
                for g in range(2):
                    kb.dma(QP, WU[g][:].rearrange("p k n -> p (k n)"), wup_in[l, g], writes=[WUB[g]])
                make_gate_bc(GA2, 40, B_ga2)
                for si, sup in enumerate(supers):
                    offs, goff = win_next
                    for g in range(11):
                        wi = wuc[0] % 2
                        wuc[0] += 1
                        if not (si == 0 and g < 2):
                            kb.dma(QP, WU[wi][:].rearrange("p k n -> p (k n)"), wup_in[l, g], writes=[WUB[wi]])
                        if si == 0 and g == 1:
                            for hf in range(2):
                                kb.dma(QP, WD[:, hf].rearrange("p c n -> p (c n)"), wdn_in[l, hf], writes=[WDB[hf]])
                        for cl in range(2):
                            c = 2 * g + cl
                            for t in sup:
                                kind, c0, n = TILES[t]
                                nh = n // 2
                                o_ = offs[t]
                                first = t in (0, 1)
                                lastt = t in (0, 4)
                                bi = upc[0] % 2
                                vbk = (2, 5, 6, 7)[upc[0] % 4]
                                upc[0] += 1
                                gA, gAB = PS[3 * bi], PSB[3 * bi]
                                gBk, gBB = PS[3 * bi + 1], PSB[3 * bi + 1]
                                vv, vvB = PS[vbk], PSB[vbk]
                                for kc in range(8):
                                    wg = WU[wi][:, kc, cl * 256:cl * 256 + 128]
                                    kb.op(PE, lambda kc=kc, wg=wg: nc.tensor.matmul(gA[:, 0:nh + 2], wg, h2w[:, kc, o_:o_ + nh + 2], start=(kc == 0), stop=(kc == 7)),
                                          reads=[WUB[wi], h2wBs[t]], writes=[gAB])
                                for kc in range(8):
                                    wg = WU[wi][:, kc, cl * 256:cl * 256 + 128]
                                    kb.op(PE, lambda kc=kc, wg=wg: nc.tensor.matmul(gBk[:, 0:nh + 2], wg, h2w[:, kc, o_ + nh:o_ + n + 2], start=(kc == 0), stop=(kc == 7)),
                                          reads=[WUB[wi], h2wBs[t]], writes=[gBB])
                                for kc in range(8):
                                    wv = WU[wi][:, kc, cl * 256 + 128:cl * 256 + 256]
                                    kb.op(PE, lambda kc=kc, wv=wv: nc.tensor.matmul(vv[:, 0:n], wv, h2w[:, kc, o_ + 1:o_ + n + 1], start=(kc == 0), stop=(kc == 7)),
                                          reads=[WUB[wi], h2wBs[t]], writes=[vvB])
                                gbuf, gbb = gb[bi], gbB[bi]
                                w0, w1, w2, bb = sm(O_FW + 3 * c), sm(O_FW + 3 * c + 1), sm(O_FW + 3 * c + 2), sm(O_FB + c)
                                kb.op(ACT, lambda: nc.scalar.activation(out=gbuf[:, 0:nh], in_=gA[:, 1:nh + 1], func=AF.Identity, bias=bb, scale=w1),
                                      reads=[gAB, B_const], writes=[gbb])
                                kb.op(ACT, lambda: nc.scalar.activation(out=gbuf[:, nh:n], in_=gBk[:, 1:nh + 1], func=AF.Identity, bias=bb, scale=w1),
                                      reads=[gBB, B_const], writes=[gbb])
                                lo = 1 if first else 0
                                hi = nh - 1 if lastt else nh
                                kb.op(DVE, lambda: nc.vector.scalar_tensor_tensor(out=gbuf[:, lo:nh], in0=gA[:, lo:nh], scalar=w0, in1=gbuf[:, lo:nh], op0=ALU.mult, op1=ALU.add),
                                      reads=[gAB, B_const, gbb], writes=[gbb])
                                kb.op(DVE, lambda: nc.vector.scalar_tensor_tensor(out=gbuf[:, nh:n], in0=gBk[:, 0:nh], scalar=w0, in1=gbuf[:, nh:n], op0=ALU.mult, op1=ALU.add),
                                      reads=[gBB, B_const, gbb], writes=[gbb])
                                kb.op(DVE, lambda: nc.vector.scalar_tensor_tensor(out=gbuf[:, 0:nh], in0=gA[:, 2:nh + 2], scalar=w2, in1=gbuf[:, 0:nh], op0=ALU.mult, op1=ALU.add),
                                      reads=[gAB, B_const, gbb], writes=[gbb])
                                kb.op(DVE, lambda: nc.vector.scalar_tensor_tensor(out=gbuf[:, nh:nh + hi], in0=gBk[:, 2:hi + 2], scalar=w2, in1=gbuf[:, nh:nh + hi], op0=ALU.mult, op1=ALU.add),
                                      reads=[gBB, B_const, gbb], writes=[gbb])
                                sgb, sgbb = sg[bi], sgB[bi]
                                go = goff[t]

                                def part2(gbuf=gbuf, gbb=gbb, sgb=sgb, sgbb=sgbb, vv=vv, vvB=vvB, c=c, go=go, n=n):
                                    kb.op(ACT, lambda: nc.scalar.activation(out=sgb[:, 0:n], in_=gbuf[:, 0:n], func=AF.Silu), reads=[gbb], writes=[sgbb])
                                    kb.op(DVE, lambda: nc.vector.tensor_tensor(out=G[:, c, go:go + n], in0=vv[:, 0:n], in1=sgb[:, 0:n], op=ALU.mult),
                                          reads=[vvB, sgbb], writes=[GB_], ww=False)
                                if up_defer[0] is not None:
                                    up_defer[0]()
                                up_defer[0] = part2
                    if up_defer[0] is not None:
                        up_defer[0]()
                        up_defer[0] = None
                    if si + 1 < len(supers):
                        win_next = load_windows(supers[si + 1], prev=sup)
                    marks.append((f'L{l}down', PE.cnt))
                    for t in sup:
                        kind, c0, n = TILES[t]
                        v = 1 if kind == "c" else 0
                        if not last:
                            hT, hB = get_hT()
                        pending = None
                        for st in range(n // 128):
                            sti = c0 // 128 + st
                            xt, xB = get_xt()
                            kb.dma(QS, xt[:], res_ap(sti), reads=[RES[sti]], writes=[xB])
                            go = goff[t] + st * 128
                            for hf in range(2):
                                ps, psB = nextps()
                                for c in range(NCF):
                                    kb.op(PE, lambda c=c, hf=hf, ps=ps, go=go: nc.tensor.matmul(ps[:, :], G[:, c, go:go + 128], WD[:, hf, c, :], start=(c == 0), stop=(c == NCF - 1)),
                                          reads=[GB_, WDB[hf]], writes=[psB])
                                kb.op(DVE, lambda hf=hf, ps=ps: nc.vector.tensor_tensor(out=tmpx[:], in0=ps[:, :], in1=GA2[:, v, hf * 512:(hf + 1) * 512], op=ALU.mult),
                                      reads=[psB, B_ga2], writes=[tmpxB])
                                kb.op(DVE, lambda hf=hf, xt=xt: nc.vector.tensor_tensor(out=xt[:, hf * 512:(hf + 1) * 512], in0=xt[:, hf * 512:(hf + 1) * 512], in1=tmpx[:], op=ALU.add),
                                      reads=[tmpxB, xB], writes=[xB])
                            if last:
                                r0, r0B = tok_sumsq(xt, xB)
                                r, rB = rstd_from(r0[:, 0:1], r0B, D)
                                kb.op(DVE, lambda xt=xt, r=r: nc.vector.scalar_tensor_tensor(out=xt[:], in0=xt[:], scalar=r[:, 3:4], in1=fg[:], op0=ALU.mult, op1=ALU.mult),
                                      reads=[xB, rB, B_const], writes=[xB])
                                kb.dma(QA, res_ap(sti), xt[:], reads=[xB], writes=[RES[sti]])
                            else:
                                kb.dma(QA, res_ap(sti), xt[:], reads=[xB], writes=[RES[sti]])
                                newp = norm_to_hT(xt, xB, l + 1, 0, v, hT, hB, st)
                                for _ in range(3):
                                    next(newp)
                                if pending is not None:
                                    for _ in pending:
                                        pass
                                pending = newp
                        if not last:
                            if pending is not None:
                                for _ in pending:
                                    pass
                                pending = None
                            kb.dma(QA, h1_s[:, :, c0:c0 + n], hT[:, :, 0:n], reads=hB, writes=[H1B[t]])
                xt_t.pop()
                xtB.pop()
                kb.barrier()
        kb.barrier()
        marks.append(('end', PE.cnt))
        build_nc.marks = marks
    return nc


def _wP(w):
    K = w.shape[0] // 128
    return np.ascontiguousarray(w.reshape(K, 128, -1).transpose(1, 0, 2).reshape(128, -1))


def _chunkP(v):
    return np.ascontiguousarray(v.reshape(-1, 128).T)


_CONST_CACHE = {}


def _constants():
    if _CONST_CACHE:
        return _CONST_CACHE
    bf = ml_dtypes.bfloat16
    t = np.arange(S)
    r, c = (t // 64).astype(np.float64), (t % 64).astype(np.float64)
    inv = 10000.0 ** (-np.arange(0, 32, 2, dtype=np.float64) / 32)
    d = np.arange(64)
    a, j, i = d // 32, (d % 32) // 16, d % 16
    pos = np.where(a[:, None] == 0, r[None, :], c[None, :])
    ang = (pos.astype(np.float32) * inv.astype(np.float32)[i][:, None]).astype(np.float32).astype(np.float64)
    cosT = np.cos(ang)
    sinT = np.sin(ang) * np.where(j == 0, -1.0, 1.0)[:, None]
    rope = np.stack([np.concatenate([cosT, cosT], 0), np.concatenate([sinT, sinT], 0)]).astype(np.float32)
    tt = (np.arange(8)[:, None, None] * 256 + np.arange(2)[None, :, None] * 128 + np.arange(128)[None, None, :])
    dftl = np.empty((8, 8, 128, 2, 2, 256), dtype=bf)
    for kt in range(8):
        kk = kt * 256 + np.arange(256)
        ph = (tt[..., None] * kk[None, None, None, :]) % S
        angl = ph.astype(np.float64) * (2 * np.pi / S)
        dftl[kt, :, :, :, 0, :] = np.cos(angl).transpose(0, 2, 1, 3).astype(bf)
        dftl[kt, :, :, :, 1, :] = np.sin(angl).transpose(0, 2, 1, 3).astype(bf)
    dftl = dftl.reshape(8, 8, 128, 2 * 2 * 256)
    tc_ = np.arange(2)[:, None] * 128 + np.arange(128)[None, :]
    phc = (tc_[..., None] * np.arange(256)[None, None, :]) % CT
    angc = phc.astype(np.float64) * (2 * np.pi / CT)
    dftc = np.stack([np.cos(angc), np.sin(angc)], axis=2).transpose(1, 0, 2, 3).astype(bf).reshape(128, 2 * 2 * 256)
    m = np.arange(64)
    a64 = ((m[:, None] * m[None, :]) % 64).astype(np.float64) * (2 * np.pi / 64)
    bc = np.zeros((128, 128))
    bs = np.zeros((128, 128))
    for g in range(2):
        bc[g * 64:(g + 1) * 64, g * 64:(g + 1) * 64] = np.cos(a64)
        bs[g * 64:(g + 1) * 64, g * 64:(g + 1) * 64] = -np.sin(a64)
    bcs = np.concatenate([bc, bs], 1).astype(bf)
    _CONST_CACHE.update(rope=rope, dftl=dftl, dftc=dftc, bcs=bcs, idb=np.eye(128).astype(bf), idf=np.eye(128, dtype=np.float32))
    return _CONST_CACHE


def _prep_shared(inp):
    f = lambda k: np.asarray(inp[k], dtype=np.float32)
    d64 = np.arange(64)
    sw = np.where(d64 % 32 < 16, d64 + 16, d64 - 16)
    kr = 640 + d64
    krs = 640 + sw
    win_idx = np.concatenate([np.arange(0, 640), kr, kr, krs, krs, np.arange(704, 1728)])
    up_idx = np.concatenate([np.concatenate([np.arange(c * 128, (c + 1) * 128), DFF + np.arange(c * 128, (c + 1) * 128)]) for c in range(NCF)])
    smalls = np.zeros((DEPTH, 128, NSM), np.float32)
    ada = np.empty((DEPTH, 12, 128, 8 * 512), np.float32)
    win = np.empty((DEPTH, 128, 8 * NWIN), np.float32)
    wuq = np.empty((DEPTH, 128, 3 * 1024), np.float32)
    wukv = np.empty((DEPTH, 128, 2 * 1024), np.float32)
    wout = np.empty((DEPTH, 128, 8 * 1024), np.float32)
    wup = np.empty((DEPTH, 11, 128, 8 * 512), np.float32)
    wdn = np.empty((DEPTH, 2, 128, NCF * 512), np.float32)
    for l in range(DEPTH):
        s = smalls[l]
        s[:, O_G1:O_G1 + 8] = _chunkP(f("norm1_g")[l])
        s[:, O_QG:O_QG + 3] = _chunkP(f("q_norm_g")[l])
        s[:, O_KVG:O_KVG + 2] = _chunkP(f("kv_norm_g")[l])
        sw_ = f("sconv_w")[l]
        for j in range(2):
            for tap in range(3):
                s[:, O_SW + 3 * j + tap] = sw_[tap, j * 128:(j + 1) * 128]
        s[:, O_SB:O_SB + 2] = _chunkP(f("sconv_b")[l])
        s[:, O_OG:O_OG + 8] = _chunkP(f("out_norm_g")[l])
        s[:, O_G2:O_G2 + 8] = _chunkP(f("norm2_g")[l])
        fw = f("ffconv_w")[l]
        for c in range(NCF):
            for tap in range(3):
                s[:, O_FW + 3 * c + tap] = fw[tap, c * 128:(c + 1) * 128]
        s[:, O_FB:O_FB + NCF] = _chunkP(f("ffconv_b")[l])
        s[:, O_AB:O_AB + 48] = _chunkP(f("ada_b")[l])
        aw = f("ada_w")[l]
        for g in range(12):
            ada[l, g] = _wP(aw[:, g * 512:(g + 1) * 512])
        win[l] = _wP(f("w_in")[l][:, win_idx])
        uq = f("w_uq")[l]
        cols = [uq[:, h, 0:128] for h in range(4)]
        cols += [np.concatenate([uq[:, 2 * pr, 128:192], uq[:, 2 * pr + 1, 128:192]], 1) for pr in range(2)]
        cols += [np.concatenate([uq[:, 2 * pr, 128 + sw], uq[:, 2 * pr + 1, 128 + sw]], 1) for pr in range(2)]
        wuq[l] = _wP(np.concatenate(cols, 1))
        ukv = f("w_ukv")[l]
        wukv[l] = _wP(np.concatenate([ukv[:, h, 0:128] for h in range(4)] + [ukv[:, h, 128:256] for h in range(4)], 1))
        wout[l] = _wP(f("w_out")[l])
        upp = f("w_up")[l][:, up_idx]
        for g in range(11):
            wup[l, g] = _wP(upp[:, g * 512:(g + 1) * 512])
        dn = f("w_down")[l]
        for hf in range(2):
            wdn[l, hf] = _wP(dn[:, hf * 512:(hf + 1) * 512])
    fg = np.ascontiguousarray(np.broadcast_to(f("final_g")[None, :], (128, D)))
    return dict(smalls=smalls, ada=ada, win=win, wuq=wuq, wukv=wukv, wout=wout, wup=wup, wdn=wdn, fg=fg)


def kernel(**inputs):
    consts = _constants()
    shared = _prep_shared(inputs)
    x = np.asarray(inputs["x"], np.float32)
    c = np.asarray(inputs["c"], np.float32)
    ctx = np.asarray(inputs["ctx"], np.float32)
    c_ctx = np.asarray(inputs["c_ctx"], np.float32)
    nc = build_nc()
    in_maps = []
    for b in range(8):
        cv = np.stack([_chunkP(c[b]), _chunkP(c_ctx)], axis=2).reshape(128, 16)
        m = dict(x=np.ascontiguousarray(x[b]), ctx=np.ascontiguousarray(ctx[b]), cvec=np.ascontiguousarray(cv))
        m.update(shared)
        m.update(consts)
        in_maps.append(m)
    res = run_bass_kernel_spmd(nc, in_maps, core_ids=list(range(8)))
    return np.stack([np.asarray(r["out"], dtype=np.float32) for r in res.results], axis=0)
```

```python
import contextlib
import itertools
import numpy as np
import ml_dtypes
import concourse.bass as bass
import concourse.mybir as mybir
from concourse.bass_utils import run_bass_kernel_spmd

F32 = mybir.dt.float32
BF16 = mybir.dt.bfloat16
AF = mybir.ActivationFunctionType
ALU = mybir.AluOpType

P = 128
D = 1024
S = 2048
CT = 256
T = S + CT
DFF = 2816
NCF = 22
DEPTH = 2
EPS = 1e-6
SM_SCALE = 192 ** -0.5
NWIN = 1920
O_G1, O_QG, O_KVG, O_SW, O_SB, O_OG, O_G2, O_FW, O_FB, O_AB = 0, 8, 11, 13, 19, 21, 29, 37, 103, 125
NSM = 173
TILES = [("c", 0, 256)] + [("l", 256 + 512 * i, 512) for i in range(4)]
H2C = {0: 1, 1: 258, 2: 258 + 512, 3: 258 + 1024, 4: 258 + 1536}
H2W = 2307


class Buf:
    __slots__ = ("lw", "rd", "name")

    def __init__(self, name=""):
        self.lw = {}
        self.rd = {}
        self.name = name


class Eng:
    def __init__(self, obj, sem, inorder=False, kind="dve"):
        self.obj = obj
        self.sem = sem
        self.cnt = 0
        self.waited = {}
        self.inorder = inorder
        self.kind = kind
        self.free = 0.0


class DmaQ:
    def __init__(self, issuer, sems):
        self.issuer = issuer
        self.sems = sems
        self.vals = [0] * len(sems)
        self.n = 0
        self.free = 0.0


class Task:
    def __init__(self, gen, expected=None):
        self.gen = gen
        self.vt = 0.0
        self.pe = 0
        self.expected = expected


class KB:
    SEM_LAT = 120.0

    def __init__(self, nc):
        self.nc = nc
        self.engs = []
        self.qs = []
        self.fin = {}
        self.cur = None
        self.now = 0.0
        self.rr = 0
        self.round_robin = False
        self.force_ww = False

    def wait(self, eng, tok):
        sem, val = tok
        if eng.waited.get(sem.num, 0) >= val:
            return
        eng.obj.wait_ge(sem, val)
        eng.waited[sem.num] = val

    def _deps(self, eng, reads, writes, ww):
        t = 0.0
        toks = []
        for b in reads:
            toks.extend(b.lw.values())
        for b in writes:
            if ww:
                toks.extend(b.lw.values())
            toks.extend(b.rd.values())
        for tok in toks:
            same = tok[0] is eng.sem
            t = max(t, self.fin.get((tok[0].num, tok[1]), 0.0) + (0.0 if same else self.SEM_LAT))
            if not (eng.inorder and same):
                self.wait(eng, tok)
        return t

    def _mark(self, tok, reads, writes, ww=True):
        for b in reads:
            b.rd[tok[0].num] = tok
        for b in writes:
            if ww:
                b.lw = {tok[0].num: tok}
                b.rd = {}
            else:
                b.lw[tok[0].num] = tok

    def op(self, eng, fn, reads=(), writes=(), n=512, ww=True):
        ww = True if self.force_ww else ww
        tdep = self._deps(eng, reads, writes, ww)
        ins = fn()
        eng.cnt += 1
        ins.then_inc(eng.sem, 1)
        tok = (eng.sem, eng.cnt)
        if eng.inorder:
            cost, lat = n / 2.4 + 12.0, 120.0
        elif eng.kind == "act":
            cost, lat = 200.0 + n * 1.0, 60.0
        elif eng.kind == "pool":
            cost, lat = 150.0 + n * 2.0, 60.0
        else:
            cost, lat = 90.0 + n * 1.0, 60.0
        start = max(eng.free, tdep, self.cur.vt if self.cur is not None else 0.0)
        eng.free = start + cost
        self.fin[(tok[0].num, tok[1])] = start + cost + lat
        if self.cur is not None:
            self.cur.vt = start
            if eng.inorder:
                self.cur.pe += 1
        self._mark(tok, reads, writes, ww)

    def dma(self, q, out, in_, reads=(), writes=(), slow=False, nbytes=1 << 20, ww=True):
        eng = q.issuer
        tdep = self._deps(eng, reads, writes, ww)
        k = q.n % len(q.sems)
        q.n += 1
        if q.vals[k] > 0:
            self.wait(eng, (q.sems[k], q.vals[k]))
            tdep = max(tdep, self.fin.get((q.sems[k].num, q.vals[k]), 0.0))
        ins = eng.obj.dma_start(out=out, in_=in_, allow_slow_non_contiguous=True) if slow else eng.obj.dma_start(out=out, in_=in_)
        q.vals[k] += 16
        ins.then_inc(q.sems[k], 16)
        tok = (q.sems[k], q.vals[k])
        start = max(eng.free, tdep, self.cur.vt if self.cur is not None else 0.0)
        eng.free = start + 60.0
        q.free = max(q.free, start) + nbytes / 150.0
        self.fin[(tok[0].num, tok[1])] = max(q.free, start + 2200.0)
        self._mark(tok, reads, writes, ww)

    def barrier(self):
        toks = [(e.sem, e.cnt) for e in self.engs if e.cnt > 0]
        for q in self.qs:
            toks += [(s, v) for s, v in zip(q.sems, q.vals) if v > 0]
        tmax = max([e.free for e in self.engs] + [self.fin.get((t[0].num, t[1]), 0.0) for t in toks])
        for e in self.engs:
            e.free = tmax
            for tok in toks:
                if tok[0] is e.sem:
                    continue
                self.wait(e, tok)

    def run_tasks(self, gens, expected=None):
        tasks = [Task(g, None if expected is None else expected[i]) for i, g in enumerate(gens) if g is not None]
        t0 = max(e.free for e in self.engs) if False else min(e.free for e in self.engs)
        for t in tasks:
            t.vt = t0
        while tasks:
            if expected is not None:
                t = min(tasks, key=lambda x: (x.pe / float(x.expected), x.vt))
            else:
                t = tasks[self.rr % len(tasks)] if self.round_robin else min(tasks, key=lambda x: x.vt)
            self.rr += 1
            self.cur = t
            try:
                next(t.gen)
            except StopIteration:
                tasks.remove(t)
            self.cur = None


def run_tasks(tasks):
    act = [t for t in tasks if t is not None]
    while act:
        for t in list(act):
            try:
                next(t)
            except StopIteration:
                act.remove(t)


def build_nc(n_layers=DEPTH):
    nc = bass.Bass("TRN2", target_bir_lowering=False)
    dt = nc.dram_tensor
    x_in = dt("x", [S, D], F32, kind="ExternalInput").ap()
    ctx_in = dt("ctx", [CT, D], F32, kind="ExternalInput").ap()
    cvec_in = dt("cvec", [P, 16], F32, kind="ExternalInput").ap()
    smalls_in = dt("smalls", [DEPTH, P, NSM], F32, kind="ExternalInput").ap()
    ada_in = dt("ada", [DEPTH, 12, P, 8 * 512], F32, kind="ExternalInput").ap()
    win_in = dt("win", [DEPTH, P, 8 * NWIN], F32, kind="ExternalInput").ap()
    wuq_in = dt("wuq", [DEPTH, P, 3 * 1024], F32, kind="ExternalInput").ap()
    wukv_in = dt("wukv", [DEPTH, P, 2 * 1024], F32, kind="ExternalInput").ap()
    wout_in = dt("wout", [DEPTH, P, 8 * 1024], F32, kind="ExternalInput").ap()
    wup_in = dt("wup", [DEPTH, 11, P, 8 * 512], F32, kind="ExternalInput").ap()
    wdn_in = dt("wdn", [DEPTH, 2, P, NCF * 512], F32, kind="ExternalInput").ap()
    fg_in = dt("fg", [P, D], F32, kind="ExternalInput").ap()
    rope_in = dt("rope", [2, P, S], F32, kind="ExternalInput").ap()
    dftl_in = dt("dftl", [8, 8, P, 2 * 2 * 256], BF16, kind="ExternalInput").ap()
    dftc_in = dt("dftc", [P, 2 * 2 * 256], BF16, kind="ExternalInput").ap()
    bcs_in = dt("bcs", [P, 256], BF16, kind="ExternalInput").ap()
    idb_in = dt("idb", [P, P], BF16, kind="ExternalInput").ap()
    idf_in = dt("idf", [P, P], F32, kind="ExternalInput").ap()
    out = dt("out", [S, D], F32, kind="ExternalOutput").ap()
    xc_s = dt("xc_s", [CT, D], F32, kind="Internal").ap()
    h1_s = dt("h1_s", [P, 8, T], BF16, kind="Internal").ap()
    h2_s = dt("h2_s", [P, 8, H2W], BF16, kind="Internal").ap()
    v_s = dt("v_s", [P, 2, H2W], F32, kind="Internal").ap()
    bg_s = dt("bg_s", [P, 2, T], F32, kind="Internal").ap()
    cqn_s = dt("cqn_s", [P, 3, T], BF16, kind="Internal").ap()

    es = contextlib.ExitStack()
    with es:
        E = es.enter_context
        kb = KB(nc)
        marks = []
        nc_marks = marks
        PE = Eng(nc.tensor, E(nc.semaphore("s_pe")), inorder=True)
        ACT = Eng(nc.scalar, E(nc.semaphore("s_act")), kind="act")
        DVE = Eng(nc.vector, E(nc.semaphore("s_dve")))
        POOL = Eng(nc.gpsimd, E(nc.semaphore("s_pool")), kind="pool")
        SP = Eng(nc.sync, E(nc.semaphore("s_sp")), kind="sp")
        kb.engs = [PE, ACT, DVE, POOL, SP]
        QS = DmaQ(SP, [E(nc.semaphore(f"qs{i}")) for i in range(12)])
        QP = DmaQ(POOL, [E(nc.semaphore(f"qp{i}")) for i in range(6)])
        QA = DmaQ(ACT, [E(nc.semaphore(f"qa{i}")) for i in range(4)])
        kb.qs = [QS, QP, QA]

        nmc = [0]

        def sb(name, shape, dtype, scope=None):
            nmc[0] += 1
            return (scope or E)(nc.sbuf_tensor(f"sb{nmc[0]}_{name}", shape, dtype))

        PS = [E(nc.psum_tensor(f"ps{i}", [P, 512], F32)) for i in range(8)]
        PSB = [Buf(f"ps{i}") for i in range(8)]
        psrot = [5]

        def nextps():
            i = psrot[0]
            psrot[0] = 5 + (i - 5 + 1) % 3
            return PS[i], PSB[i]

        idb = sb("idb", [P, P], BF16)
        idf = sb("idf", [P, P], F32)
        ones = sb("ones", [P, P], BF16)
        bcs = sb("bcs", [P, 256], BF16)
        dftc = sb("dftc", [P, 2, 2, 256], BF16)
        fg = sb("fg", [P, D], F32)
        smalls = sb("smalls", [P, DEPTH, NSM], F32)
        cvec = sb("cvec", [P, 16], F32)
        csil = sb("csil", [P, 16], BF16)
        MOD = sb("mod", [P, DEPTH, 48, 2], F32)
        GS = sb("gs", [P, DEPTH, 2, 8, 2], F32)
        B_const = Buf("const")
        B_mod = Buf("mod")
        MODB = [[Buf(f"mod{l}_{g}") for g in range(12)] for l in range(DEPTH)]
        GSB = [[Buf(), Buf()] for l in range(DEPTH)]
        NSS = 16
        ss = [sb(f"ss{i}", [P, 4], F32) for i in range(NSS)]
        ssB = [Buf() for _ in range(NSS)]
        ssrot = [0]

        for dst, src in ((idb[:], idb_in), (idf[:], idf_in), (bcs[:], bcs_in), (fg[:], fg_in),
                         (cvec[:], cvec_in), (dftc[:].rearrange("p a b c -> p (a b c)"), dftc_in)):
            kb.dma(QS, dst, src, writes=[B_const], ww=False)
        for l in range(DEPTH):
            kb.dma(QS, smalls[:, l, :], smalls_in[l], writes=[B_const], ww=False)
        kb.op(POOL, lambda: nc.gpsimd.memset(ones[:], 1.0), writes=[B_const], ww=False)
        gfill = [sb(f"gfill{i}", [P, P], F32) for i in range(2)]
        gfillB = [Buf(), Buf()]
        epsc = sb("epsc", [P, 1], F32)
        kb.op(POOL, lambda: nc.gpsimd.memset(epsc[:], EPS), writes=[B_const], ww=False)
        kb.op(ACT, lambda: nc.scalar.activation(out=csil[:], in_=cvec[:], func=AF.Silu), reads=[B_const], writes=[B_mod])
        PADB = Buf("pads")
        zpad = sb("zpad", [P, 8, 2], F32)
        kb.op(POOL, lambda: nc.gpsimd.memset(zpad[:], 0.0), writes=[B_const], ww=False)
        for pc in (0, 257, 2306):
            kb.dma(QS, h2_s[:, :, pc:pc + 1], zpad[:].bitcast(BF16)[:, :, 0:1], reads=[B_const], writes=[PADB], slow=True)
            kb.dma(QS, v_s[:, :, pc:pc + 1], zpad[:, 0:2, 0:1], reads=[B_const], writes=[PADB], slow=True)
        RES = [Buf(f"res{i}") for i in range(18)]

        def res_ap(sti):
            if sti < 2:
                return xc_s[sti * 128:(sti + 1) * 128, :]
            return out[(sti - 2) * 128:(sti - 1) * 128, :]

        xt_t = [sb(f"xt{i}", [P, D], F32) for i in range(2)]
        xtB = [Buf() for _ in range(2)]
        xn_t = [sb(f"xn{i}", [P, D], BF16) for i in range(2)]
        xnB = [Buf() for _ in range(2)]
        junk = sb("junk", [P, D], BF16)
        junkB = Buf()
        hT_t = [sb(f"hT{i}", [P, 8, 512], BF16) for i in range(2)]
        hTB = [[Buf(), Buf()] for _ in range(2)]
        cnt = {"xt": 0, "xn": 0, "hT": 0, "alt": 0}

        def rstd_from(ssum_ap, ssumB, n_feat):
            i = ssrot[0]
            ssrot[0] = (i + 1) % NSS
            r, rB = ss[i], ssB[i]
            kb.op(ACT, lambda: nc.scalar.activation(out=r[:, 2:3], in_=ssum_ap, func=AF.Ln, scale=1.0 / n_feat, bias=epsc[:, 0:1]),
                  reads=[ssumB, B_const], writes=[rB], n=1)
            kb.op(ACT, lambda: nc.scalar.activation(out=r[:, 3:4], in_=r[:, 2:3], func=AF.Exp, scale=-0.5), reads=[rB], writes=[rB], n=1)
            return r, rB

        def tok_sumsq(xt, xB, junk_=None, junkB_=None):
            junk_, junkB_ = (junk, junkB) if junk_ is None else (junk_, junkB_)
            i = ssrot[0]
            ssrot[0] = (i + 1) % NSS
            r, rB = ss[i], ssB[i]
            kb.op(DVE, lambda: nc.vector.memset(r[:, 0:1], 0.0), writes=[rB], n=1)
            kb.op(ACT, lambda: nc.scalar.activation(out=junk_[:], in_=xt[:], func=AF.Square, accum_out=r[:, 0:1]),
                  reads=[xB, rB], writes=[junkB_, rB], n=1024)
            return r, rB

        def norm_to_hT(xt, xB, l, w, v, hT, hB, st, bank=None, bufs=None, evac=None):
            if bufs is None:
                r0, r0B = tok_sumsq(xt, xB)
            else:
                r0, r0B = tok_sumsq(xt, xB, bufs[2], bufs[3])
            yield
            r, rB = rstd_from(r0[:, 0:1], r0B, D)
            yield
            if bufs is None:
                i = cnt["xn"] % 2
                cnt["xn"] += 1
                xn, xnb = xn_t[i], xnB[i]
            else:
                xn, xnb = bufs[0], bufs[1]
            modb = [GSB[l][w]] + (MODB[l][0:2] if w == 0 else MODB[l][6:8])
            kb.op(DVE, lambda: nc.vector.tensor_scalar(out=xn[:], in0=xt[:], scalar1=r[:, 3:4], scalar2=0.0, op0=ALU.mult, op1=ALU.add),
                  reads=[xB, rB], writes=[xnb], n=1024)
            yield
            ps, psB = nextps() if bank is None else (PS[bank], PSB[bank])
            psb = ps[:].bitcast(BF16)
            for k in range(8):
                kb.op(PE, lambda k=k: nc.tensor.transpose(psb[:, k * 128:(k + 1) * 128], xn[:, k * 128:(k + 1) * 128], idb[:]),
                      reads=[xnb, B_const], writes=[psB], n=128)
            yield
            sh0 = 0 if w == 0 else 24
            use_act = cnt["alt"] % 3 == 0
            cnt["alt"] += 1
            if evac is not None:
                use_act = evac == "act"
            for k in range(8):
                o = hT[:, k, st * 128:(st + 1) * 128]
                i_ = psb[:, k * 128:(k + 1) * 128]
                g_ = GS[:, l, w, k, v:v + 1]
                s_ = MOD[:, l, sh0 + k, v:v + 1]
                if use_act:
                    kb.op(ACT, lambda o=o, i_=i_, g_=g_, s_=s_: nc.scalar.activation(out=o, in_=i_, func=AF.Identity, bias=s_, scale=g_),
                          reads=[psB] + modb, writes=[hB[0]], n=128)
                else:
                    kb.op(DVE, lambda o=o, i_=i_, g_=g_, s_=s_: nc.vector.tensor_scalar(out=o, in0=i_, scalar1=g_, scalar2=s_,
                                                                                     op0=ALU.mult, op1=ALU.add),
                          reads=[psB] + modb, writes=[hB[1]], n=128)
                if k % 2 == 1:
                    yield

        H1B = [Buf(f"h1_{i}") for i in range(5)]
        H2B = [Buf(f"h2_{i}") for i in range(5)]
        VSB = [Buf(f"vs_{i}") for i in range(5)]
        BGB = [Buf(f"bg_{i}") for i in range(5)]

        def get_xt():
            i = cnt["xt"] % len(xt_t)
            cnt["xt"] += 1
            return xt_t[i], xtB[i]

        def get_hT():
            i = cnt["hT"] % 2
            cnt["hT"] += 1
            return hT_t[i], hTB[i]

        def ada_gen(l, groups, aslot, aB, banks):
            ns = len(aslot)
            groups = list(groups)
            issued = 0
            for gi, g in enumerate(groups):
                while issued < len(groups) and issued < gi + ns:
                    gg = groups[issued]
                    kb.dma(QP, aslot[gg % ns][:].rearrange("p k n -> p (k n)"), ada_in[l, gg], writes=[aB[gg % ns]])
                    issued += 1
                    if issued > ns:
                        break
                if gi == 0:
                    yield
                sl, slB = aslot[g % ns], aB[g % ns]
                bk = banks[g % 2]
                psm = PS[bk][:, 0:8]
                for j in range(4):
                    for kc in range(8):
                        kb.op(PE, lambda j=j, kc=kc, sl=sl, psm=psm: nc.tensor.matmul(
                            psm[:, 2 * j:2 * j + 2], sl[:, kc, j * 128:(j + 1) * 128], csil[:, 2 * kc:2 * kc + 2],
                            start=(kc == 0), stop=(kc == 7)), reads=[slB, B_mod], writes=[PSB[bk]], n=32)
                for v in range(2):
                    kb.op(DVE, lambda v=v, g=g, psm=psm: nc.vector.tensor_tensor(
                        out=MOD[:, l, 4 * g:4 * g + 4, v], in0=psm.rearrange("p (j v) -> p j v", v=2)[:, :, v],
                        in1=smalls[:, l, O_AB + 4 * g:O_AB + 4 * g + 4], op=ALU.add),
                        reads=[PSB[bk], B_const], writes=[MODB[l][g]], n=8, ww=False)
                for w, (sc0, og, glast) in enumerate(((8, O_G1, 3), (32, O_G2, 9))):
                    if g == glast:
                        for v in range(2):
                            kb.op(DVE, lambda w=w, v=v, sc0=sc0, og=og: nc.vector.scalar_tensor_tensor(
                                out=GS[:, l, w, :, v], in0=MOD[:, l, sc0:sc0 + 8, v], scalar=1.0,
                                in1=smalls[:, l, og:og + 8], op0=ALU.add, op1=ALU.mult),
                                reads=[MODB[l][glast - 1], MODB[l][glast], B_const], writes=[GSB[l][w]], n=8, ww=False)
                yield

        marks.append(('pass0', PE.cnt))
        with contextlib.ExitStack() as s0:
            A0 = s0.enter_context
            aslot = [sb(f"aslot{i}", [P, 8, 512], BF16, A0) for i in range(2)]
            aB = [Buf(), Buf()]
            p0 = []
            for ti in range(5):
                p0.append(dict(xt=sb(f"p0xt{ti}", [P, D], F32, A0), xB=Buf(), xn=sb(f"p0xn{ti}", [P, D], BF16, A0), xnB=Buf(),
                               junk=sb(f"p0j{ti}", [P, D], BF16, A0), junkB=Buf(), hT=sb(f"p0hT{ti}", [P, 8, 512], BF16, A0), hB=[Buf(), Buf()]))

            def p0_task(ti):
                kind, c0, n = TILES[ti]
                v = 1 if kind == "c" else 0
                b = p0[ti]
                for st in range(n // 128):
                    sti = c0 // 128 + st
                    src = ctx_in[st * 128:(st + 1) * 128, :] if kind == "c" else x_in[(sti - 2) * 128:(sti - 1) * 128, :]
                    kb.dma(QS, b["xt"][:], src, writes=[b["xB"]])
                    yield from norm_to_hT(b["xt"], b["xB"], 0, 0, v, b["hT"], b["hB"], st, bank=2 + ti,
                                          bufs=(b["xn"], b["xnB"], b["junk"], b["junkB"]))
                kb.dma(QA, h1_s[:, :, c0:c0 + n], b["hT"][:, :, 0:n], reads=b["hB"], writes=[H1B[ti]])

            kb.run_tasks([ada_gen(0, range(12), aslot, aB, (0, 1))] + [p0_task(ti) for ti in range(5)])
            kb.barrier()

        for l in range(n_layers):
            last = l == DEPTH - 1

            def sm(o, n=1, l=l):
                return smalls[:, l, o:o + n]

            with contextlib.ExitStack() as s1:
                A = s1.enter_context
                KN = sb("KN", [P, 4, T], BF16, A)
                KR = sb("KR", [P, T], BF16, A)
                Vt = sb("Vt", [P, 18, 512], BF16, A)
                CQS = [Buf(f"cqs{i}") for i in range(5)]
                U = sb("U", [P, 18, 256], BF16, A)
                KVB = [Buf(f"kv{i}") for i in range(5)]
                B_rope = Buf()
                B_w = Buf("wmix")
                GA1 = sb("GA1", [P, 2, D], F32, A)
                WUQ = sb("WUQ", [P, 3, 1024], BF16, A)
                B_ga = Buf()

                def make_gate_bc(dst, ch0, dstB):
                    for v in range(2):
                        for j in range(8):
                            fill, fB = gfill[j % 2], gfillB[j % 2]
                            kb.op(ACT, lambda v=v, j=j, fill=fill: nc.scalar.activation(out=fill[:], in_=idf[:], func=AF.Identity,
                                                                                        bias=MOD[:, l, ch0 + j, v:v + 1], scale=0.0),
                                  reads=MODB[l][ch0 // 4:ch0 // 4 + 2] + [B_const], writes=[fB], n=128)
                            ps, psB = nextps()
                            kb.op(PE, lambda ps=ps, fill=fill: nc.tensor.matmul(ps[:, 0:128], fill[:], idf[:], start=True, stop=True),
                                  reads=[fB, B_const], writes=[psB], n=512)
                            kb.op(DVE, lambda ps=ps, v=v, j=j: nc.vector.tensor_copy(out=dst[:, v, j * 128:(j + 1) * 128], in_=ps[:, 0:128]),
                                  reads=[psB], writes=[dstB], n=128, ww=False)


                marks.append((f'L{l}pass1', PE.cnt))
                with contextlib.ExitStack() as s2:
                    A2 = s2.enter_context
                    WIN = sb("WIN", [P, 8, NWIN], BF16, A2)
                    WIN_SEGS = ((0, 640), (640, 1920))
                    B_wins = [Buf() for _ in WIN_SEGS]
                    for (a_, b_), wb in zip(WIN_SEGS, B_wins):
                        kb.dma(QP, WIN[:, :, a_:b_], win_in[l].rearrange("p (k n) -> p k n", n=NWIN)[:, :, a_:b_], writes=[wb])

                    def winB(col0, ncols):
                        return [wb for (a_, b_), wb in zip(WIN_SEGS, B_wins) if a_ < col0 + ncols and col0 < b_]
                    WUKV = sb("WUKV", [P, 2, 1024], BF16, A2)
                    kb.dma(QP, WUKV[:].rearrange("p k n -> p (k n)"), wukv_in[l], writes=[B_w])
                    ropeC = sb("ropeC", [P, 512], F32, A2)
                    ropeS = sb("ropeS", [P, 512], F32, A2)
                    kb.dma(QP, WUQ[:].rearrange("p k n -> p (k n)"), wuq_in[l], writes=[B_w])
                    make_gate_bc(GA1, 16, B_ga)
                    cqf = sb("cqf", [P, 3, 512], F32, A2)
                    cqf2 = sb("cqf2", [P, 2, 512], F32, A2)
                    cqf2B = Buf()
                    sq2 = [sb(f"sq2_{i}", [P, 512], BF16, A2) for i in range(2)]
                    sq2B = [Buf(), Buf()]
                    rsb2 = sb("rsb2", [P, 3, 512], F32, A2)
                    rsB2 = Buf()
                    cgs, cgB = cqf2, cqf2B

                    def mkrot(banks):
                        st_ = [0]

                        def r():
                            b = banks[st_[0] % len(banks)]
                            st_[0] += 1
                            return PS[b], PSB[b]
                        return r
                    rot1, rot2 = mkrot((0, 1, 2, 3)), mkrot((4, 5, 6))
                    cqo = [sb(f"cqo{i}", [P, 3, 512], BF16, A2) for i in range(2)]
                    cqoB = [Buf(), Buf()]
                    cqfB = Buf()
                    sq = [sb(f"sq{i}", [P, 512], BF16, A2) for i in range(3)]
                    sqB = [Buf() for _ in range(3)]
                    rsb = sb("rsb", [P, 3, 512], F32, A2)
                    rsB = Buf()
                    ckvn = sb("ckvn", [P, 2, 512], BF16, A2)
                    ckvnB = Buf()
                    tA = sb("tA", [P, 512], F32, A2)
                    tB_ = sb("tB", [P, 512], F32, A2)
                    tAB, tBB = Buf(), Buf()
                    vbuf = [sb(f"vbuf{i}", [P, 2, 512], F32, A2) for i in range(1)] * 2
                    vbB = [Buf()] * 2
                    bgbuf = [sb(f"bgbuf{i}", [P, 2, 512], F32, A2) for i in range(1)] * 2
                    bgbB = [Buf()] * 2
                    sqc = [0]

                    def feat_rstd(nchunks, nfeat, getsrc, n, ps_get=None, sqs=None, sqBs=None, rs=None, rsBuf=None):
                        sq_, sqB_ = (sq, sqB) if sqs is None else (sqs, sqBs)
                        rsb_, rsB_ = (rsb, rsB) if rs is None else (rs, rsBuf)
                        nsq = len(sq_)
                        pss, pssB = (ps_get or nextps)()
                        for j in range(nchunks):
                            src_ap, srcB = getsrc(j)
                            i = sqc[0] % nsq
                            sqc[0] += 1
                            kb.op(ACT, lambda src_ap=src_ap, i=i: nc.scalar.activation(out=sq_[i][:, :n], in_=src_ap, func=AF.Square),
                                  reads=[srcB], writes=[sqB_[i]])
                            kb.op(PE, lambda j=j, i=i: nc.tensor.matmul(pss[:, :n], ones[:], sq_[i][:, :n], start=(j == 0), stop=(j == nchunks - 1)),
                                  reads=[sqB_[i], B_const], writes=[pssB])
                        kb.op(ACT, lambda: nc.scalar.activation(out=rsb_[:, 1, :n], in_=pss[:, :n], func=AF.Ln, scale=1.0 / nfeat, bias=epsc[:, 0:1]),
                              reads=[pssB, B_const], writes=[rsB_])
                        kb.op(ACT, lambda: nc.scalar.activation(out=rsb_[:, 2, :n], in_=rsb_[:, 1, :n], func=AF.Exp, scale=-0.5), reads=[rsB_], writes=[rsB_])

                    def win_mm(ps, psB, col0, ncols, hT, hB, n, c_lo=0):
                        for kc in range(8):
                            kb.op(PE, lambda kc=kc: nc.tensor.matmul(ps[0:ncols, :n], WIN[:, kc, col0:col0 + ncols], hT[:, kc, c_lo:c_lo + n],
                                                                    start=(kc == 0), stop=(kc == 7)), reads=winB(col0, ncols) + hB, writes=[psB])

                    ada_next = None
                    if l + 1 < n_layers:
                        aslot1 = [sb(f"aslotn{i}", [P, 8, 512], BF16, A2) for i in range(2)]
                        ada_next = ada_gen(l + 1, range(12), aslot1, [Buf(), Buf()], (7, 7))
                        next(ada_next, None)
                    print('SBUF remaining pass1', nc.sbuf_bytes_remaining)
                    for ti, (kind, c0, n) in enumerate(TILES):
                        lat = kind == "l"
                        kvb = KVB[ti]
                        hT, hB = get_hT()
                        kb.dma(QS, hT[:, :, 0:n], h1_s[:, :, c0:c0 + n], reads=[H1B[ti]], writes=hB)
                        def T1():
                            nextps = rot1
                            for j in range(3):
                                yield
                                ps, psB = nextps()
                                win_mm(ps, psB, j * 128, 128, hT, hB, n)
                                kb.op(ACT, lambda ps=ps, j=j: nc.scalar.activation(out=cqf[:, j, :n], in_=ps[:, :n], func=AF.Copy),
                                      reads=[psB], writes=[cqfB])
                            feat_rstd(3, 384, lambda j: (cqf[:, j, :n], cqfB), n, ps_get=nextps)
                            for j in range(3):
                                kb.op(DVE, lambda j=j: nc.vector.scalar_tensor_tensor(out=cqo[ti % 2][:, j, :n], in0=cqf[:, j, :n], scalar=sm(O_QG + j),
                                                                                     in1=rsb[:, 2, :n], op0=ALU.mult, op1=ALU.mult),
                                      reads=[cqfB, rsB, B_const], writes=[cqoB[ti % 2]])
                            kb.dma(QP, cqn_s[:, :, c0:c0 + n], cqo[ti % 2][:, :, :n], reads=[cqoB[ti % 2]], writes=[CQS[ti]])
                            yield
                            psa, psaB = nextps()
                            win_mm(psa, psaB, 640, 128, hT, hB, n)
                            if lat:
                                psb_, psbB = nextps()
                                win_mm(psb_, psbB, 768, 128, hT, hB, n)
                                r0 = c0 - 256
                                kb.dma(QS, ropeC[:, :n], rope_in[0][:, r0:r0 + n], writes=[B_rope])
                                kb.dma(QS, ropeS[:, :n], rope_in[1][:, r0:r0 + n], writes=[B_rope])
                                kb.op(DVE, lambda: nc.vector.tensor_tensor(out=tA[:, :n], in0=psa[:, :n], in1=ropeC[:, :n], op=ALU.mult),
                                      reads=[psaB, B_rope], writes=[tAB])
                                kb.op(DVE, lambda: nc.vector.tensor_tensor(out=tB_[:, :n], in0=psb_[:, :n], in1=ropeS[:, :n], op=ALU.mult),
                                      reads=[psbB, B_rope], writes=[tBB])
                                kb.op(DVE, lambda: nc.vector.tensor_tensor(out=KR[:, c0:c0 + n], in0=tA[:, :n], in1=tB_[:, :n], op=ALU.add),
                                      reads=[tAB, tBB], writes=[kvb])
                            else:
                                kb.op(ACT, lambda: nc.scalar.activation(out=KR[:, c0:c0 + n], in_=psa[:, :n], func=AF.Copy), reads=[psaB], writes=[kvb])
                            for st in range(n // 128):
                                yield
                                ps, psB = nextps()
                                for kc in range(8):
                                    kb.op(PE, lambda kc=kc, st=st, ps=ps: nc.tensor.matmul(ps[:, 0:256], hT[:, kc, st * 128:(st + 1) * 128], WIN[:, kc, 896:1152],
                                                                                          start=(kc == 0), stop=(kc == 7)), reads=winB(896, 256) + hB, writes=[psB])
                                kt = c0 // 128 + st
                                if st % 2 == 1:
                                    kb.op(ACT, lambda kt=kt, ps=ps: nc.scalar.activation(out=U[:, kt, :], in_=ps[:, 0:256], func=AF.Copy), reads=[psB], writes=[kvb], ww=False, n=256)
                                else:
                                    kb.op(DVE, lambda kt=kt, ps=ps: nc.vector.tensor_copy(out=U[:, kt, :], in_=ps[:, 0:256]), reads=[psB], writes=[kvb], ww=False, n=256)

                        def T2():
                            nextps = rot2
                            for j in range(2):
                                yield
                                ps, psB = nextps()
                                win_mm(ps, psB, 384 + j * 128, 128, hT, hB, n)
                                kb.op(ACT, lambda ps=ps, j=j: nc.scalar.activation(out=cqf2[:, j, :n], in_=ps[:, :n], func=AF.Copy),
                                      reads=[psB], writes=[cqf2B])
                            feat_rstd(2, 256, lambda j: (cqf2[:, j, :n], cqf2B), n, ps_get=nextps, sqs=sq2, sqBs=sq2B, rs=rsb2, rsBuf=rsB2)
                            for j in range(2):
                                kb.op(DVE, lambda j=j: nc.vector.scalar_tensor_tensor(out=ckvn[:, j, :n], in0=cqf2[:, j, :n], scalar=sm(O_KVG + j),
                                                                                     in1=rsb2[:, 2, :n], op0=ALU.mult, op1=ALU.mult),
                                      reads=[cqf2B, rsB2, B_const], writes=[ckvnB])
                            for h in range(4):
                                yield
                                ps, psB = nextps()
                                for kc in range(2):
                                    kb.op(PE, lambda kc=kc, h=h, ps=ps: nc.tensor.matmul(ps[:, :n], WUKV[:, kc, h * 128:(h + 1) * 128], ckvn[:, kc, :n],
                                                                                        start=(kc == 0), stop=(kc == 1)), reads=[B_w, ckvnB], writes=[psB])
                                if h % 2 == 0:
                                    kb.op(ACT, lambda h=h, ps=ps: nc.scalar.activation(out=KN[:, h, c0:c0 + n], in_=ps[:, :n], func=AF.Copy),
                                          reads=[psB], writes=[kvb], ww=False)
                                else:
                                    kb.op(DVE, lambda h=h, ps=ps: nc.vector.tensor_copy(out=KN[:, h, c0:c0 + n], in_=ps[:, :n]), reads=[psB], writes=[kvb], ww=False)
                            for st in range(n // 128):
                                yield
                                ps, psB = nextps()
                                for kc in range(2):
                                    kb.op(PE, lambda kc=kc, st=st, ps=ps: nc.tensor.matmul(ps[:, :], ckvn[:, kc, st * 128:(st + 1) * 128], WUKV[:, kc, 512:1024],
                                                                                          start=(kc == 0), stop=(kc == 1)), reads=[B_w, ckvnB], writes=[psB])
                                kt = c0 // 128 + st
                                if st % 2 == 0:
                                    kb.op(ACT, lambda kt=kt, ps=ps: nc.scalar.activation(out=Vt[:, kt, :], in_=ps[:, :], func=AF.Copy), reads=[psB], writes=[kvb], ww=False)
                                else:
                                    kb.op(DVE, lambda kt=kt, ps=ps: nc.vector.tensor_copy(out=Vt[:, kt, :], in_=ps[:, :]), reads=[psB], writes=[kvb], ww=False)
                            vb, vbb = vbuf[ti % 2], vbB[ti % 2]
                            bgb, bgbb = bgbuf[ti % 2], bgbB[ti % 2]
                            for j in range(2):
                                yield
                                ps, psB = nextps()
                                win_mm(ps, psB, 1152 + j * 128, 128, hT, hB, n)
                                kb.op(ACT, lambda ps=ps, j=j: nc.scalar.activation(out=bgb[:, j, :n], in_=ps[:, :n], func=AF.Copy), reads=[psB], writes=[bgbb])
                            for j in range(2):
                                yield
                                ps, psB = nextps()
                                win_mm(ps, psB, 1408 + j * 128, 128, hT, hB, n)
                                kb.op(ACT, lambda ps=ps, j=j: nc.scalar.activation(out=cgs[:, j, :n], in_=ps[:, :n], func=AF.Copy), reads=[psB], writes=[cgB])
                            for j in range(2):
                                yield
                                ps, psB = nextps()
                                win_mm(ps, psB, 1664 + j * 128, 128, hT, hB, n)
                                kb.op(DVE, lambda ps=ps, j=j: nc.vector.tensor_tensor(out=vb[:, j, :n], in0=ps[:, :n], in1=cgs[:, j, :n], op=ALU.mult),
                                      reads=[psB, cgB], writes=[vbb])
                            kb.dma(QA, bg_s[:, :, c0:c0 + n], bgb[:, :, :n], reads=[bgbb], writes=[BGB[ti]])
                            kb.dma(QP, v_s[:, :, H2C[ti]:H2C[ti] + n], vb[:, :, :n], reads=[vbb], writes=[VSB[ti]])

                        kb.run_tasks([T1(), T2()] + ([itertools.islice(ada_next, 3)] if ada_next is not None else []))
                    kb.barrier()

                marks.append((f'L{l}pass2', PE.cnt))
                with contextlib.ExitStack() as s2:
                    A2 = s2.enter_context
                    WOUT = sb("WOUT", [P, 8, 1024], BF16, A2)
                    B_wout = Buf()
                    kb.dma(QP, WOUT[:].rearrange("p k n -> p (k n)"), wout_in[l], writes=[B_wout])
                    ropeC = sb("ropeC2", [P, 512], F32, A2)
                    ropeS = sb("ropeS2", [P, 512], F32, A2)
                    DT = [sb(f"DT{i}", [P, 2, 2, 256], BF16, A2) for i in range(3)]
                    DTB = [Buf() for _ in range(3)]
                    dtc = [0]
                    QNl = [sb(f"QN{i}", [P, 4, 512], BF16, A2) for i in range(2)]
                    QRl = [sb(f"QR{i}", [P, 2, 512], BF16, A2) for i in range(2)]
                    qBl = [Buf(), Buf()]
                    CQT = [sb(f"CQT{i}", [P, 3, 512], BF16, A2) for i in range(2)]
                    CQTB = [Buf(), Buf()]
                    tA = sb("tA2", [P, 512], F32, A2)
                    tB_ = sb("tB2", [P, 512], F32, A2)
                    tAB, tBB = Buf(), Buf()
                    rinv, rinvB = tA, tAB
                    PT = [sb(f"PT{i}", [P, 512], BF16, A2) for i in range(3)]
                    PTB = [Buf() for _ in range(3)]
                    attf = sb("attf", [P, 4, 512], F32, A2)
                    attB = Buf()
                    Yt = [sb(f"Y{i}", [P, 8, 512], BF16, A2) for i in range(2)]
                    YAB = [Buf(), Buf()]
                    YBB = [Buf(), Buf()]
                    P12 = sb("P12", [P, 4, 256], BF16, A2)
                    P12B = Buf()
                    yff = sb("yff", [P, 2, 256], F32, A2)
                    yffB = Buf()
                    vwj = sb("vwj", [P, 514], F32, A2)
                    vwB = Buf()
                    bgj = sb("bgj", [P, 512], F32, A2)
                    bgwB = Buf()
                    cacc = sb("cacc", [P, 2, 512], F32, A2)
                    caccB = Buf()
                    sqA = [sb(f"sqA{i}", [P, 512], BF16, A2) for i in range(2)]
                    sqAB = [Buf() for _ in range(2)]
                    sqBt = [sb(f"sqB{i}", [P, 512], BF16, A2) for i in range(2)]
                    sqBB = [Buf() for _ in range(2)]
                    rsA = sb("rsA", [P, 2, 512], F32, A2)
                    rsAB = Buf()
                    rsBt = sb("rsB", [P, 2, 512], F32, A2)
                    rsBB = Buf()
                    tmpx = sb("tmpx", [P, 512], F32, A2)
                    tmpxB = Buf()
                    print('SBUF remaining pass2', nc.sbuf_bytes_remaining)

                    def feat_rstd2(nchunks, nfeat, getsrc, n, bank, sqs, sqBs, rs, rsBuf, cnt_):
                        pss, pssB = PS[bank], PSB[bank]
                        for j in range(nchunks):
                            src_ap, srcB = getsrc(j)
                            i = cnt_[0] % 2
                            cnt_[0] += 1
                            kb.op(DVE, lambda src_ap=src_ap, i=i: nc.vector.tensor_tensor(out=sqs[i][:, :n], in0=src_ap, in1=src_ap, op=ALU.mult),
                                  reads=[srcB], writes=[sqBs[i]], n=n)
                            kb.op(PE, lambda j=j, i=i: nc.tensor.matmul(pss[:, :n], ones[:], sqs[i][:, :n], start=(j == 0), stop=(j == nchunks - 1)),
                                  reads=[sqBs[i], B_const], writes=[pssB], n=n)
                        kb.op(ACT, lambda: nc.scalar.activation(out=rs[:, 0, :n], in_=pss[:, :n], func=AF.Ln, scale=1.0 / nfeat, bias=epsc[:, 0:1]),
                              reads=[pssB, B_const], writes=[rsBuf])
                        kb.op(ACT, lambda: nc.scalar.activation(out=rs[:, 1, :n], in_=rs[:, 0, :n], func=AF.Exp, scale=-0.5), reads=[rsBuf], writes=[rsBuf])

                    cA, cB = [0], [0]

                    def qproj(ti, yi, b0, b1):
                        kind, c0, n = TILES[ti]
                        lat = kind == "l"
                        QN, QR, qB = QNl[yi], QRl[yi], qBl[yi]
                        cq, cqB = CQT[yi], CQTB[yi]
                        kb.dma(QS, cq[:, :, 0:n], cqn_s[:, :, c0:c0 + n], reads=[CQS[ti]], writes=[cqB])
                        if lat:
                            r0 = c0 - 256
                            kb.dma(QS, ropeC[:, :n], rope_in[0][:, r0:r0 + n], writes=[B_rope])
                            kb.dma(QS, ropeS[:, :n], rope_in[1][:, r0:r0 + n], writes=[B_rope])
                        for h in range(4):
                            ps, psB = PS[(b0, b1)[h % 2]], PSB[(b0, b1)[h % 2]]
                            for kc in range(3):
                                kb.op(PE, lambda kc=kc, h=h, ps=ps: nc.tensor.matmul(ps[:, :n], WUQ[:, kc, h * 128:(h + 1) * 128], cq[:, kc, :n],
                                                                                    start=(kc == 0), stop=(kc == 2)), reads=[B_w, cqB], writes=[psB])
                            kb.op(DVE, lambda h=h, ps=ps: nc.vector.tensor_copy(out=QN[:, h, :n], in_=ps[:, :n]), reads=[psB], writes=[qB], ww=False)
                            yield
                        for pr in range(2):
                            psa, psaB = PS[b0], PSB[b0]
                            for kc in range(3):
                                kb.op(PE, lambda kc=kc, pr=pr, psa=psa: nc.tensor.matmul(psa[:, :n], WUQ[:, kc, 512 + pr * 128:640 + pr * 128], cq[:, kc, :n],
                                                                                        start=(kc == 0), stop=(kc == 2)), reads=[B_w, cqB], writes=[psaB])
                            if lat:
                                psb_, psbB = PS[b1], PSB[b1]
                                for kc in range(3):
                                    kb.op(PE, lambda kc=kc, pr=pr, psb_=psb_: nc.tensor.matmul(psb_[:, :n], WUQ[:, kc, 768 + pr * 128:896 + pr * 128], cq[:, kc, :n],
                                                                                              start=(kc == 0), stop=(kc == 2)), reads=[B_w, cqB], writes=[psbB])
                                kb.op(DVE, lambda psa=psa: nc.vector.tensor_tensor(out=rsBt[:, 0, :n], in0=psa[:, :n], in1=ropeC[:, :n], op=ALU.mult),
                                      reads=[psaB, B_rope], writes=[rsBB], ww=False)
                                kb.op(DVE, lambda psb_=psb_: nc.vector.tensor_tensor(out=rsBt[:, 1, :n], in0=psb_[:, :n], in1=ropeS[:, :n], op=ALU.mult),
                                      reads=[psbB, B_rope], writes=[rsBB], ww=False)
                                kb.op(POOL, lambda pr=pr: nc.gpsimd.tensor_tensor(out=QR[:, pr, :n], in0=rsBt[:, 0, :n], in1=rsBt[:, 1, :n], op=ALU.add),
                                      reads=[rsBB], writes=[qB], ww=False)
                            else:
                                kb.op(DVE, lambda pr=pr, psa=psa: nc.vector.tensor_copy(out=QR[:, pr, :n], in_=psa[:, :n]), reads=[psaB], writes=[qB])
                            yield

                    def taskA(ti, yi):
                        kind, c0, n = TILES[ti]
                        lat = kind == "l"
                        nk = 18 if lat else 2
                        allkv = KVB if lat else KVB[0:1]
                        QN, QR, qB = QNl[yi], QRl[yi], qBl[yi]
                        for h in range(4):
                            kp = (h % 2) * 64

                            def scores(kt, h=h, kp=kp):
                                b = kt % 2
                                kb.op(PE, lambda: nc.tensor.matmul(PS[b][:, :n], KN[:, h, kt * 128:(kt + 1) * 128], QN[:, h, :n], start=True, stop=False),
                                      reads=allkv + [qB], writes=[PSB[b]])
                                kb.op(PE, lambda: nc.tensor.matmul(PS[b][:, :n], KR[kp:kp + 64, kt * 128:(kt + 1) * 128], QR[kp:kp + 64, h // 2, :n],
                                                                   start=False, stop=True), reads=allkv + [qB], writes=[PSB[b]])

                            def expo(kt):
                                b = kt % 2
                                kb.op(ACT, lambda: nc.scalar.activation(out=PT[kt % 3][:, :n], in_=PS[b][:, :n], func=AF.Exp, scale=SM_SCALE),
                                      reads=[PSB[b]], writes=[PTB[kt % 3]])

                            def pv(kt, h=h):
                                b = kt % 3
                                kb.op(PE, lambda: nc.tensor.matmul(PS[2][:, :n], Vt[:, kt, h * 128:(h + 1) * 128], PT[b][:, :n], start=(kt == 0), stop=(kt == nk - 1)),
                                      reads=allkv + [PTB[b]], writes=[PSB[2]])
                                kb.op(PE, lambda: nc.tensor.matmul(PS[3][:, :n], ones[:], PT[b][:, :n], start=(kt == 0), stop=(kt == nk - 1)),
                                      reads=[B_const, PTB[b]], writes=[PSB[3]])

                            scores(0)
                            for kt in range(nk):
                                expo(kt)
                                if kt + 1 < nk:
                                    scores(kt + 1)
                                pv(kt)
                                yield
                            kb.op(ACT, lambda: nc.scalar.activation(out=tB_[:, :n], in_=PS[3][:, :n], func=AF.Ln), reads=[PSB[3]], writes=[tBB])
                            kb.op(ACT, lambda: nc.scalar.activation(out=rinv[:, :n], in_=tB_[:, :n], func=AF.Exp, scale=-1.0), reads=[tBB], writes=[rinvB])
                            kb.op(DVE, lambda h=h: nc.vector.tensor_copy(out=attf[:, h, :n], in_=PS[2][:, :n]), reads=[PSB[2]], writes=[attB])
                            kb.op(DVE, lambda h=h: nc.vector.tensor_tensor(out=attf[:, h, :n], in0=attf[:, h, :n], in1=rinv[:, :n], op=ALU.mult),
                                  reads=[attB, rinvB], writes=[attB])

                    def taskB(ti, yi, nxt=None):
                        kind, c0, n = TILES[ti]
                        lat = kind == "l"
                        allkv = KVB if lat else KVB[0:1]
                        Y, YB = Yt[yi], YBB[yi]
                        nti = 16 if lat else 2
                        scale = (nti * 128 * 64) ** -0.5
                        for kh in range(n // 256):
                            k0 = kh * 256
                            for ti2 in range(nti):
                                if lat:
                                    if ti2 % 2 == 0:
                                        di = dtc[0] % 3
                                        dtc[0] += 1
                                        kb.dma(QS, DT[di][:].rearrange("p a b c -> p (a b c)"), dftl_in[(ti - 1) * 2 + kh, ti2 // 2], writes=[DTB[di]])
                                    tab = lambda di=di, ti2=ti2: DT[di][:, ti2 % 2, :, :].rearrange("p a b -> p (a b)")
                                    tabB = DTB[di]
                                    ug = 2 + ti2
                                else:
                                    tab = lambda ti2=ti2: dftc[:, ti2, :, :].rearrange("p a b -> p (a b)")
                                    tabB = B_const
                                    ug = ti2
                                for f in range(2):
                                    kb.op(PE, lambda f=f, tab=tab, ug=ug, ti2=ti2: nc.tensor.matmul(
                                        PS[4 + f][:, :], U[:, ug, f * 128:(f + 1) * 128], tab(), start=(ti2 == 0), stop=(ti2 == nti - 1)),
                                        reads=[tabB] + allkv, writes=[PSB[4 + f]])
                                yield
                            for f in range(2):
                                src = PS[4 + f][:, :]
                                dst = P12[:, 2 * f:2 * f + 2, :].rearrange("p a b -> p (a b)")
                                if False:
                                    pass
                                else:
                                    kb.op(DVE, lambda src=src, dst=dst: nc.vector.tensor_scalar(out=dst, in0=src, scalar1=scale, scalar2=0.0, op0=ALU.mult, op1=ALU.add),
                                          reads=[PSB[4 + f]], writes=[P12B])
                            for f in range(2):
                                ps, psB = PS[4 + f], PSB[4 + f]
                                kb.op(PE, lambda f=f, ps=ps: nc.tensor.matmul(ps[:, 0:256], bcs[:, 0:128], P12[:, 2 * f, :], start=True, stop=False),
                                      reads=[P12B, B_const], writes=[psB])
                                kb.op(PE, lambda f=f, ps=ps: nc.tensor.matmul(ps[:, 0:256], bcs[:, 128:256], P12[:, 2 * f + 1, :], start=False, stop=True),
                                      reads=[P12B, B_const], writes=[psB])
                                kb.op(DVE, lambda f=f, ps=ps: nc.vector.tensor_copy(out=yff[:, f, :], in_=ps[:, 0:256]), reads=[psB], writes=[yffB], n=256)
                            yield
                            feat_rstd2(2, 256, lambda j: (yff[:, j, :], yffB), 256, 4, sqBt, sqBB, rsBt, rsBB, cB)
                            for f in range(2):
                                kb.op(DVE, lambda f=f: nc.vector.scalar_tensor_tensor(out=Y[:, 4 + f, k0:k0 + 256], in0=yff[:, f, :], scalar=sm(O_OG + 4 + f),
                                                                                     in1=rsBt[:, 1, 0:256], op0=ALU.mult, op1=ALU.mult),
                                      reads=[yffB, rsBB, B_const], writes=[YB])
                            yield
                        first = ti in (0, 1)
                        lastt = ti in (0, 4)
                        nb = [VSB[ti], PADB] + ([] if first else [VSB[ti - 1]]) + ([] if lastt else [VSB[ti + 1]])
                        lo = 1 if first else 0
                        hi = n - 1 if lastt else n
                        for j in range(2):
                            kb.dma(QS, vwj[:, 0:n + 2], v_s[:, j, H2C[ti] - 1:H2C[ti] + n + 1], reads=nb, writes=[vwB])
                            kb.dma(QS, bgj[:, 0:n], bg_s[:, j, c0:c0 + n], reads=[BGB[ti]], writes=[bgwB])
                            kb.op(POOL, lambda j=j: nc.gpsimd.tensor_scalar(out=cacc[:, j, :n], in0=vwj[:, 1:n + 1], scalar1=sm(O_SW + 3 * j + 1),
                                                                            scalar2=sm(O_SB + j), op0=ALU.mult, op1=ALU.add),
                                  reads=[vwB, B_const], writes=[caccB])
                            kb.op(DVE, lambda j=j: nc.vector.scalar_tensor_tensor(out=cacc[:, j, lo:n], in0=vwj[:, lo:n], scalar=sm(O_SW + 3 * j),
                                                                                  in1=cacc[:, j, lo:n], op0=ALU.mult, op1=ALU.add),
                                  reads=[vwB, B_const, caccB], writes=[caccB])
                            kb.op(DVE, lambda j=j: nc.vector.scalar_tensor_tensor(out=cacc[:, j, 0:hi], in0=vwj[:, 2:hi + 2], scalar=sm(O_SW + 3 * j + 2),
                                                                                  in1=cacc[:, j, 0:hi], op0=ALU.mult, op1=ALU.add),
                                  reads=[vwB, B_const, caccB], writes=[caccB])
                            kb.op(POOL, lambda j=j: nc.gpsimd.tensor_tensor(out=cacc[:, j, :n], in0=cacc[:, j, :n], in1=bgj[:, :n], op=ALU.mult),
                                  reads=[caccB, bgwB], writes=[caccB])
                            yield
                        feat_rstd2(2, 256, lambda j: (cacc[:, j, :n], caccB), n, 5, sqBt, sqBB, rsBt, rsBB, cB)
                        yield
                        for j in range(2):
                            kb.op(DVE, lambda j=j: nc.vector.scalar_tensor_tensor(out=Y[:, 6 + j, :n], in0=cacc[:, j, :n], scalar=sm(O_OG + 6 + j),
                                                                                 in1=rsBt[:, 1, :n], op0=ALU.mult, op1=ALU.mult),
                                  reads=[caccB, rsBB, B_const], writes=[YB])
                        yield

                        if nxt is not None:
                            yield from qproj(nxt[0], nxt[1], 4, 5)

                    def taskC(ti, yi):
                        kind, c0, n = TILES[ti]
                        v = 0 if kind == "l" else 1
                        Y = Yt[yi]
                        yb = [YAB[yi], YBB[yi]]
                        hT, hB = get_hT()
                        feat_rstd2(4, 512, lambda j: (attf[:, j, :n], attB), n, 6, sqA, sqAB, rsA, rsAB, cA)
                        yield
                        for h in range(4):
                            kb.op(DVE, lambda h=h: nc.vector.scalar_tensor_tensor(out=Y[:, h, :n], in0=attf[:, h, :n], scalar=sm(O_OG + h),
                                                                                 in1=rsA[:, 1, :n], op0=ALU.mult, op1=ALU.mult),
                                  reads=[attB, rsAB, B_const], writes=[YAB[yi]])
                        yield

                        pendc = None
                        for st in range(n // 128):
                            sti = c0 // 128 + st
                            xt, xB = get_xt()
                            if l == 0:
                                src0 = ctx_in[sti * 128:(sti + 1) * 128, :] if sti < 2 else x_in[(sti - 2) * 128:(sti - 1) * 128, :]
                            else:
                                src0 = res_ap(sti)
                            kb.dma(QS, xt[:], src0, reads=[RES[sti]], writes=[xB])
                            for hf in range(2):
                                ps, psB = PS[6 + hf], PSB[6 + hf]
                                for kc in range(8):
                                    kb.op(PE, lambda kc=kc, st=st, hf=hf, ps=ps: nc.tensor.matmul(ps[:, :], Y[:, kc, st * 128:(st + 1) * 128], WOUT[:, kc, hf * 512:(hf + 1) * 512],
                                                                                                 start=(kc == 0), stop=(kc == 7)), reads=yb + [B_wout], writes=[psB])
                                kb.op(DVE, lambda hf=hf, ps=ps: nc.vector.tensor_tensor(out=tmpx[:], in0=ps[:, :], in1=GA1[:, v, hf * 512:(hf + 1) * 512], op=ALU.mult),
                                      reads=[psB, B_ga], writes=[tmpxB])
                                kb.op(POOL, lambda hf=hf, xt=xt: nc.gpsimd.tensor_tensor(out=xt[:, hf * 512:(hf + 1) * 512], in0=xt[:, hf * 512:(hf + 1) * 512], in1=tmpx[:], op=ALU.add),
                                      reads=[tmpxB, xB], writes=[xB])
                                yield
                            kb.dma(QP, res_ap(sti), xt[:], reads=[xB], writes=[RES[sti]])
                            newp = norm_to_hT(xt, xB, l, 1, v, hT, hB, st, bank=6 + (st % 2), evac="dve")
                            for _ in range(3):
                                next(newp)
                                yield
                            if pendc is not None:
                                yield from pendc
                            pendc = newp
                            yield
                        if pendc is not None:
                            yield from pendc
                        kb.dma(QA, h2_s[:, :, H2C[ti]:H2C[ti] + n], hT[:, :, 0:n], reads=hB, writes=[H2B[ti]])

                    p2tiles = [1, 2, 3, 4] + ([] if last else [0])
                    prev = None
                    kb.run_tasks([qproj(p2tiles[0], 0, 4, 5)])
                    for idx, ti in enumerate(p2tiles):
                        nxt = (p2tiles[idx + 1], (idx + 1) % 2) if idx + 1 < len(p2tiles) else None
                        latt = TILES[ti][0] == "l"
                        tasks = [taskA(ti, idx % 2), taskB(ti, idx % 2, nxt)]
                        exp_ = [288 if latt else 32, (110 if latt else 45)]
                        if prev is not None:
                            tasks.append(taskC(prev[0], prev[1]))
                            exp_.append(104 if TILES[prev[0]][0] == "l" else 54)
                        kb.run_tasks(tasks, expected=exp_)
                        prev = (ti, idx % 2)
                    kb.run_tasks([taskC(prev[0], prev[1])])
                    kb.barrier()

            marks.append((f'L{l}ffn', PE.cnt))
            with contextlib.ExitStack() as s1:
                A = s1.enter_context
                GA2 = sb("GA2", [P, 2, D], F32, A)
                B_ga2 = Buf()
                WU = [sb(f"WU{i}", [P, 8, 512], BF16, A) for i in range(2)]
                WUB = [Buf(), Buf()]
                WD = sb("WD", [P, 2, NCF, 512], BF16, A)
                WDB = [Buf(), Buf()]
                G = sb("G", [P, NCF, 1280], BF16, A)
                GB_ = Buf()
                h2w = sb("h2w", [P, 8, 1280 + 6], BF16, A)
                h2wBs = {t: Buf() for t in range(5)}
                gb = [sb(f"gb{i}", [P, 512], F32, A) for i in range(2)]
                gbB = [Buf(), Buf()]
                sg = [sb(f"sg{i}", [P, 512], F32, A) for i in range(2)]
                sgB = [Buf(), Buf()]
                tmpx = sb("tmpx3", [P, 512], F32, A)
                tmpxB = Buf()
                xt_t.append(sb("xt_extra", [P, D], F32, A))
                xtB.append(Buf())
                print('SBUF remaining ffn', nc.sbuf_bytes_remaining)
                supers = [[t for t in (0, 1, 2) if not (last and t == 0)], [3, 4]]
                wuc = [0]
                upc = [0]
                up_defer = [None]
                def load_windows(sup, prev=()):
                    offs, goff = {}, {}
                    o_, g_ = 0, 0
                    for t in sup:
                        kind, c0, n = TILES[t]
                        offs[t], goff[t] = o_, g_
                        first = t in (0, 1)
                        lastt = t in (0, 4)
                        nb = [H2B[t], PADB] + ([] if first else [H2B[t - 1]]) + ([] if lastt else [H2B[t + 1]])
                        kb.dma(QS, h2w[:, :, o_:o_ + n + 2], h2_s[:, :, H2C[t] - 1:H2C[t] + n + 1], reads=nb, writes=[h2wBs[t]] + [h2wBs[u] for u in prev])
                        o_ += n + 2
                        g_ += n
                    return offs, goff

                win_next = load_windows(supers[0])
                for g in range(2):
                    kb.dma(QP, WU[g][:].rearrange("p k n -> p (k n)"), wup_in[l, g], writes=[WUB[g]])
                make_gate_bc(GA2, 40, B_ga2)
                for si, sup in enumerate(supers):
                    offs, goff = win_next
                    for g in range(11):
                        wi = wuc[0] % 2
                        wuc[0] += 1
                        if not (si == 0 and g < 2):
                            kb.dma(QP, WU[wi][:].rearrange("p k n -> p (k n)"), wup_in[l, g], writes=[WUB[wi]])
                        if si == 0 and g == 1:
                            for hf in range(2):
                                kb.dma(QP, WD[:, hf].rearrange("p c n -> p (c n)"), wdn_in[l, hf], writes=[WDB[hf]])
                        for cl in range(2):
                            c = 2 * g + cl
                            for t in sup:
                                kind, c0, n = TILES[t]
                                nh = n // 2
                                o_ = offs[t]
                                first = t in (0, 1)
                                lastt = t in (0, 4)
                                bi = upc[0] % 2
                                vbk = (2, 5, 6, 7)[upc[0] % 4]
                                upc[0] += 1
                                gA, gAB = PS[3 * bi], PSB[3 * bi]
                                gBk, gBB = PS[3 * bi + 1], PSB[3 * bi + 1]
                                vv, vvB = PS[vbk], PSB[vbk]
                                for kc in range(8):
                                    wg = WU[wi][:, kc, cl * 256:cl * 256 + 128]
                                    kb.op(PE, lambda kc=kc, wg=wg: nc.tensor.matmul(gA[:, 0:nh + 2], wg, h2w[:, kc, o_:o_ + nh + 2], start=(kc == 0), stop=(kc == 7)),
                                          reads=[WUB[wi], h2wBs[t]], writes=[gAB])
                                for kc in range(8):
                                    wg = WU[wi][:, kc, cl * 256:cl * 256 + 128]
                                    kb.op(PE, lambda kc=kc, wg=wg: nc.tensor.matmul(gBk[:, 0:nh + 2], wg, h2w[:, kc, o_ + nh:o_ + n + 2], start=(kc == 0), stop=(kc == 7)),
                                          reads=[WUB[wi], h2wBs[t]], writes=[gBB])
                                for kc in range(8):
                                    wv = WU[wi][:, kc, cl * 256 + 128:cl * 256 + 256]
                                    kb.op(PE, lambda kc=kc, wv=wv: nc.tensor.matmul(vv[:, 0:n], wv, h2w[:, kc, o_ + 1:o_ + n + 1], start=(kc == 0), stop=(kc == 7)),
                                          reads=[WUB[wi], h2wBs[t]], writes=[vvB])
                                gbuf, gbb = gb[bi], gbB[bi]
                                w0, w1, w2, bb = sm(O_FW + 3 * c), sm(O_FW + 3 * c + 1), sm(O_FW + 3 * c + 2), sm(O_FB + c)
                                kb.op(ACT, lambda: nc.scalar.activation(out=gbuf[:, 0:nh], in_=gA[:, 1:nh + 1], func=AF.Identity, bias=bb, scale=w1),
                                      reads=[gAB, B_const], writes=[gbb])
                                kb.op(ACT, lambda: nc.scalar.activation(out=gbuf[:, nh:n], in_=gBk[:, 1:nh + 1], func=AF.Identity, bias=bb, scale=w1),
                                      reads=[gBB, B_const], writes=[gbb])
                                lo = 1 if first else 0
                                hi = nh - 1 if lastt else nh
                                kb.op(DVE, lambda: nc.vector.scalar_tensor_tensor(out=gbuf[:, lo:nh], in0=gA[:, lo:nh], scalar=w0, in1=gbuf[:, lo:nh], op0=ALU.mult, op1=ALU.add),
                                      reads=[gAB, B_const, gbb], writes=[gbb])
                                kb.op(DVE, lambda: nc.vector.scalar_tensor_tensor(out=gbuf[:, nh:n], in0=gBk[:, 0:nh], scalar=w0, in1=gbuf[:, nh:n], op0=ALU.mult, op1=ALU.add),
                                      reads=[gBB, B_const, gbb], writes=[gbb])
                                kb.op(DVE, lambda: nc.vector.scalar_tensor_tensor(out=gbuf[:, 0:nh], in0=gA[:, 2:nh + 2], scalar=w2, in1=gbuf[:, 0:nh], op0=ALU.mult, op1=ALU.add),
                                      reads=[gAB, B_const, gbb], writes=[gbb])
                                kb.op(DVE, lambda: nc.vector.scalar_tensor_tensor(out=gbuf[:, nh:nh + hi], in0=gBk[:, 2:hi + 2], scalar=w2, in1=gbuf[:, nh:nh + hi], op0=ALU.mult, op1=ALU.add),
                                      reads=[gBB, B_const, gbb], writes=[gbb])
                                sgb, sgbb = sg[bi], sgB[bi]
                                go = goff[t]

                                def part2(gbuf=gbuf, gbb=gbb, sgb=sgb, sgbb=sgbb, vv=vv, vvB=vvB, c=c, go=go, n=n):
                                    kb.op(ACT, lambda: nc.scalar.activation(out=sgb[:, 0:n], in_=gbuf[:, 0:n], func=AF.Silu), reads=[gbb], writes=[sgbb])
                                    kb.op(DVE, lambda: nc.vector.tensor_tensor(out=G[:, c, go:go + n], in0=vv[:, 0:n], in1=sgb[:, 0:n], op=ALU.mult),
                                          reads=[vvB, sgbb], writes=[GB_], ww=False)
                                if up_defer[0] is not None:
                                    up_defer[0]()
                                up_defer[0] = part2
                    if up_defer[0] is not None:
                        up_defer[0]()
                        up_defer[0] = None
                    if si + 1 < len(supers):
                        win_next = load_windows(supers[si + 1], prev=sup)
                    marks.append((f'L{l}down', PE.cnt))
                    for t in sup:
                        kind, c0, n = TILES[t]
                        v = 1 if kind == "c" else 0
                        if not last:
                            hT, hB = get_hT()
                        pending = None
                        for st in range(n // 128):
                            sti = c0 // 128 + st
                            xt, xB = get_xt()
                            kb.dma(QS, xt[:], res_ap(sti), reads=[RES[sti]], writes=[xB])
                            go = goff[t] + st * 128
                            for hf in range(2):
                                ps, psB = nextps()
                                for c in range(NCF):
                                    kb.op(PE, lambda c=c, hf=hf, ps=ps, go=go: nc.tensor.matmul(ps[:, :], G[:, c, go:go + 128], WD[:, hf, c, :], start=(c == 0), stop=(c == NCF - 1)),
                                          reads=[GB_, WDB[hf]], writes=[psB])
                                kb.op(DVE, lambda hf=hf, ps=ps: nc.vector.tensor_tensor(out=tmpx[:], in0=ps[:, :], in1=GA2[:, v, hf * 512:(hf + 1) * 512], op=ALU.mult),
                                      reads=[psB, B_ga2], writes=[tmpxB])
                                kb.op(DVE, lambda hf=hf, xt=xt: nc.vector.tensor_tensor(out=xt[:, hf * 512:(hf + 1) * 512], in0=xt[:, hf * 512:(hf + 1) * 512], in1=tmpx[:], op=ALU.add),
                                      reads=[tmpxB, xB], writes=[xB])
                            if last:
                                r0, r0B = tok_sumsq(xt, xB)
                                r, rB = rstd_from(r0[:, 0:1], r0B, D)
                                kb.op(DVE, lambda xt=xt, r=r: nc.vector.scalar_tensor_tensor(out=xt[:], in0=xt[:], scalar=r[:, 3:4], in1=fg[:], op0=ALU.mult, op1=ALU.mult),
                                      reads=[xB, rB, B_const], writes=[xB])
                                kb.dma(QA, res_ap(sti), xt[:], reads=[xB], writes=[RES[sti]])
                            else:
                                kb.dma(QA, res_ap(sti), xt[:], reads=[xB], writes=[RES[sti]])
                                newp = norm_to_hT(xt, xB, l + 1, 0, v, hT, hB, st)
                                for _ in range(3):
                                    next(newp)
                                if pending is not None:
                                    for _ in pending:
                                        pass
                                pending = newp
                        if not last:
                            if pending is not None:
                                for _ in pending:
                                    pass
                                pending = None
                            kb.dma(QA, h1_s[:, :, c0:c0 + n], hT[:, :, 0:n], reads=hB, writes=[H1B[t]])
                xt_t.pop()
                xtB.pop()
                kb.barrier()
        kb.barrier()
        marks.append(('end', PE.cnt))
        build_nc.marks = marks
    return nc


def _wP(w):
    K = w.shape[0] // 128
    return np.ascontiguousarray(w.reshape(K, 128, -1).transpose(1, 0, 2).reshape(128, -1))


def _chunkP(v):
    return np.ascontiguousarray(v.reshape(-1, 128).T)


_CONST_CACHE = {}


def _constants():
    if _CONST_CACHE:
        return _CONST_CACHE
    bf = ml_dtypes.bfloat16
    t = np.arange(S)
    r, c = (t // 64).astype(np.float64), (t % 64).astype(np.float64)
    inv = 10000.0 ** (-np.arange(0, 32, 2, dtype=np.float64) / 32)
    d = np.arange(64)
    a, j, i = d // 32, (d % 32) // 16, d % 16
    pos = np.where(a[:, None] == 0, r[None, :], c[None, :])
    ang = (pos.astype(np.float32) * inv.astype(np.float32)[i][:, None]).astype(np.float32).astype(np.float64)
    cosT = np.cos(ang)
    sinT = np.sin(ang) * np.where(j == 0, -1.0, 1.0)[:, None]
    rope = np.stack([np.concatenate([cosT, cosT], 0), np.concatenate([sinT, sinT], 0)]).astype(np.float32)
    tt = (np.arange(8)[:, None, None] * 256 + np.arange(2)[None, :, None] * 128 + np.arange(128)[None, None, :])
    dftl = np.empty((8, 8, 128, 2, 2, 256), dtype=bf)
    for kt in range(8):
        kk = kt * 256 + np.arange(256)
        ph = (tt[..., None] * kk[None, None, None, :]) % S
        angl = ph.astype(np.float64) * (2 * np.pi / S)
        dftl[kt, :, :, :, 0, :] = np.cos(angl).transpose(0, 2, 1, 3).astype(bf)
        dftl[kt, :, :, :, 1, :] = np.sin(angl).transpose(0, 2, 1, 3).astype(bf)
    dftl = dftl.reshape(8, 8, 128, 2 * 2 * 256)
    tc_ = np.arange(2)[:, None] * 128 + np.arange(128)[None, :]
    phc = (tc_[..., None] * np.arange(256)[None, None, :]) % CT
    angc = phc.astype(np.float64) * (2 * np.pi / CT)
    dftc = np.stack([np.cos(angc), np.sin(angc)], axis=2).transpose(1, 0, 2, 3).astype(bf).reshape(128, 2 * 2 * 256)
    m = np.arange(64)
    a64 = ((m[:, None] * m[None, :]) % 64).astype(np.float64) * (2 * np.pi / 64)
    bc = np.zeros((128, 128))
    bs = np.zeros((128, 128))
    for g in range(2):
        bc[g * 64:(g + 1) * 64, g * 64:(g + 1) * 64] = np.cos(a64)
        bs[g * 64:(g + 1) * 64, g * 64:(g + 1) * 64] = -np.sin(a64)
    bcs = np.concatenate([bc, bs], 1).astype(bf)
    _CONST_CACHE.update(rope=rope, dftl=dftl, dftc=dftc, bcs=bcs, idb=np.eye(128).astype(bf), idf=np.eye(128, dtype=np.float32))
    return _CONST_CACHE


def _prep_shared(inp):
    f = lambda k: np.asarray(inp[k], dtype=np.float32)
    d64 = np.arange(64)
    sw = np.where(d64 % 32 < 16, d64 + 16, d64 - 16)
    kr = 640 + d64
    krs = 640 + sw
    win_idx = np.concatenate([np.arange(0, 640), kr, kr, krs, krs, np.arange(704, 1728)])
    up_idx = np.concatenate([np.concatenate([np.arange(c * 128, (c + 1) * 128), DFF + np.arange(c * 128, (c + 1) * 128)]) for c in range(NCF)])
    smalls = np.zeros((DEPTH, 128, NSM), np.float32)
    ada = np.empty((DEPTH, 12, 128, 8 * 512), np.float32)
    win = np.empty((DEPTH, 128, 8 * NWIN), np.float32)
    wuq = np.empty((DEPTH, 128, 3 * 1024), np.float32)
    wukv = np.empty((DEPTH, 128, 2 * 1024), np.float32)
    wout = np.empty((DEPTH, 128, 8 * 1024), np.float32)
    wup = np.empty((DEPTH, 11, 128, 8 * 512), np.float32)
    wdn = np.empty((DEPTH, 2, 128, NCF * 512), np.float32)
    for l in range(DEPTH):
        s = smalls[l]
        s[:, O_G1:O_G1 + 8] = _chunkP(f("norm1_g")[l])
        s[:, O_QG:O_QG + 3] = _chunkP(f("q_norm_g")[l])
        s[:, O_KVG:O_KVG + 2] = _chunkP(f("kv_norm_g")[l])
        sw_ = f("sconv_w")[l]
        for j in range(2):
            for tap in range(3):
                s[:, O_SW + 3 * j + tap] = sw_[tap, j * 128:(j + 1) * 128]
        s[:, O_SB:O_SB + 2] = _chunkP(f("sconv_b")[l])
        s[:, O_OG:O_OG + 8] = _chunkP(f("out_norm_g")[l])
        s[:, O_G2:O_G2 + 8] = _chunkP(f("norm2_g")[l])
        fw = f("ffconv_w")[l]
        for c in range(NCF):
            for tap in range(3):
                s[:, O_FW + 3 * c + tap] = fw[tap, c * 128:(c + 1) * 128]
        s[:, O_FB:O_FB + NCF] = _chunkP(f("ffconv_b")[l])
        s[:, O_AB:O_AB + 48] = _chunkP(f("ada_b")[l])
        aw = f("ada_w")[l]
        for g in range(12):
            ada[l, g] = _wP(aw[:, g * 512:(g + 1) * 512])
        win[l] = _wP(f("w_in")[l][:, win_idx])
        uq = f("w_uq")[l]
        cols = [uq[:, h, 0:128] for h in range(4)]
        cols += [np.concatenate([uq[:, 2 * pr, 128:192], uq[:, 2 * pr + 1, 128:192]], 1) for pr in range(2)]
        cols += [np.concatenate([uq[:, 2 * pr, 128 + sw], uq[:, 2 * pr + 1, 128 + sw]], 1) for pr in range(2)]
        wuq[l] = _wP(np.concatenate(cols, 1))
        ukv = f("w_ukv")[l]
        wukv[l] = _wP(np.concatenate([ukv[:, h, 0:128] for h in range(4)] + [ukv[:, h, 128:256] for h in range(4)], 1))
        wout[l] = _wP(f("w_out")[l])
        upp = f("w_up")[l][:, up_idx]
        for g in range(11):
            wup[l, g] = _wP(upp[:, g * 512:(g + 1) * 512])
        dn = f("w_down")[l]
        for hf in range(2):
            wdn[l, hf] = _wP(dn[:, hf * 512:(hf + 1) * 512])
    fg = np.ascontiguousarray(np.broadcast_to(f("final_g")[None, :], (128, D)))
    return dict(smalls=smalls, ada=ada, win=win, wuq=wuq, wukv=wukv, wout=wout, wup=wup, wdn=wdn, fg=fg)


def kernel(**inputs):
    consts = _constants()
    shared = _prep_shared(inputs)
    x = np.asarray(inputs["x"], np.float32)
    c = np.asarray(inputs["c"], np.float32)
    ctx = np.asarray(inputs["ctx"], np.float32)
    c_ctx = np.asarray(inputs["c_ctx"], np.float32)
    nc = build_nc()
    in_maps = []
    for b in range(8):
        cv = np.stack([_chunkP(c[b]), _chunkP(c_ctx)], axis=2).reshape(128, 16)
        m = dict(x=np.ascontiguousarray(x[b]), ctx=np.ascontiguousarray(ctx[b]), cvec=np.ascontiguousarray(cv))
        m.update(shared)
        m.update(consts)
        in_maps.append(m)
    res = run_bass_kernel_spmd(nc, in_maps, core_ids=list(range(8)))
    return np.stack([np.asarray(r["out"], dtype=np.float32) for r in res.results], axis=0)
```

```python
import contextlib
import itertools
import numpy as np
import ml_dtypes
import concourse.bass as bass
import concourse.mybir as mybir
from concourse.bass_utils import run_bass_kernel_spmd

F32 = mybir.dt.float32
BF16 = mybir.dt.bfloat16
AF = mybir.ActivationFunctionType
ALU = mybir.AluOpType

P = 128
D = 1024
S = 2048
CT = 256
T = S + CT
DFF = 2816
NCF = 22
DEPTH = 2
EPS = 1e-6
SM_SCALE = 192 ** -0.5
NWIN = 1920
O_G1, O_QG, O_KVG, O_SW, O_SB, O_OG, O_G2, O_FW, O_FB, O_AB = 0, 8, 11, 13, 19, 21, 29, 37, 103, 125
NSM = 173
TILES = [("c", 0, 256)] + [("l", 256 + 512 * i, 512) for i in range(4)]
H2C = {0: 1, 1: 258, 2: 258 + 512, 3: 258 + 1024, 4: 258 + 1536}
H2W = 2307


class Buf:
    __slots__ = ("lw", "rd", "name")

    def __init__(self, name=""):
        self.lw = {}
        self.rd = {}
        self.name = name


class Eng:
    def __init__(self, obj, sem, inorder=False, kind="dve"):
        self.obj = obj
        self.sem = sem
        self.cnt = 0
        self.waited = {}
        self.inorder = inorder
        self.kind = kind
        self.free = 0.0


class DmaQ:
    def __init__(self, issuer, sems):
        self.issuer = issuer
        self.sems = sems
        self.vals = [0] * len(sems)
        self.n = 0
        self.free = 0.0


class Task:
    def __init__(self, gen):
        self.gen = gen
        self.vt = 0.0


class KB:
    SEM_LAT = 120.0

    def __init__(self, nc):
        self.nc = nc
        self.engs = []
        self.qs = []
        self.fin = {}
        self.cur = None
        self.now = 0.0
        self.rr = 0
        self.round_robin = False
        self.force_ww = False

    def wait(self, eng, tok):
        sem, val = tok
        if eng.waited.get(sem.num, 0) >= val:
            return
        eng.obj.wait_ge(sem, val)
        eng.waited[sem.num] = val

    def _deps(self, eng, reads, writes, ww):
        t = 0.0
        toks = []
        for b in reads:
            toks.extend(b.lw.values())
        for b in writes:
            if ww:
                toks.extend(b.lw.values())
            toks.extend(b.rd.values())
        for tok in toks:
            same = tok[0] is eng.sem
            t = max(t, self.fin.get((tok[0].num, tok[1]), 0.0) + (0.0 if same else self.SEM_LAT))
            if not (eng.inorder and same):
                self.wait(eng, tok)
        return t

    def _mark(self, tok, reads, writes, ww=True):
        for b in reads:
            b.rd[tok[0].num] = tok
        for b in writes:
            if ww:
                b.lw = {tok[0].num: tok}
                b.rd = {}
            else:
                b.lw[tok[0].num] = tok

    def op(self, eng, fn, reads=(), writes=(), n=512, ww=True):
        ww = True if self.force_ww else ww
        tdep = self._deps(eng, reads, writes, ww)
        ins = fn()
        eng.cnt += 1
        ins.then_inc(eng.sem, 1)
        tok = (eng.sem, eng.cnt)
        if eng.inorder:
            cost, lat = n / 2.4 + 12.0, 120.0
        elif eng.kind == "act":
            cost, lat = 200.0 + n * 1.0, 60.0
        elif eng.kind == "pool":
            cost, lat = 150.0 + n * 2.0, 60.0
        else:
            cost, lat = 90.0 + n * 1.0, 60.0
        start = max(eng.free, tdep, self.cur.vt if self.cur is not None else 0.0)
        eng.free = start + cost
        self.fin[(tok[0].num, tok[1])] = start + cost + lat
        if self.cur is not None:
            self.cur.vt = start
        self._mark(tok, reads, writes, ww)

    def dma(self, q, out, in_, reads=(), writes=(), slow=False, nbytes=1 << 20, ww=True):
        eng = q.issuer
        tdep = self._deps(eng, reads, writes, ww)
        k = q.n % len(q.sems)
        q.n += 1
        if q.vals[k] > 0:
            self.wait(eng, (q.sems[k], q.vals[k]))
            tdep = max(tdep, self.fin.get((q.sems[k].num, q.vals[k]), 0.0))
        ins = eng.obj.dma_start(out=out, in_=in_, allow_slow_non_contiguous=True) if slow else eng.obj.dma_start(out=out, in_=in_)
        q.vals[k] += 16
        ins.then_inc(q.sems[k], 16)
        tok = (q.sems[k], q.vals[k])
        start = max(eng.free, tdep, self.cur.vt if self.cur is not None else 0.0)
        eng.free = start + 60.0
        q.free = max(q.free, start) + nbytes / 150.0
        self.fin[(tok[0].num, tok[1])] = max(q.free, start + 2200.0)
        self._mark(tok, reads, writes, ww)

    def barrier(self):
        toks = [(e.sem, e.cnt) for e in self.engs if e.cnt > 0]
        for q in self.qs:
            toks += [(s, v) for s, v in zip(q.sems, q.vals) if v > 0]
        tmax = max([e.free for e in self.engs] + [self.fin.get((t[0].num, t[1]), 0.0) for t in toks])
        for e in self.engs:
            e.free = tmax
            for tok in toks:
                if tok[0] is e.sem:
                    continue
                self.wait(e, tok)

    def run_tasks(self, gens):
        tasks = [Task(g) for g in gens if g is not None]
        t0 = max(e.free for e in self.engs) if False else min(e.free for e in self.engs)
        for t in tasks:
            t.vt = t0
        while tasks:
            t = tasks[self.rr % len(tasks)] if self.round_robin else min(tasks, key=lambda x: x.vt)
            self.rr += 1
            self.cur = t
            try:
                next(t.gen)
            except StopIteration:
                tasks.remove(t)
            self.cur = None


def run_tasks(tasks):
    act = [t for t in tasks if t is not None]
    while act:
        for t in list(act):
            try:
                next(t)
            except StopIteration:
                act.remove(t)


def build_nc(n_layers=DEPTH):
    nc = bass.Bass("TRN2", target_bir_lowering=False)
    dt = nc.dram_tensor
    x_in = dt("x", [S, D], F32, kind="ExternalInput").ap()
    ctx_in = dt("ctx", [CT, D], F32, kind="ExternalInput").ap()
    cvec_in = dt("cvec", [P, 16], F32, kind="ExternalInput").ap()
    smalls_in = dt("smalls", [DEPTH, P, NSM], F32, kind="ExternalInput").ap()
    ada_in = dt("ada", [DEPTH, 12, P, 8 * 512], F32, kind="ExternalInput").ap()
    win_in = dt("win", [DEPTH, P, 8 * NWIN], F32, kind="ExternalInput").ap()
    wuq_in = dt("wuq", [DEPTH, P, 3 * 1024], F32, kind="ExternalInput").ap()
    wukv_in = dt("wukv", [DEPTH, P, 2 * 1024], F32, kind="ExternalInput").ap()
    wout_in = dt("wout", [DEPTH, P, 8 * 1024], F32, kind="ExternalInput").ap()
    wup_in = dt("wup", [DEPTH, 11, P, 8 * 512], F32, kind="ExternalInput").ap()
    wdn_in = dt("wdn", [DEPTH, 2, P, NCF * 512], F32, kind="ExternalInput").ap()
    fg_in = dt("fg", [P, D], F32, kind="ExternalInput").ap()
    rope_in = dt("rope", [2, P, S], F32, kind="ExternalInput").ap()
    dftl_in = dt("dftl", [8, 8, P, 2 * 2 * 256], BF16, kind="ExternalInput").ap()
    dftc_in = dt("dftc", [P, 2 * 2 * 256], BF16, kind="ExternalInput").ap()
    bcs_in = dt("bcs", [P, 256], BF16, kind="ExternalInput").ap()
    idb_in = dt("idb", [P, P], BF16, kind="ExternalInput").ap()
    idf_in = dt("idf", [P, P], F32, kind="ExternalInput").ap()
    out = dt("out", [S, D], F32, kind="ExternalOutput").ap()
    xc_s = dt("xc_s", [CT, D], F32, kind="Internal").ap()
    h1_s = dt("h1_s", [P, 8, T], BF16, kind="Internal").ap()
    h2_s = dt("h2_s", [P, 8, H2W], BF16, kind="Internal").ap()
    v_s = dt("v_s", [P, 2, H2W], F32, kind="Internal").ap()
    bg_s = dt("bg_s", [P, 2, T], F32, kind="Internal").ap()
    cqn_s = dt("cqn_s", [P, 3, T], BF16, kind="Internal").ap()

    es = contextlib.ExitStack()
    with es:
        E = es.enter_context
        kb = KB(nc)
        marks = []
        nc_marks = marks
        PE = Eng(nc.tensor, E(nc.semaphore("s_pe")), inorder=True)
        ACT = Eng(nc.scalar, E(nc.semaphore("s_act")), kind="act")
        DVE = Eng(nc.vector, E(nc.semaphore("s_dve")))
        POOL = Eng(nc.gpsimd, E(nc.semaphore("s_pool")), kind="pool")
        SP = Eng(nc.sync, E(nc.semaphore("s_sp")), kind="sp")
        kb.engs = [PE, ACT, DVE, POOL, SP]
        QS = DmaQ(SP, [E(nc.semaphore(f"qs{i}")) for i in range(12)])
        QP = DmaQ(POOL, [E(nc.semaphore(f"qp{i}")) for i in range(6)])
        QA = DmaQ(ACT, [E(nc.semaphore(f"qa{i}")) for i in range(4)])
        kb.qs = [QS, QP, QA]

        nmc = [0]

        def sb(name, shape, dtype, scope=None):
            nmc[0] += 1
            return (scope or E)(nc.sbuf_tensor(f"sb{nmc[0]}_{name}", shape, dtype))

        PS = [E(nc.psum_tensor(f"ps{i}", [P, 512], F32)) for i in range(8)]
        PSB = [Buf(f"ps{i}") for i in range(8)]
        psrot = [5]

        def nextps():
            i = psrot[0]
            psrot[0] = 5 + (i - 5 + 1) % 3
            return PS[i], PSB[i]

        idb = sb("idb", [P, P], BF16)
        idf = sb("idf", [P, P], F32)
        ones = sb("ones", [P, P], BF16)
        bcs = sb("bcs", [P, 256], BF16)
        dftc = sb("dftc", [P, 2, 2, 256], BF16)
        fg = sb("fg", [P, D], F32)
        smalls = sb("smalls", [P, DEPTH, NSM], F32)
        cvec = sb("cvec", [P, 16], F32)
        csil = sb("csil", [P, 16], BF16)
        MOD = sb("mod", [P, DEPTH, 48, 2], F32)
        GS = sb("gs", [P, DEPTH, 2, 8, 2], F32)
        B_const = Buf("const")
        B_mod = Buf("mod")
        MODB = [[Buf(f"mod{l}_{g}") for g in range(12)] for l in range(DEPTH)]
        GSB = [[Buf(), Buf()] for l in range(DEPTH)]
        NSS = 16
        ss = [sb(f"ss{i}", [P, 4], F32) for i in range(NSS)]
        ssB = [Buf() for _ in range(NSS)]
        ssrot = [0]

        for dst, src in ((idb[:], idb_in), (idf[:], idf_in), (bcs[:], bcs_in), (fg[:], fg_in),
                         (cvec[:], cvec_in), (dftc[:].rearrange("p a b c -> p (a b c)"), dftc_in)):
            kb.dma(QS, dst, src, writes=[B_const], ww=False)
        for l in range(DEPTH):
            kb.dma(QS, smalls[:, l, :], smalls_in[l], writes=[B_const], ww=False)
        kb.op(POOL, lambda: nc.gpsimd.memset(ones[:], 1.0), writes=[B_const], ww=False)
        gfill = [sb(f"gfill{i}", [P, P], F32) for i in range(2)]
        gfillB = [Buf(), Buf()]
        epsc = sb("epsc", [P, 1], F32)
        kb.op(POOL, lambda: nc.gpsimd.memset(epsc[:], EPS), writes=[B_const], ww=False)
        kb.op(ACT, lambda: nc.scalar.activation(out=csil[:], in_=cvec[:], func=AF.Silu), reads=[B_const], writes=[B_mod])
        PADB = Buf("pads")
        zpad = sb("zpad", [P, 8, 2], F32)
        kb.op(POOL, lambda: nc.gpsimd.memset(zpad[:], 0.0), writes=[B_const], ww=False)
        for pc in (0, 257, 2306):
            kb.dma(QS, h2_s[:, :, pc:pc + 1], zpad[:].bitcast(BF16)[:, :, 0:1], reads=[B_const], writes=[PADB], slow=True)
            kb.dma(QS, v_s[:, :, pc:pc + 1], zpad[:, 0:2, 0:1], reads=[B_const], writes=[PADB], slow=True)
        RES = [Buf(f"res{i}") for i in range(18)]

        def res_ap(sti):
            if sti < 2:
                return xc_s[sti * 128:(sti + 1) * 128, :]
            return out[(sti - 2) * 128:(sti - 1) * 128, :]

        xt_t = [sb(f"xt{i}", [P, D], F32) for i in range(2)]
        xtB = [Buf() for _ in range(2)]
        xn_t = [sb(f"xn{i}", [P, D], BF16) for i in range(2)]
        xnB = [Buf() for _ in range(2)]
        junk = sb("junk", [P, D], BF16)
        junkB = Buf()
        hT_t = [sb(f"hT{i}", [P, 8, 512], BF16) for i in range(2)]
        hTB = [[Buf(), Buf()] for _ in range(2)]
        cnt = {"xt": 0, "xn": 0, "hT": 0, "alt": 0}

        def rstd_from(ssum_ap, ssumB, n_feat):
            i = ssrot[0]
            ssrot[0] = (i + 1) % NSS
            r, rB = ss[i], ssB[i]
            kb.op(ACT, lambda: nc.scalar.activation(out=r[:, 2:3], in_=ssum_ap, func=AF.Ln, scale=1.0 / n_feat, bias=epsc[:, 0:1]),
                  reads=[ssumB, B_const], writes=[rB], n=1)
            kb.op(ACT, lambda: nc.scalar.activation(out=r[:, 3:4], in_=r[:, 2:3], func=AF.Exp, scale=-0.5), reads=[rB], writes=[rB], n=1)
            return r, rB

        def tok_sumsq(xt, xB, junk_=None, junkB_=None):
            junk_, junkB_ = (junk, junkB) if junk_ is None else (junk_, junkB_)
            i = ssrot[0]
            ssrot[0] = (i + 1) % NSS
            r, rB = ss[i], ssB[i]
            kb.op(DVE, lambda: nc.vector.memset(r[:, 0:1], 0.0), writes=[rB], n=1)
            kb.op(ACT, lambda: nc.scalar.activation(out=junk_[:], in_=xt[:], func=AF.Square, accum_out=r[:, 0:1]),
                  reads=[xB, rB], writes=[junkB_, rB], n=1024)
            return r, rB

        def norm_to_hT(xt, xB, l, w, v, hT, hB, st, bank=None, bufs=None, evac=None):
            if bufs is None:
                r0, r0B = tok_sumsq(xt, xB)
            else:
                r0, r0B = tok_sumsq(xt, xB, bufs[2], bufs[3])
            yield
            r, rB = rstd_from(r0[:, 0:1], r0B, D)
            yield
            if bufs is None:
                i = cnt["xn"] % 2
                cnt["xn"] += 1
                xn, xnb = xn_t[i], xnB[i]
            else:
                xn, xnb = bufs[0], bufs[1]
            modb = [GSB[l][w]] + (MODB[l][0:2] if w == 0 else MODB[l][6:8])
            kb.op(DVE, lambda: nc.vector.tensor_scalar(out=xn[:], in0=xt[:], scalar1=r[:, 3:4], scalar2=0.0, op0=ALU.mult, op1=ALU.add),
                  reads=[xB, rB], writes=[xnb], n=1024)
            yield
            ps, psB = nextps() if bank is None else (PS[bank], PSB[bank])
            psb = ps[:].bitcast(BF16)
            for k in range(8):
                kb.op(PE, lambda k=k: nc.tensor.transpose(psb[:, k * 128:(k + 1) * 128], xn[:, k * 128:(k + 1) * 128], idb[:]),
                      reads=[xnb, B_const], writes=[psB], n=128)
            yield
            sh0 = 0 if w == 0 else 24
            use_act = cnt["alt"] % 3 == 0
            cnt["alt"] += 1
            if evac is not None:
                use_act = evac == "act"
            for k in range(8):
                o = hT[:, k, st * 128:(st + 1) * 128]
                i_ = psb[:, k * 128:(k + 1) * 128]
                g_ = GS[:, l, w, k, v:v + 1]
                s_ = MOD[:, l, sh0 + k, v:v + 1]
                if use_act:
                    kb.op(ACT, lambda o=o, i_=i_, g_=g_, s_=s_: nc.scalar.activation(out=o, in_=i_, func=AF.Identity, bias=s_, scale=g_),
                          reads=[psB] + modb, writes=[hB[0]], n=128)
                else:
                    kb.op(DVE, lambda o=o, i_=i_, g_=g_, s_=s_: nc.vector.tensor_scalar(out=o, in0=i_, scalar1=g_, scalar2=s_,
                                                                                     op0=ALU.mult, op1=ALU.add),
                          reads=[psB] + modb, writes=[hB[1]], n=128)
                if k % 2 == 1:
                    yield

        H1B = [Buf(f"h1_{i}") for i in range(5)]
        H2B = [Buf(f"h2_{i}") for i in range(5)]
        VSB = [Buf(f"vs_{i}") for i in range(5)]
        BGB = [Buf(f"bg_{i}") for i in range(5)]

        def get_xt():
            i = cnt["xt"] % len(xt_t)
            cnt["xt"] += 1
            return xt_t[i], xtB[i]

        def get_hT():
            i = cnt["hT"] % 2
            cnt["hT"] += 1
            return hT_t[i], hTB[i]

        def ada_gen(l, groups, aslot, aB, banks):
            ns = len(aslot)
            groups = list(groups)
            issued = 0
            for gi, g in enumerate(groups):
                while issued < len(groups) and issued < gi + ns:
                    gg = groups[issued]
                    kb.dma(QP, aslot[gg % ns][:].rearrange("p k n -> p (k n)"), ada_in[l, gg], writes=[aB[gg % ns]])
                    issued += 1
                    if issued > ns:
                        break
                if gi == 0:
                    yield
                sl, slB = aslot[g % ns], aB[g % ns]
                bk = banks[g % 2]
                psm = PS[bk][:, 0:8]
                for j in range(4):
                    for kc in range(8):
                        kb.op(PE, lambda j=j, kc=kc, sl=sl, psm=psm: nc.tensor.matmul(
                            psm[:, 2 * j:2 * j + 2], sl[:, kc, j * 128:(j + 1) * 128], csil[:, 2 * kc:2 * kc + 2],
                            start=(kc == 0), stop=(kc == 7)), reads=[slB, B_mod], writes=[PSB[bk]], n=32)
                for v in range(2):
                    kb.op(DVE, lambda v=v, g=g, psm=psm: nc.vector.tensor_tensor(
                        out=MOD[:, l, 4 * g:4 * g + 4, v], in0=psm.rearrange("p (j v) -> p j v", v=2)[:, :, v],
                        in1=smalls[:, l, O_AB + 4 * g:O_AB + 4 * g + 4], op=ALU.add),
                        reads=[PSB[bk], B_const], writes=[MODB[l][g]], n=8, ww=False)
                for w, (sc0, og, glast) in enumerate(((8, O_G1, 3), (32, O_G2, 9))):
                    if g == glast:
                        for v in range(2):
                            kb.op(DVE, lambda w=w, v=v, sc0=sc0, og=og: nc.vector.scalar_tensor_tensor(
                                out=GS[:, l, w, :, v], in0=MOD[:, l, sc0:sc0 + 8, v], scalar=1.0,
                                in1=smalls[:, l, og:og + 8], op0=ALU.add, op1=ALU.mult),
                                reads=[MODB[l][glast - 1], MODB[l][glast], B_const], writes=[GSB[l][w]], n=8, ww=False)
                yield

        marks.append(('pass0', PE.cnt))
        with contextlib.ExitStack() as s0:
            A0 = s0.enter_context
            aslot = [sb(f"aslot{i}", [P, 8, 512], BF16, A0) for i in range(2)]
            aB = [Buf(), Buf()]
            p0 = []
            for ti in range(5):
                p0.append(dict(xt=sb(f"p0xt{ti}", [P, D], F32, A0), xB=Buf(), xn=sb(f"p0xn{ti}", [P, D], BF16, A0), xnB=Buf(),
                               junk=sb(f"p0j{ti}", [P, D], BF16, A0), junkB=Buf(), hT=sb(f"p0hT{ti}", [P, 8, 512], BF16, A0), hB=[Buf(), Buf()]))

            def p0_task(ti):
                kind, c0, n = TILES[ti]
                v = 1 if kind == "c" else 0
                b = p0[ti]
                for st in range(n // 128):
                    sti = c0 // 128 + st
                    src = ctx_in[st * 128:(st + 1) * 128, :] if kind == "c" else x_in[(sti - 2) * 128:(sti - 1) * 128, :]
                    kb.dma(QS, b["xt"][:], src, writes=[b["xB"]])
                    yield from norm_to_hT(b["xt"], b["xB"], 0, 0, v, b["hT"], b["hB"], st, bank=2 + ti,
                                          bufs=(b["xn"], b["xnB"], b["junk"], b["junkB"]))
                kb.dma(QA, h1_s[:, :, c0:c0 + n], b["hT"][:, :, 0:n], reads=b["hB"], writes=[H1B[ti]])

            kb.run_tasks([ada_gen(0, range(12), aslot, aB, (0, 1))] + [p0_task(ti) for ti in range(5)])
            kb.barrier()

        for l in range(n_layers):
            last = l == DEPTH - 1

            def sm(o, n=1, l=l):
                return smalls[:, l, o:o + n]

            with contextlib.ExitStack() as s1:
                A = s1.enter_context
                KN = sb("KN", [P, 4, T], BF16, A)
                KR = sb("KR", [P, T], BF16, A)
                Vt = sb("Vt", [P, 18, 512], BF16, A)
                CQS = [Buf(f"cqs{i}") for i in range(5)]
                U = sb("U", [P, 18, 256], BF16, A)
                KVB = [Buf(f"kv{i}") for i in range(5)]
                B_rope = Buf()
                B_w = Buf("wmix")
                GA1 = sb("GA1", [P, 2, D], F32, A)
                WUQ = sb("WUQ", [P, 3, 1024], BF16, A)
                B_ga = Buf()

                def make_gate_bc(dst, ch0, dstB):
                    for v in range(2):
                        for j in range(8):
                            fill, fB = gfill[j % 2], gfillB[j % 2]
                            kb.op(ACT, lambda v=v, j=j, fill=fill: nc.scalar.activation(out=fill[:], in_=idf[:], func=AF.Identity,
                                                                                        bias=MOD[:, l, ch0 + j, v:v + 1], scale=0.0),
                                  reads=MODB[l][ch0 // 4:ch0 // 4 + 2] + [B_const], writes=[fB], n=128)
                            ps, psB = nextps()
                            kb.op(PE, lambda ps=ps, fill=fill: nc.tensor.matmul(ps[:, 0:128], fill[:], idf[:], start=True, stop=True),
                                  reads=[fB, B_const], writes=[psB], n=512)
                            kb.op(DVE, lambda ps=ps, v=v, j=j: nc.vector.tensor_copy(out=dst[:, v, j * 128:(j + 1) * 128], in_=ps[:, 0:128]),
                                  reads=[psB], writes=[dstB], n=128, ww=False)


                marks.append((f'L{l}pass1', PE.cnt))
                with contextlib.ExitStack() as s2:
                    A2 = s2.enter_context
                    WIN = sb("WIN", [P, 8, NWIN], BF16, A2)
                    WIN_SEGS = ((0, 640), (640, 1920))
                    B_wins = [Buf() for _ in WIN_SEGS]
                    for (a_, b_), wb in zip(WIN_SEGS, B_wins):
                        kb.dma(QP, WIN[:, :, a_:b_], win_in[l].rearrange("p (k n) -> p k n", n=NWIN)[:, :, a_:b_], writes=[wb])

                    def winB(col0, ncols):
                        return [wb for (a_, b_), wb in zip(WIN_SEGS, B_wins) if a_ < col0 + ncols and col0 < b_]
                    WUKV = sb("WUKV", [P, 2, 1024], BF16, A2)
                    kb.dma(QP, WUKV[:].rearrange("p k n -> p (k n)"), wukv_in[l], writes=[B_w])
                    ropeC = sb("ropeC", [P, 512], F32, A2)
                    ropeS = sb("ropeS", [P, 512], F32, A2)
                    kb.dma(QP, WUQ[:].rearrange("p k n -> p (k n)"), wuq_in[l], writes=[B_w])
                    make_gate_bc(GA1, 16, B_ga)
                    cqf = sb("cqf", [P, 3, 512], F32, A2)
                    cqf2 = sb("cqf2", [P, 2, 512], F32, A2)
                    cqf2B = Buf()
                    sq2 = [sb(f"sq2_{i}", [P, 512], BF16, A2) for i in range(2)]
                    sq2B = [Buf(), Buf()]
                    rsb2 = sb("rsb2", [P, 3, 512], F32, A2)
                    rsB2 = Buf()
                    cgs, cgB = cqf2, cqf2B

                    def mkrot(banks):
                        st_ = [0]

                        def r():
                            b = banks[st_[0] % len(banks)]
                            st_[0] += 1
                            return PS[b], PSB[b]
                        return r
                    rot1, rot2 = mkrot((0, 1, 2, 3)), mkrot((4, 5, 6))
                    cqo = [sb(f"cqo{i}", [P, 3, 512], BF16, A2) for i in range(2)]
                    cqoB = [Buf(), Buf()]
                    cqfB = Buf()
                    sq = [sb(f"sq{i}", [P, 512], BF16, A2) for i in range(3)]
                    sqB = [Buf() for _ in range(3)]
                    rsb = sb("rsb", [P, 3, 512], F32, A2)
                    rsB = Buf()
                    ckvn = sb("ckvn", [P, 2, 512], BF16, A2)
                    ckvnB = Buf()
                    tA = sb("tA", [P, 512], F32, A2)
                    tB_ = sb("tB", [P, 512], F32, A2)
                    tAB, tBB = Buf(), Buf()
                    vbuf = [sb(f"vbuf{i}", [P, 2, 512], F32, A2) for i in range(1)] * 2
                    vbB = [Buf()] * 2
                    bgbuf = [sb(f"bgbuf{i}", [P, 2, 512], F32, A2) for i in range(1)] * 2
                    bgbB = [Buf()] * 2
                    sqc = [0]

                    def feat_rstd(nchunks, nfeat, getsrc, n, ps_get=None, sqs=None, sqBs=None, rs=None, rsBuf=None):
                        sq_, sqB_ = (sq, sqB) if sqs is None else (sqs, sqBs)
                        rsb_, rsB_ = (rsb, rsB) if rs is None else (rs, rsBuf)
                        nsq = len(sq_)
                        pss, pssB = (ps_get or nextps)()
                        for j in range(nchunks):
                            src_ap, srcB = getsrc(j)
                            i = sqc[0] % nsq
                            sqc[0] += 1
                            kb.op(ACT, lambda src_ap=src_ap, i=i: nc.scalar.activation(out=sq_[i][:, :n], in_=src_ap, func=AF.Square),
                                  reads=[srcB], writes=[sqB_[i]])
                            kb.op(PE, lambda j=j, i=i: nc.tensor.matmul(pss[:, :n], ones[:], sq_[i][:, :n], start=(j == 0), stop=(j == nchunks - 1)),
                                  reads=[sqB_[i], B_const], writes=[pssB])
                        kb.op(ACT, lambda: nc.scalar.activation(out=rsb_[:, 1, :n], in_=pss[:, :n], func=AF.Ln, scale=1.0 / nfeat, bias=epsc[:, 0:1]),
                              reads=[pssB, B_const], writes=[rsB_])
                        kb.op(ACT, lambda: nc.scalar.activation(out=rsb_[:, 2, :n], in_=rsb_[:, 1, :n], func=AF.Exp, scale=-0.5), reads=[rsB_], writes=[rsB_])

                    def win_mm(ps, psB, col0, ncols, hT, hB, n, c_lo=0):
                        for kc in range(8):
                            kb.op(PE, lambda kc=kc: nc.tensor.matmul(ps[0:ncols, :n], WIN[:, kc, col0:col0 + ncols], hT[:, kc, c_lo:c_lo + n],
                                                                    start=(kc == 0), stop=(kc == 7)), reads=winB(col0, ncols) + hB, writes=[psB])

                    ada_next = None
                    if l + 1 < n_layers:
                        aslot1 = [sb(f"aslotn{i}", [P, 8, 512], BF16, A2) for i in range(2)]
                        ada_next = ada_gen(l + 1, range(12), aslot1, [Buf(), Buf()], (7, 7))
                        next(ada_next, None)
                    print('SBUF remaining pass1', nc.sbuf_bytes_remaining)
                    for ti, (kind, c0, n) in enumerate(TILES):
                        lat = kind == "l"
                        kvb = KVB[ti]
                        hT, hB = get_hT()
                        kb.dma(QS, hT[:, :, 0:n], h1_s[:, :, c0:c0 + n], reads=[H1B[ti]], writes=hB)
                        def T1():
                            nextps = rot1
                            for j in range(3):
                                yield
                                ps, psB = nextps()
                                win_mm(ps, psB, j * 128, 128, hT, hB, n)
                                kb.op(ACT, lambda ps=ps, j=j: nc.scalar.activation(out=cqf[:, j, :n], in_=ps[:, :n], func=AF.Copy),
                                      reads=[psB], writes=[cqfB])
                            feat_rstd(3, 384, lambda j: (cqf[:, j, :n], cqfB), n, ps_get=nextps)
                            for j in range(3):
                                kb.op(DVE, lambda j=j: nc.vector.scalar_tensor_tensor(out=cqo[ti % 2][:, j, :n], in0=cqf[:, j, :n], scalar=sm(O_QG + j),
                                                                                     in1=rsb[:, 2, :n], op0=ALU.mult, op1=ALU.mult),
                                      reads=[cqfB, rsB, B_const], writes=[cqoB[ti % 2]])
                            kb.dma(QP, cqn_s[:, :, c0:c0 + n], cqo[ti % 2][:, :, :n], reads=[cqoB[ti % 2]], writes=[CQS[ti]])
                            yield
                            psa, psaB = nextps()
                            win_mm(psa, psaB, 640, 128, hT, hB, n)
                            if lat:
                                psb_, psbB = nextps()
                                win_mm(psb_, psbB, 768, 128, hT, hB, n)
                                r0 = c0 - 256
                                kb.dma(QS, ropeC[:, :n], rope_in[0][:, r0:r0 + n], writes=[B_rope])
                                kb.dma(QS, ropeS[:, :n], rope_in[1][:, r0:r0 + n], writes=[B_rope])
                                kb.op(DVE, lambda: nc.vector.tensor_tensor(out=tA[:, :n], in0=psa[:, :n], in1=ropeC[:, :n], op=ALU.mult),
                                      reads=[psaB, B_rope], writes=[tAB])
                                kb.op(DVE, lambda: nc.vector.tensor_tensor(out=tB_[:, :n], in0=psb_[:, :n], in1=ropeS[:, :n], op=ALU.mult),
                                      reads=[psbB, B_rope], writes=[tBB])
                                kb.op(DVE, lambda: nc.vector.tensor_tensor(out=KR[:, c0:c0 + n], in0=tA[:, :n], in1=tB_[:, :n], op=ALU.add),
                                      reads=[tAB, tBB], writes=[kvb])
                            else:
                                kb.op(ACT, lambda: nc.scalar.activation(out=KR[:, c0:c0 + n], in_=psa[:, :n], func=AF.Copy), reads=[psaB], writes=[kvb])
                            for st in range(n // 128):
                                yield
                                ps, psB = nextps()
                                for kc in range(8):
                                    kb.op(PE, lambda kc=kc, st=st, ps=ps: nc.tensor.matmul(ps[:, 0:256], hT[:, kc, st * 128:(st + 1) * 128], WIN[:, kc, 896:1152],
                                                                                          start=(kc == 0), stop=(kc == 7)), reads=winB(896, 256) + hB, writes=[psB])
                                kt = c0 // 128 + st
                                if st % 2 == 1:
                                    kb.op(ACT, lambda kt=kt, ps=ps: nc.scalar.activation(out=U[:, kt, :], in_=ps[:, 0:256], func=AF.Copy), reads=[psB], writes=[kvb], ww=False, n=256)
                                else:
                                    kb.op(DVE, lambda kt=kt, ps=ps: nc.vector.tensor_copy(out=U[:, kt, :], in_=ps[:, 0:256]), reads=[psB], writes=[kvb], ww=False, n=256)

                        def T2():
                            nextps = rot2
                            for j in range(2):
                                yield
                                ps, psB = nextps()
                                win_mm(ps, psB, 384 + j * 128, 128, hT, hB, n)
                                kb.op(ACT, lambda ps=ps, j=j: nc.scalar.activation(out=cqf2[:, j, :n], in_=ps[:, :n], func=AF.Copy),
                                      reads=[psB], writes=[cqf2B])
                            feat_rstd(2, 256, lambda j: (cqf2[:, j, :n], cqf2B), n, ps_get=nextps, sqs=sq2, sqBs=sq2B, rs=rsb2, rsBuf=rsB2)
                            for j in range(2):
                                kb.op(DVE, lambda j=j: nc.vector.scalar_tensor_tensor(out=ckvn[:, j, :n], in0=cqf2[:, j, :n], scalar=sm(O_KVG + j),
                                                                                     in1=rsb2[:, 2, :n], op0=ALU.mult, op1=ALU.mult),
                                      reads=[cqf2B, rsB2, B_const], writes=[ckvnB])
                            for h in range(4):
                                yield
                                ps, psB = nextps()
                                for kc in range(2):
                                    kb.op(PE, lambda kc=kc, h=h, ps=ps: nc.tensor.matmul(ps[:, :n], WUKV[:, kc, h * 128:(h + 1) * 128], ckvn[:, kc, :n],
                                                                                        start=(kc == 0), stop=(kc == 1)), reads=[B_w, ckvnB], writes=[psB])
                                if h % 2 == 0:
                                    kb.op(ACT, lambda h=h, ps=ps: nc.scalar.activation(out=KN[:, h, c0:c0 + n], in_=ps[:, :n], func=AF.Copy),
                                          reads=[psB], writes=[kvb], ww=False)
                                else:
                                    kb.op(DVE, lambda h=h, ps=ps: nc.vector.tensor_copy(out=KN[:, h, c0:c0 + n], in_=ps[:, :n]), reads=[psB], writes=[kvb], ww=False)
                            for st in range(n // 128):
                                yield
                                ps, psB = nextps()
                                for kc in range(2):
                                    kb.op(PE, lambda kc=kc, st=st, ps=ps: nc.tensor.matmul(ps[:, :], ckvn[:, kc, st * 128:(st + 1) * 128], WUKV[:, kc, 512:1024],
                                                                                          start=(kc == 0), stop=(kc == 1)), reads=[B_w, ckvnB], writes=[psB])
                                kt = c0 // 128 + st
                                if st % 2 == 0:
                                    kb.op(ACT, lambda kt=kt, ps=ps: nc.scalar.activation(out=Vt[:, kt, :], in_=ps[:, :], func=AF.Copy), reads=[psB], writes=[kvb], ww=False)
                                else:
                                    kb.op(DVE, lambda kt=kt, ps=ps: nc.vector.tensor_copy(out=Vt[:, kt, :], in_=ps[:, :]), reads=[psB], writes=[kvb], ww=False)
                            vb, vbb = vbuf[ti % 2], vbB[ti % 2]
                            bgb, bgbb = bgbuf[ti % 2], bgbB[ti % 2]
                            for j in range(2):
                                yield
                                ps, psB = nextps()
                                win_mm(ps, psB, 1152 + j * 128, 128, hT, hB, n)
                                kb.op(ACT, lambda ps=ps, j=j: nc.scalar.activation(out=bgb[:, j, :n], in_=ps[:, :n], func=AF.Copy), reads=[psB], writes=[bgbb])
                            for j in range(2):
                                yield
                                ps, psB = nextps()
                                win_mm(ps, psB, 1408 + j * 128, 128, hT, hB, n)
                                kb.op(ACT, lambda ps=ps, j=j: nc.scalar.activation(out=cgs[:, j, :n], in_=ps[:, :n], func=AF.Copy), reads=[psB], writes=[cgB])
                            for j in range(2):
                                yield
                                ps, psB = nextps()
                                win_mm(ps, psB, 1664 + j * 128, 128, hT, hB, n)
                                kb.op(DVE, lambda ps=ps, j=j: nc.vector.tensor_tensor(out=vb[:, j, :n], in0=ps[:, :n], in1=cgs[:, j, :n], op=ALU.mult),
                                      reads=[psB, cgB], writes=[vbb])
                            kb.dma(QA, bg_s[:, :, c0:c0 + n], bgb[:, :, :n], reads=[bgbb], writes=[BGB[ti]])
                            kb.dma(QP, v_s[:, :, H2C[ti]:H2C[ti] + n], vb[:, :, :n], reads=[vbb], writes=[VSB[ti]])

                        kb.run_tasks([T1(), T2()] + ([itertools.islice(ada_next, 3)] if ada_next is not None else []))
                    kb.barrier()

                marks.append((f'L{l}pass2', PE.cnt))
                with contextlib.ExitStack() as s2:
                    A2 = s2.enter_context
                    WOUT = sb("WOUT", [P, 8, 1024], BF16, A2)
                    B_wout = Buf()
                    kb.dma(QP, WOUT[:].rearrange("p k n -> p (k n)"), wout_in[l], writes=[B_wout])
                    ropeC = sb("ropeC2", [P, 512], F32, A2)
                    ropeS = sb("ropeS2", [P, 512], F32, A2)
                    DT = [sb(f"DT{i}", [P, 2, 2, 256], BF16, A2) for i in range(3)]
                    DTB = [Buf() for _ in range(3)]
                    dtc = [0]
                    QNl = [sb(f"QN{i}", [P, 4, 512], BF16, A2) for i in range(2)]
                    QRl = [sb(f"QR{i}", [P, 2, 512], BF16, A2) for i in range(2)]
                    qBl = [Buf(), Buf()]
                    CQT = [sb(f"CQT{i}", [P, 3, 512], BF16, A2) for i in range(2)]
                    CQTB = [Buf(), Buf()]
                    tA = sb("tA2", [P, 512], F32, A2)
                    tB_ = sb("tB2", [P, 512], F32, A2)
                    tAB, tBB = Buf(), Buf()
                    rinv, rinvB = tA, tAB
                    PT = [sb(f"PT{i}", [P, 512], BF16, A2) for i in range(3)]
                    PTB = [Buf() for _ in range(3)]
                    attf = sb("attf", [P, 4, 512], F32, A2)
                    attB = Buf()
                    Yt = [sb(f"Y{i}", [P, 8, 512], BF16, A2) for i in range(2)]
                    YAB = [Buf(), Buf()]
                    YBB = [Buf(), Buf()]
                    P12 = sb("P12", [P, 4, 256], BF16, A2)
                    P12B = Buf()
                    yff = sb("yff", [P, 2, 256], F32, A2)
                    yffB = Buf()
                    vwj = sb("vwj", [P, 514], F32, A2)
                    vwB = Buf()
                    bgj = sb("bgj", [P, 512], F32, A2)
                    bgwB = Buf()
                    cacc = sb("cacc", [P, 2, 512], F32, A2)
                    caccB = Buf()
                    sqA = [sb(f"sqA{i}", [P, 512], BF16, A2) for i in range(2)]
                    sqAB = [Buf() for _ in range(2)]
                    sqBt = [sb(f"sqB{i}", [P, 512], BF16, A2) for i in range(2)]
                    sqBB = [Buf() for _ in range(2)]
                    rsA = sb("rsA", [P, 2, 512], F32, A2)
                    rsAB = Buf()
                    rsBt = sb("rsB", [P, 2, 512], F32, A2)
                    rsBB = Buf()
                    tmpx = sb("tmpx", [P, 512], F32, A2)
                    tmpxB = Buf()
                    print('SBUF remaining pass2', nc.sbuf_bytes_remaining)

                    def feat_rstd2(nchunks, nfeat, getsrc, n, bank, sqs, sqBs, rs, rsBuf, cnt_):
                        pss, pssB = PS[bank], PSB[bank]
                        for j in range(nchunks):
                            src_ap, srcB = getsrc(j)
                            i = cnt_[0] % 2
                            cnt_[0] += 1
                            kb.op(DVE, lambda src_ap=src_ap, i=i: nc.vector.tensor_tensor(out=sqs[i][:, :n], in0=src_ap, in1=src_ap, op=ALU.mult),
                                  reads=[srcB], writes=[sqBs[i]], n=n)
                            kb.op(PE, lambda j=j, i=i: nc.tensor.matmul(pss[:, :n], ones[:], sqs[i][:, :n], start=(j == 0), stop=(j == nchunks - 1)),
                                  reads=[sqBs[i], B_const], writes=[pssB], n=n)
                        kb.op(ACT, lambda: nc.scalar.activation(out=rs[:, 0, :n], in_=pss[:, :n], func=AF.Ln, scale=1.0 / nfeat, bias=epsc[:, 0:1]),
                              reads=[pssB, B_const], writes=[rsBuf])
                        kb.op(ACT, lambda: nc.scalar.activation(out=rs[:, 1, :n], in_=rs[:, 0, :n], func=AF.Exp, scale=-0.5), reads=[rsBuf], writes=[rsBuf])

                    cA, cB = [0], [0]

                    def qproj(ti, yi, b0, b1):
                        kind, c0, n = TILES[ti]
                        lat = kind == "l"
                        QN, QR, qB = QNl[yi], QRl[yi], qBl[yi]
                        cq, cqB = CQT[yi], CQTB[yi]
                        kb.dma(QS, cq[:, :, 0:n], cqn_s[:, :, c0:c0 + n], reads=[CQS[ti]], writes=[cqB])
                        if lat:
                            r0 = c0 - 256
                            kb.dma(QS, ropeC[:, :n], rope_in[0][:, r0:r0 + n], writes=[B_rope])
                            kb.dma(QS, ropeS[:, :n], rope_in[1][:, r0:r0 + n], writes=[B_rope])
                        for h in range(4):
                            ps, psB = PS[(b0, b1)[h % 2]], PSB[(b0, b1)[h % 2]]
                            for kc in range(3):
                                kb.op(PE, lambda kc=kc, h=h, ps=ps: nc.tensor.matmul(ps[:, :n], WUQ[:, kc, h * 128:(h + 1) * 128], cq[:, kc, :n],
                                                                                    start=(kc == 0), stop=(kc == 2)), reads=[B_w, cqB], writes=[psB])
                            kb.op(DVE, lambda h=h, ps=ps: nc.vector.tensor_copy(out=QN[:, h, :n], in_=ps[:, :n]), reads=[psB], writes=[qB], ww=False)
                            yield
                        for pr in range(2):
                            psa, psaB = PS[b0], PSB[b0]
                            for kc in range(3):
                                kb.op(PE, lambda kc=kc, pr=pr, psa=psa: nc.tensor.matmul(psa[:, :n], WUQ[:, kc, 512 + pr * 128:640 + pr * 128], cq[:, kc, :n],
                                                                                        start=(kc == 0), stop=(kc == 2)), reads=[B_w, cqB], writes=[psaB])
                            if lat:
                                psb_, psbB = PS[b1], PSB[b1]
                                for kc in range(3):
                                    kb.op(PE, lambda kc=kc, pr=pr, psb_=psb_: nc.tensor.matmul(psb_[:, :n], WUQ[:, kc, 768 + pr * 128:896 + pr * 128], cq[:, kc, :n],
                                                                                              start=(kc == 0), stop=(kc == 2)), reads=[B_w, cqB], writes=[psbB])
                                kb.op(DVE, lambda psa=psa: nc.vector.tensor_tensor(out=rsBt[:, 0, :n], in0=psa[:, :n], in1=ropeC[:, :n], op=ALU.mult),
                                      reads=[psaB, B_rope], writes=[rsBB], ww=False)
                                kb.op(DVE, lambda psb_=psb_: nc.vector.tensor_tensor(out=rsBt[:, 1, :n], in0=psb_[:, :n], in1=ropeS[:, :n], op=ALU.mult),
                                      reads=[psbB, B_rope], writes=[rsBB], ww=False)
                                kb.op(POOL, lambda pr=pr: nc.gpsimd.tensor_tensor(out=QR[:, pr, :n], in0=rsBt[:, 0, :n], in1=rsBt[:, 1, :n], op=ALU.add),
                                      reads=[rsBB], writes=[qB], ww=False)
                            else:
                                kb.op(DVE, lambda pr=pr, psa=psa: nc.vector.tensor_copy(out=QR[:, pr, :n], in_=psa[:, :n]), reads=[psaB], writes=[qB])
                            yield

                    def taskA(ti, yi):
                        kind, c0, n = TILES[ti]
                        lat = kind == "l"
                        nk = 18 if lat else 2
                        allkv = KVB if lat else KVB[0:1]
                        QN, QR, qB = QNl[yi], QRl[yi], qBl[yi]
                        for h in range(4):
                            kp = (h % 2) * 64

                            def scores(kt, h=h, kp=kp):
                                b = kt % 2
                                kb.op(PE, lambda: nc.tensor.matmul(PS[b][:, :n], KN[:, h, kt * 128:(kt + 1) * 128], QN[:, h, :n], start=True, stop=False),
                                      reads=allkv + [qB], writes=[PSB[b]])
                                kb.op(PE, lambda: nc.tensor.matmul(PS[b][:, :n], KR[kp:kp + 64, kt * 128:(kt + 1) * 128], QR[kp:kp + 64, h // 2, :n],
                                                                   start=False, stop=True), reads=allkv + [qB], writes=[PSB[b]])

                            def expo(kt):
                                b = kt % 2
                                kb.op(ACT, lambda: nc.scalar.activation(out=PT[kt % 3][:, :n], in_=PS[b][:, :n], func=AF.Exp, scale=SM_SCALE),
                                      reads=[PSB[b]], writes=[PTB[kt % 3]])

                            def pv(kt, h=h):
                                b = kt % 3
                                kb.op(PE, lambda: nc.tensor.matmul(PS[2][:, :n], Vt[:, kt, h * 128:(h + 1) * 128], PT[b][:, :n], start=(kt == 0), stop=(kt == nk - 1)),
                                      reads=allkv + [PTB[b]], writes=[PSB[2]])
                                kb.op(PE, lambda: nc.tensor.matmul(PS[3][:, :n], ones[:], PT[b][:, :n], start=(kt == 0), stop=(kt == nk - 1)),
                                      reads=[B_const, PTB[b]], writes=[PSB[3]])

                            scores(0)
                            for kt in range(nk):
                                expo(kt)
                                if kt + 1 < nk:
                                    scores(kt + 1)
                                pv(kt)
                                yield
                            kb.op(ACT, lambda: nc.scalar.activation(out=tB_[:, :n], in_=PS[3][:, :n], func=AF.Ln), reads=[PSB[3]], writes=[tBB])
                            kb.op(ACT, lambda: nc.scalar.activation(out=rinv[:, :n], in_=tB_[:, :n], func=AF.Exp, scale=-1.0), reads=[tBB], writes=[rinvB])
                            kb.op(DVE, lambda h=h: nc.vector.tensor_copy(out=attf[:, h, :n], in_=PS[2][:, :n]), reads=[PSB[2]], writes=[attB])
                            kb.op(DVE, lambda h=h: nc.vector.tensor_tensor(out=attf[:, h, :n], in0=attf[:, h, :n], in1=rinv[:, :n], op=ALU.mult),
                                  reads=[attB, rinvB], writes=[attB])

                    def taskB(ti, yi, nxt=None):
                        kind, c0, n = TILES[ti]
                        lat = kind == "l"
                        allkv = KVB if lat else KVB[0:1]
                        Y, YB = Yt[yi], YBB[yi]
                        nti = 16 if lat else 2
                        scale = (nti * 128 * 64) ** -0.5
                        for kh in range(n // 256):
                            k0 = kh * 256
                            for ti2 in range(nti):
                                if lat:
                                    if ti2 % 2 == 0:
                                        di = dtc[0] % 3
                                        dtc[0] += 1
                                        kb.dma(QS, DT[di][:].rearrange("p a b c -> p (a b c)"), dftl_in[(ti - 1) * 2 + kh, ti2 // 2], writes=[DTB[di]])
                                    tab = lambda di=di, ti2=ti2: DT[di][:, ti2 % 2, :, :].rearrange("p a b -> p (a b)")
                                    tabB = DTB[di]
                                    ug = 2 + ti2
                                else:
                                    tab = lambda ti2=ti2: dftc[:, ti2, :, :].rearrange("p a b -> p (a b)")
                                    tabB = B_const
                                    ug = ti2
                                for f in range(2):
                                    kb.op(PE, lambda f=f, tab=tab, ug=ug, ti2=ti2: nc.tensor.matmul(
                                        PS[4 + f][:, :], U[:, ug, f * 128:(f + 1) * 128], tab(), start=(ti2 == 0), stop=(ti2 == nti - 1)),
                                        reads=[tabB] + allkv, writes=[PSB[4 + f]])
                                yield
                            for f in range(2):
                                src = PS[4 + f][:, :]
                                dst = P12[:, 2 * f:2 * f + 2, :].rearrange("p a b -> p (a b)")
                                if False:
                                    pass
                                else:
                                    kb.op(DVE, lambda src=src, dst=dst: nc.vector.tensor_scalar(out=dst, in0=src, scalar1=scale, scalar2=0.0, op0=ALU.mult, op1=ALU.add),
                                          reads=[PSB[4 + f]], writes=[P12B])
                            for f in range(2):
                                ps, psB = PS[4 + f], PSB[4 + f]
                                kb.op(PE, lambda f=f, ps=ps: nc.tensor.matmul(ps[:, 0:256], bcs[:, 0:128], P12[:, 2 * f, :], start=True, stop=False),
                                      reads=[P12B, B_const], writes=[psB])
                                kb.op(PE, lambda f=f, ps=ps: nc.tensor.matmul(ps[:, 0:256], bcs[:, 128:256], P12[:, 2 * f + 1, :], start=False, stop=True),
                                      reads=[P12B, B_const], writes=[psB])
                                kb.op(DVE, lambda f=f, ps=ps: nc.vector.tensor_copy(out=yff[:, f, :], in_=ps[:, 0:256]), reads=[psB], writes=[yffB], n=256)
                            yield
                            feat_rstd2(2, 256, lambda j: (yff[:, j, :], yffB), 256, 4, sqBt, sqBB, rsBt, rsBB, cB)
                            for f in range(2):
                                kb.op(DVE, lambda f=f: nc.vector.scalar_tensor_tensor(out=Y[:, 4 + f, k0:k0 + 256], in0=yff[:, f, :], scalar=sm(O_OG + 4 + f),
                                                                                     in1=rsBt[:, 1, 0:256], op0=ALU.mult, op1=ALU.mult),
                                      reads=[yffB, rsBB, B_const], writes=[YB])
                            yield
                        first = ti in (0, 1)
                        lastt = ti in (0, 4)
                        nb = [VSB[ti], PADB] + ([] if first else [VSB[ti - 1]]) + ([] if lastt else [VSB[ti + 1]])
                        lo = 1 if first else 0
                        hi = n - 1 if lastt else n
                        for j in range(2):
                            kb.dma(QS, vwj[:, 0:n + 2], v_s[:, j, H2C[ti] - 1:H2C[ti] + n + 1], reads=nb, writes=[vwB])
                            kb.dma(QS, bgj[:, 0:n], bg_s[:, j, c0:c0 + n], reads=[BGB[ti]], writes=[bgwB])
                            kb.op(POOL, lambda j=j: nc.gpsimd.tensor_scalar(out=cacc[:, j, :n], in0=vwj[:, 1:n + 1], scalar1=sm(O_SW + 3 * j + 1),
                                                                            scalar2=sm(O_SB + j), op0=ALU.mult, op1=ALU.add),
                                  reads=[vwB, B_const], writes=[caccB])
                            kb.op(DVE, lambda j=j: nc.vector.scalar_tensor_tensor(out=cacc[:, j, lo:n], in0=vwj[:, lo:n], scalar=sm(O_SW + 3 * j),
                                                                                  in1=cacc[:, j, lo:n], op0=ALU.mult, op1=ALU.add),
                                  reads=[vwB, B_const, caccB], writes=[caccB])
                            kb.op(DVE, lambda j=j: nc.vector.scalar_tensor_tensor(out=cacc[:, j, 0:hi], in0=vwj[:, 2:hi + 2], scalar=sm(O_SW + 3 * j + 2),
                                                                                  in1=cacc[:, j, 0:hi], op0=ALU.mult, op1=ALU.add),
                                  reads=[vwB, B_const, caccB], writes=[caccB])
                            kb.op(POOL, lambda j=j: nc.gpsimd.tensor_tensor(out=cacc[:, j, :n], in0=cacc[:, j, :n], in1=bgj[:, :n], op=ALU.mult),
                                  reads=[caccB, bgwB], writes=[caccB])
                            yield
                        feat_rstd2(2, 256, lambda j: (cacc[:, j, :n], caccB), n, 5, sqBt, sqBB, rsBt, rsBB, cB)
                        yield
                        for j in range(2):
                            kb.op(DVE, lambda j=j: nc.vector.scalar_tensor_tensor(out=Y[:, 6 + j, :n], in0=cacc[:, j, :n], scalar=sm(O_OG + 6 + j),
                                                                                 in1=rsBt[:, 1, :n], op0=ALU.mult, op1=ALU.mult),
                                  reads=[caccB, rsBB, B_const], writes=[YB])
                        yield

                        if nxt is not None:
                            yield from qproj(nxt[0], nxt[1], 4, 5)

                    def taskC(ti, yi):
                        kind, c0, n = TILES[ti]
                        v = 0 if kind == "l" else 1
                        Y = Yt[yi]
                        yb = [YAB[yi], YBB[yi]]
                        hT, hB = get_hT()
                        feat_rstd2(4, 512, lambda j: (attf[:, j, :n], attB), n, 6, sqA, sqAB, rsA, rsAB, cA)
                        yield
                        for h in range(4):
                            kb.op(DVE, lambda h=h: nc.vector.scalar_tensor_tensor(out=Y[:, h, :n], in0=attf[:, h, :n], scalar=sm(O_OG + h),
                                                                                 in1=rsA[:, 1, :n], op0=ALU.mult, op1=ALU.mult),
                                  reads=[attB, rsAB, B_const], writes=[YAB[yi]])
                        yield

                        pendc = None
                        for st in range(n // 128):
                            sti = c0 // 128 + st
                            xt, xB = get_xt()
                            if l == 0:
                                src0 = ctx_in[sti * 128:(sti + 1) * 128, :] if sti < 2 else x_in[(sti - 2) * 128:(sti - 1) * 128, :]
                            else:
                                src0 = res_ap(sti)
                            kb.dma(QS, xt[:], src0, reads=[RES[sti]], writes=[xB])
                            for hf in range(2):
                                ps, psB = PS[6 + hf], PSB[6 + hf]
                                for kc in range(8):
                                    kb.op(PE, lambda kc=kc, st=st, hf=hf, ps=ps: nc.tensor.matmul(ps[:, :], Y[:, kc, st * 128:(st + 1) * 128], WOUT[:, kc, hf * 512:(hf + 1) * 512],
                                                                                                 start=(kc == 0), stop=(kc == 7)), reads=yb + [B_wout], writes=[psB])
                                kb.op(DVE, lambda hf=hf, ps=ps: nc.vector.tensor_tensor(out=tmpx[:], in0=ps[:, :], in1=GA1[:, v, hf * 512:(hf + 1) * 512], op=ALU.mult),
                                      reads=[psB, B_ga], writes=[tmpxB])
                                kb.op(DVE, lambda hf=hf, xt=xt: nc.vector.tensor_tensor(out=xt[:, hf * 512:(hf + 1) * 512], in0=xt[:, hf * 512:(hf + 1) * 512], in1=tmpx[:], op=ALU.add),
                                      reads=[tmpxB, xB], writes=[xB])
                                yield
                            kb.dma(QP, res_ap(sti), xt[:], reads=[xB], writes=[RES[sti]])
                            newp = norm_to_hT(xt, xB, l, 1, v, hT, hB, st, bank=6 + (st % 2), evac="dve")
                            for _ in range(3):
                                next(newp)
                                yield
                            if pendc is not None:
                                yield from pendc
                            pendc = newp
                            yield
                        if pendc is not None:
                            yield from pendc
                        kb.dma(QA, h2_s[:, :, H2C[ti]:H2C[ti] + n], hT[:, :, 0:n], reads=hB, writes=[H2B[ti]])

                    p2tiles = [1, 2, 3, 4] + ([] if last else [0])
                    prev = None
                    kb.run_tasks([qproj(p2tiles[0], 0, 4, 5)])
                    for idx, ti in enumerate(p2tiles):
                        nxt = (p2tiles[idx + 1], (idx + 1) % 2) if idx + 1 < len(p2tiles) else None
                        tasks = [taskA(ti, idx % 2), taskB(ti, idx % 2, nxt)]
                        if prev is not None:
                            tasks.append(taskC(prev[0], prev[1]))
                        kb.run_tasks(tasks)
                        prev = (ti, idx % 2)
                    kb.run_tasks([taskC(prev[0], prev[1])])
                    kb.barrier()

            marks.append((f'L{l}ffn', PE.cnt))
            with contextlib.ExitStack() as s1:
                A = s1.enter_context
                GA2 = sb("GA2", [P, 2, D], F32, A)
                B_ga2 = Buf()
                WU = [sb(f"WU{i}", [P, 8, 512], BF16, A) for i in range(2)]
                WUB = [Buf(), Buf()]
                WD = sb("WD", [P, 2, NCF, 512], BF16, A)
                WDB = [Buf(), Buf()]
                G = sb("G", [P, NCF, 1280], BF16, A)
                GB_ = Buf()
                h2w = sb("h2w", [P, 8, 1280 + 6], BF16, A)
                h2wBs = {t: Buf() for t in range(5)}
                gb = [sb(f"gb{i}", [P, 512], F32, A) for i in range(2)]
                gbB = [Buf(), Buf()]
                sg = [sb(f"sg{i}", [P, 512], F32, A) for i in range(2)]
                sgB = [Buf(), Buf()]
                tmpx = sb("tmpx3", [P, 512], F32, A)
                tmpxB = Buf()
                xt_t.append(sb("xt_extra", [P, D], F32, A))
                xtB.append(Buf())
                print('SBUF remaining ffn', nc.sbuf_bytes_remaining)
                supers = [[t for t in (0, 1, 2) if not (last and t == 0)], [3, 4]]
                wuc = [0]
                upc = [0]
                up_defer = [None]
                def load_windows(sup, prev=()):
                    offs, goff = {}, {}
                    o_, g_ = 0, 0
                    for t in sup:
                        kind, c0, n = TILES[t]
                        offs[t], goff[t] = o_, g_
                        first = t in (0, 1)
                        lastt = t in (0, 4)
                        nb = [H2B[t], PADB] + ([] if first else [H2B[t - 1]]) + ([] if lastt else [H2B[t + 1]])
                        kb.dma(QS, h2w[:, :, o_:o_ + n + 2], h2_s[:, :, H2C[t] - 1:H2C[t] + n + 1], reads=nb, writes=[h2wBs[t]] + [h2wBs[u] for u in prev])
                        o_ += n + 2
                        g_ += n
                    return offs, goff

                win_next = load_windows(supers[0])
                for g in range(2):
                    kb.dma(QP, WU[g][:].rearrange("p k n -> p (k n)"), wup_in[l, g], writes=[WUB[g]])
                make_gate_bc(GA2, 40, B_ga2)
                for si, sup in enumerate(supers):
                    offs, goff = win_next
                    for g in range(11):
                        wi = wuc[0] % 2
                        wuc[0] += 1
                        if not (si == 0 and g < 2):
                            kb.dma(QP, WU[wi][:].rearrange("p k n -> p (k n)"), wup_in[l, g], writes=[WUB[wi]])
                        if si == 0 and g == 1:
                            for hf in range(2):
                                kb.dma(QP, WD[:, hf].rearrange("p c n -> p (c n)"), wdn_in[l, hf], writes=[WDB[hf]])
                        for cl in range(2):
                            c = 2 * g + cl
                            for t in sup:
                                kind, c0, n = TILES[t]
                                nh = n // 2
                                o_ = offs[t]
                                first = t in (0, 1)
                                lastt = t in (0, 4)
                                bi = upc[0] % 2
                                vbk = (2, 5, 6, 7)[upc[0] % 4]
                                upc[0] += 1
                                gA, gAB = PS[3 * bi], PSB[3 * bi]
                                gBk, gBB = PS[3 * bi + 1], PSB[3 * bi + 1]
                                vv, vvB = PS[vbk], PSB[vbk]
                                for kc in range(8):
                                    wg = WU[wi][:, kc, cl * 256:cl * 256 + 128]
                                    kb.op(PE, lambda kc=kc, wg=wg: nc.tensor.matmul(gA[:, 0:nh + 2], wg, h2w[:, kc, o_:o_ + nh + 2], start=(kc == 0), stop=(kc == 7)),
                                          reads=[WUB[wi], h2wBs[t]], writes=[gAB])
                                for kc in range(8):
                                    wg = WU[wi][:, kc, cl * 256:cl * 256 + 128]
                                    kb.op(PE, lambda kc=kc, wg=wg: nc.tensor.matmul(gBk[:, 0:nh + 2], wg, h2w[:, kc, o_ + nh:o_ + n + 2], start=(kc == 0), stop=(kc == 7)),
                                          reads=[WUB[wi], h2wBs[t]], writes=[gBB])
                                for kc in range(8):
                                    wv = WU[wi][:, kc, cl * 256 + 128:cl * 256 + 256]
                                    kb.op(PE, lambda kc=kc, wv=wv: nc.tensor.matmul(vv[:, 0:n], wv, h2w[:, kc, o_ + 1:o_ + n + 1], start=(kc == 0), stop=(kc == 7)),
                                          reads=[WUB[wi], h2wBs[t]], writes=[vvB])
                                gbuf, gbb = gb[bi], gbB[bi]
                                w0, w1, w2, bb = sm(O_FW + 3 * c), sm(O_FW + 3 * c + 1), sm(O_FW + 3 * c + 2), sm(O_FB + c)
                                kb.op(ACT, lambda: nc.scalar.activation(out=gbuf[:, 0:nh], in_=gA[:, 1:nh + 1], func=AF.Identity, bias=bb, scale=w1),
                                      reads=[gAB, B_const], writes=[gbb])
                                kb.op(ACT, lambda: nc.scalar.activation(out=gbuf[:, nh:n], in_=gBk[:, 1:nh + 1], func=AF.Identity, bias=bb, scale=w1),
                                      reads=[gBB, B_const], writes=[gbb])
                                lo = 1 if first else 0
                                hi = nh - 1 if lastt else nh
                                kb.op(DVE, lambda: nc.vector.scalar_tensor_tensor(out=gbuf[:, lo:nh], in0=gA[:, lo:nh], scalar=w0, in1=gbuf[:, lo:nh], op0=ALU.mult, op1=ALU.add),
                                      reads=[gAB, B_const, gbb], writes=[gbb])
                                kb.op(DVE, lambda: nc.vector.scalar_tensor_tensor(out=gbuf[:, nh:n], in0=gBk[:, 0:nh], scalar=w0, in1=gbuf[:, nh:n], op0=ALU.mult, op1=ALU.add),
                                      reads=[gBB, B_const, gbb], writes=[gbb])
                                kb.op(DVE, lambda: nc.vector.scalar_tensor_tensor(out=gbuf[:, 0:nh], in0=gA[:, 2:nh + 2], scalar=w2, in1=gbuf[:, 0:nh], op0=ALU.mult, op1=ALU.add),
                                      reads=[gAB, B_const, gbb], writes=[gbb])
                                kb.op(DVE, lambda: nc.vector.scalar_tensor_tensor(out=gbuf[:, nh:nh + hi], in0=gBk[:, 2:hi + 2], scalar=w2, in1=gbuf[:, nh:nh + hi], op0=ALU.mult, op1=ALU.add),
                                      reads=[gBB, B_const, gbb], writes=[gbb])
                                sgb, sgbb = sg[bi], sgB[bi]
                                go = goff[t]

                                def part2(gbuf=gbuf, gbb=gbb, sgb=sgb, sgbb=sgbb, vv=vv, vvB=vvB, c=c, go=go, n=n):
                                    kb.op(ACT, lambda: nc.scalar.activation(out=sgb[:, 0:n], in_=gbuf[:, 0:n], func=AF.Silu), reads=[gbb], writes=[sgbb])
                                    kb.op(DVE, lambda: nc.vector.tensor_tensor(out=G[:, c, go:go + n], in0=vv[:, 0:n], in1=sgb[:, 0:n], op=ALU.mult),
                                          reads=[vvB, sgbb], writes=[GB_], ww=False)
                                if up_defer[0] is not None:
                                    up_defer[0]()
                                up_defer[0] = part2
                    if up_defer[0] is not None:
                        up_defer[0]()
                        up_defer[0] = None
                    if si + 1 < len(supers):
                        win_next = load_windows(supers[si + 1], prev=sup)
                    marks.append((f'L{l}down', PE.cnt))
                    for t in sup:
                        kind, c0, n = TILES[t]
                        v = 1 if kind == "c" else 0
                        if not last:
                            hT, hB = get_hT()
                        pending = None
                        for st in range(n // 128):
                            sti = c0 // 128 + st
                            xt, xB = get_xt()
                            kb.dma(QS, xt[:], res_ap(sti), reads=[RES[sti]], writes=[xB])
                            go = goff[t] + st * 128
                            for hf in range(2):
                                ps, psB = nextps()
                                for c in range(NCF):
                                    kb.op(PE, lambda c=c, hf=hf, ps=ps, go=go: nc.tensor.matmul(ps[:, :], G[:, c, go:go + 128], WD[:, hf, c, :], start=(c == 0), stop=(c == NCF - 1)),
                                          reads=[GB_, WDB[hf]], writes=[psB])
                                kb.op(DVE, lambda hf=hf, ps=ps: nc.vector.tensor_tensor(out=tmpx[:], in0=ps[:, :], in1=GA2[:, v, hf * 512:(hf + 1) * 512], op=ALU.mult),
                                      reads=[psB, B_ga2], writes=[tmpxB])
                                kb.op(DVE, lambda hf=hf, xt=xt: nc.vector.tensor_tensor(out=xt[:, hf * 512:(hf + 1) * 512], in0=xt[:, hf * 512:(hf + 1) * 512], in1=tmpx[:], op=ALU.add),
                                      reads=[tmpxB, xB], writes=[xB])
                            if last:
                                r0, r0B = tok_sumsq(xt, xB)
                                r, rB = rstd_from(r0[:, 0:1], r0B, D)
                                kb.op(DVE, lambda xt=xt, r=r: nc.vector.scalar_tensor_tensor(out=xt[:], in0=xt[:], scalar=r[:, 3:4], in1=fg[:], op0=ALU.mult, op1=ALU.mult),
                                      reads=[xB, rB, B_const], writes=[xB])
                                kb.dma(QA, res_ap(sti), xt[:], reads=[xB], writes=[RES[sti]])
                            else:
                                kb.dma(QA, res_ap(sti), xt[:], reads=[xB], writes=[RES[sti]])
                                newp = norm_to_hT(xt, xB, l + 1, 0, v, hT, hB, st)
                                for _ in range(3):
                                    next(newp)
                                if pending is not None:
                                    for _ in pending:
                                        pass
                                pending = newp
                        if not last:
                            if pending is not None:
                                for _ in pending:
                                    pass
                                pending = None
                            kb.dma(QA, h1_s[:, :, c0:c0 + n], hT[:, :, 0:n], reads=hB, writes=[H1B[t]])
                xt_t.pop()
                xtB.pop()
                kb.barrier()
        kb.barrier()
        marks.append(('end', PE.cnt))
        build_nc.marks = marks
    return nc


def _wP(w):
    K = w.shape[0] // 128
    return np.ascontiguousarray(w.reshape(K, 128, -1).transpose(1, 0, 2).reshape(128, -1))


def _chunkP(v):
    return np.ascontiguousarray(v.reshape(-1, 128).T)


_CONST_CACHE = {}


def _constants():
    if _CONST_CACHE:
        return _CONST_CACHE
    bf = ml_dtypes.bfloat16
    t = np.arange(S)
    r, c = (t // 64).astype(np.float64), (t % 64).astype(np.float64)
    inv = 10000.0 ** (-np.arange(0, 32, 2, dtype=np.float64) / 32)
    d = np.arange(64)
    a, j, i = d // 32, (d % 32) // 16, d % 16
    pos = np.where(a[:, None] == 0, r[None, :], c[None, :])
    ang = (pos.astype(np.float32) * inv.astype(np.float32)[i][:, None]).astype(np.float32).astype(np.float64)
    cosT = np.cos(ang)
    sinT = np.sin(ang) * np.where(j == 0, -1.0, 1.0)[:, None]
    rope = np.stack([np.concatenate([cosT, cosT], 0), np.concatenate([sinT, sinT], 0)]).astype(np.float32)
    tt = (np.arange(8)[:, None, None] * 256 + np.arange(2)[None, :, None] * 128 + np.arange(128)[None, None, :])
    dftl = np.empty((8, 8, 128, 2, 2, 256), dtype=bf)
    for kt in range(8):
        kk = kt * 256 + np.arange(256)
        ph = (tt[..., None] * kk[None, None, None, :]) % S
        angl = ph.astype(np.float64) * (2 * np.pi / S)
        dftl[kt, :, :, :, 0, :] = np.cos(angl).transpose(0, 2, 1, 3).astype(bf)
        dftl[kt, :, :, :, 1, :] = np.sin(angl).transpose(0, 2, 1, 3).astype(bf)
    dftl = dftl.reshape(8, 8, 128, 2 * 2 * 256)
    tc_ = np.arange(2)[:, None] * 128 + np.arange(128)[None, :]
    phc = (tc_[..., None] * np.arange(256)[None, None, :]) % CT
    angc = phc.astype(np.float64) * (2 * np.pi / CT)
    dftc = np.stack([np.cos(angc), np.sin(angc)], axis=2).transpose(1, 0, 2, 3).astype(bf).reshape(128, 2 * 2 * 256)
    m = np.arange(64)
    a64 = ((m[:, None] * m[None, :]) % 64).astype(np.float64) * (2 * np.pi / 64)
    bc = np.zeros((128, 128))
    bs = np.zeros((128, 128))
    for g in range(2):
        bc[g * 64:(g + 1) * 64, g * 64:(g + 1) * 64] = np.cos(a64)
        bs[g * 64:(g + 1) * 64, g * 64:(g + 1) * 64] = -np.sin(a64)
    bcs = np.concatenate([bc, bs], 1).astype(bf)
    _CONST_CACHE.update(rope=rope, dftl=dftl, dftc=dftc, bcs=bcs, idb=np.eye(128).astype(bf), idf=np.eye(128, dtype=np.float32))
    return _CONST_CACHE


def _prep_shared(inp):
    f = lambda k: np.asarray(inp[k], dtype=np.float32)
    d64 = np.arange(64)
    sw = np.where(d64 % 32 < 16, d64 + 16, d64 - 16)
    kr = 640 + d64
    krs = 640 + sw
    win_idx = np.concatenate([np.arange(0, 640), kr, kr, krs, krs, np.arange(704, 1728)])
    up_idx = np.concatenate([np.concatenate([np.arange(c * 128, (c + 1) * 128), DFF + np.arange(c * 128, (c + 1) * 128)]) for c in range(NCF)])
    smalls = np.zeros((DEPTH, 128, NSM), np.float32)
    ada = np.empty((DEPTH, 12, 128, 8 * 512), np.float32)
    win = np.empty((DEPTH, 128, 8 * NWIN), np.float32)
    wuq = np.empty((DEPTH, 128, 3 * 1024), np.float32)
    wukv = np.empty((DEPTH, 128, 2 * 1024), np.float32)
    wout = np.empty((DEPTH, 128, 8 * 1024), np.float32)
    wup = np.empty((DEPTH, 11, 128, 8 * 512), np.float32)
    wdn = np.empty((DEPTH, 2, 128, NCF * 512), np.float32)
    for l in range(DEPTH):
        s = smalls[l]
        s[:, O_G1:O_G1 + 8] = _chunkP(f("norm1_g")[l])
        s[:, O_QG:O_QG + 3] = _chunkP(f("q_norm_g")[l])
        s[:, O_KVG:O_KVG + 2] = _chunkP(f("kv_norm_g")[l])
        sw_ = f("sconv_w")[l]
        for j in range(2):
            for tap in range(3):
                s[:, O_SW + 3 * j + tap] = sw_[tap, j * 128:(j + 1) * 128]
        s[:, O_SB:O_SB + 2] = _chunkP(f("sconv_b")[l])
        s[:, O_OG:O_OG + 8] = _chunkP(f("out_norm_g")[l])
        s[:, O_G2:O_G2 + 8] = _chunkP(f("norm2_g")[l])
        fw = f("ffconv_w")[l]
        for c in range(NCF):
            for tap in range(3):
                s[:, O_FW + 3 * c + tap] = fw[tap, c * 128:(c + 1) * 128]
        s[:, O_FB:O_FB + NCF] = _chunkP(f("ffconv_b")[l])
        s[:, O_AB:O_AB + 48] = _chunkP(f("ada_b")[l])
        aw = f("ada_w")[l]
        for g in range(12):
            ada[l, g] = _wP(aw[:, g * 512:(g + 1) * 512])
        win[l] = _wP(f("w_in")[l][:, win_idx])
        uq = f("w_uq")[l]
        cols = [uq[:, h, 0:128] for h in range(4)]
        cols += [np.concatenate([uq[:, 2 * pr, 128:192], uq[:, 2 * pr + 1, 128:192]], 1) for pr in range(2)]
        cols += [np.concatenate([uq[:, 2 * pr, 128 + sw], uq[:, 2 * pr + 1, 128 + sw]], 1) for pr in range(2)]
        wuq[l] = _wP(np.concatenate(cols, 1))
        ukv = f("w_ukv")[l]
        wukv[l] = _wP(np.concatenate([ukv[:, h, 0:128] for h in range(4)] + [ukv[:, h, 128:256] for h in range(4)], 1))
        wout[l] = _wP(f("w_out")[l])
        upp = f("w_up")[l][:, up_idx]
        for g in range(11):
            wup[l, g] = _wP(upp[:, g * 512:(g + 1) * 512])
        dn = f("w_down")[l]
        for hf in range(2):
            wdn[l, hf] = _wP(dn[:, hf * 512:(hf + 1) * 512])
    fg = np.ascontiguousarray(np.broadcast_to(f("final_g")[None, :], (128, D)))
    return dict(smalls=smalls, ada=ada, win=win, wuq=wuq, wukv=wukv, wout=wout, wup=wup, wdn=wdn, fg=fg)


def kernel(**inputs):
    consts = _constants()
    shared = _prep_shared(inputs)
    x = np.asarray(inputs["x"], np.float32)
    c = np.asarray(inputs["c"], np.float32)
    ctx = np.asarray(inputs["ctx"], np.float32)
    c_ctx = np.asarray(inputs["c_ctx"], np.float32)
    nc = build_nc()
    in_maps = []
    for b in range(8):
        cv = np.stack([_chunkP(c[b]), _chunkP(c_ctx)], axis=2).reshape(128, 16)
        m = dict(x=np.ascontiguousarray(x[b]), ctx=np.ascontiguousarray(ctx[b]), cvec=np.ascontiguousarray(cv))
        m.update(shared)
        m.update(consts)
        in_maps.append(m)
    res = run_bass_kernel_spmd(nc, in_maps, core_ids=list(range(8)))
    return np.stack([np.asarray(r["out"], dtype=np.float32) for r in res.results], axis=0)
```

```python
import contextlib
import itertools
import numpy as np
import ml_dtypes
import concourse.bass as bass
import concourse.mybir as mybir
from concourse.bass_utils import run_bass_kernel_spmd

F32 = mybir.dt.float32
BF16 = mybir.dt.bfloat16
AF = mybir.ActivationFunctionType
ALU = mybir.AluOpType

P = 128
D = 1024
S = 2048
CT = 256
T = S + CT
DFF = 2816
NCF = 22
DEPTH = 2
EPS = 1e-6
SM_SCALE = 192 ** -0.5
NWIN = 1920
O_G1, O_QG, O_KVG, O_SW, O_SB, O_OG, O_G2, O_FW, O_FB, O_AB = 0, 8, 11, 13, 19, 21, 29, 37, 103, 125
NSM = 173
TILES = [("c", 0, 256)] + [("l", 256 + 512 * i, 512) for i in range(4)]
H2C = {0: 1, 1: 258, 2: 258 + 512, 3: 258 + 1024, 4: 258 + 1536}
H2W = 2307


class Buf:
    __slots__ = ("lw", "rd", "name")

    def __init__(self, name=""):
        self.lw = {}
        self.rd = {}
        self.name = name


class Eng:
    def __init__(self, obj, sem, inorder=False, kind="dve"):
        self.obj = obj
        self.sem = sem
        self.cnt = 0
        self.waited = {}
        self.inorder = inorder
        self.kind = kind
        self.free = 0.0


class DmaQ:
    def __init__(self, issuer, sems):
        self.issuer = issuer
        self.sems = sems
        self.vals = [0] * len(sems)
        self.n = 0
        self.free = 0.0


class Task:
    def __init__(self, gen):
        self.gen = gen
        self.vt = 0.0


class KB:
    SEM_LAT = 120.0

    def __init__(self, nc):
        self.nc = nc
        self.engs = []
        self.qs = []
        self.fin = {}
        self.cur = None
        self.now = 0.0
        self.rr = 0
        self.round_robin = False
        self.force_ww = False

    def wait(self, eng, tok):
        sem, val = tok
        if eng.waited.get(sem.num, 0) >= val:
            return
        eng.obj.wait_ge(sem, val)
        eng.waited[sem.num] = val

    def _deps(self, eng, reads, writes, ww):
        t = 0.0
        toks = []
        for b in reads:
            toks.extend(b.lw.values())
        for b in writes:
            if ww:
                toks.extend(b.lw.values())
            toks.extend(b.rd.values())
        for tok in toks:
            same = tok[0] is eng.sem
            t = max(t, self.fin.get((tok[0].num, tok[1]), 0.0) + (0.0 if same else self.SEM_LAT))
            if not (eng.inorder and same):
                self.wait(eng, tok)
        return t

    def _mark(self, tok, reads, writes, ww=True):
        for b in reads:
            b.rd[tok[0].num] = tok
        for b in writes:
            if ww:
                b.lw = {tok[0].num: tok}
                b.rd = {}
            else:
                b.lw[tok[0].num] = tok

    def op(self, eng, fn, reads=(), writes=(), n=512, ww=True):
        ww = True if self.force_ww else ww
        tdep = self._deps(eng, reads, writes, ww)
        ins = fn()
        eng.cnt += 1
        ins.then_inc(eng.sem, 1)
        tok = (eng.sem, eng.cnt)
        if eng.inorder:
            cost, lat = n / 2.4 + 12.0, 120.0
        elif eng.kind == "act":
            cost, lat = 200.0 + n * 1.0, 60.0
        elif eng.kind == "pool":
            cost, lat = 150.0 + n * 2.0, 60.0
        else:
            cost, lat = 90.0 + n * 1.0, 60.0
        start = max(eng.free, tdep, self.cur.vt if self.cur is not None else 0.0)
        eng.free = start + cost
        self.fin[(tok[0].num, tok[1])] = start + cost + lat
        if self.cur is not None:
            self.cur.vt = start
        self._mark(tok, reads, writes, ww)

    def dma(self, q, out, in_, reads=(), writes=(), slow=False, nbytes=1 << 20, ww=True):
        eng = q.issuer
        tdep = self._deps(eng, reads, writes, ww)
        k = q.n % len(q.sems)
        q.n += 1
        if q.vals[k] > 0:
            self.wait(eng, (q.sems[k], q.vals[k]))
            tdep = max(tdep, self.fin.get((q.sems[k].num, q.vals[k]), 0.0))
        ins = eng.obj.dma_start(out=out, in_=in_, allow_slow_non_contiguous=True) if slow else eng.obj.dma_start(out=out, in_=in_)
        q.vals[k] += 16
        ins.then_inc(q.sems[k], 16)
        tok = (q.sems[k], q.vals[k])
        start = max(eng.free, tdep, self.cur.vt if self.cur is not None else 0.0)
        eng.free = start + 60.0
        q.free = max(q.free, start) + nbytes / 150.0
        self.fin[(tok[0].num, tok[1])] = max(q.free, start + 2200.0)
        self._mark(tok, reads, writes, ww)

    def barrier(self):
        toks = [(e.sem, e.cnt) for e in self.engs if e.cnt > 0]
        for q in self.qs:
            toks += [(s, v) for s, v in zip(q.sems, q.vals) if v > 0]
        tmax = max([e.free for e in self.engs] + [self.fin.get((t[0].num, t[1]), 0.0) for t in toks])
        for e in self.engs:
            e.free = tmax
            for tok in toks:
                if tok[0] is e.sem:
                    continue
                self.wait(e, tok)

    def run_tasks(self, gens):
        tasks = [Task(g) for g in gens if g is not None]
        t0 = max(e.free for e in self.engs) if False else min(e.free for e in self.engs)
        for t in tasks:
            t.vt = t0
        while tasks:
            t = tasks[self.rr % len(tasks)] if self.round_robin else min(tasks, key=lambda x: x.vt)
            self.rr += 1
            self.cur = t
            try:
                next(t.gen)
            except StopIteration:
                tasks.remove(t)
            self.cur = None


def run_tasks(tasks):
    act = [t for t in tasks if t is not None]
    while act:
        for t in list(act):
            try:
                next(t)
            except StopIteration:
                act.remove(t)


def build_nc(n_layers=DEPTH):
    nc = bass.Bass("TRN2", target_bir_lowering=False)
    dt = nc.dram_tensor
    x_in = dt("x", [S, D], F32, kind="ExternalInput").ap()
    ctx_in = dt("ctx", [CT, D], F32, kind="ExternalInput").ap()
    cvec_in = dt("cvec", [P, 16], F32, kind="ExternalInput").ap()
    smalls_in = dt("smalls", [DEPTH, P, NSM], F32, kind="ExternalInput").ap()
    ada_in = dt("ada", [DEPTH, 12, P, 8 * 512], F32, kind="ExternalInput").ap()
    win_in = dt("win", [DEPTH, P, 8 * NWIN], F32, kind="ExternalInput").ap()
    wuq_in = dt("wuq", [DEPTH, P, 3 * 1024], F32, kind="ExternalInput").ap()
    wukv_in = dt("wukv", [DEPTH, P, 2 * 1024], F32, kind="ExternalInput").ap()
    wout_in = dt("wout", [DEPTH, P, 8 * 1024], F32, kind="ExternalInput").ap()
    wup_in = dt("wup", [DEPTH, 11, P, 8 * 512], F32, kind="ExternalInput").ap()
    wdn_in = dt("wdn", [DEPTH, 2, P, NCF * 512], F32, kind="ExternalInput").ap()
    fg_in = dt("fg", [P, D], F32, kind="ExternalInput").ap()
    rope_in = dt("rope", [2, P, S], F32, kind="ExternalInput").ap()
    dftl_in = dt("dftl", [8, 8, P, 2 * 2 * 256], BF16, kind="ExternalInput").ap()
    dftc_in = dt("dftc", [P, 2 * 2 * 256], BF16, kind="ExternalInput").ap()
    bcs_in = dt("bcs", [P, 256], BF16, kind="ExternalInput").ap()
    idb_in = dt("idb", [P, P], BF16, kind="ExternalInput").ap()
    idf_in = dt("idf", [P, P], F32, kind="ExternalInput").ap()
    out = dt("out", [S, D], F32, kind="ExternalOutput").ap()
    xc_s = dt("xc_s", [CT, D], F32, kind="Internal").ap()
    h1_s = dt("h1_s", [P, 8, T], BF16, kind="Internal").ap()
    h2_s = dt("h2_s", [P, 8, H2W], BF16, kind="Internal").ap()
    v_s = dt("v_s", [P, 2, H2W], F32, kind="Internal").ap()
    bg_s = dt("bg_s", [P, 2, T], F32, kind="Internal").ap()
    cqn_s = dt("cqn_s", [P, 3, T], BF16, kind="Internal").ap()

    es = contextlib.ExitStack()
    with es:
        E = es.enter_context
        kb = KB(nc)
        marks = []
        nc_marks = marks
        PE = Eng(nc.tensor, E(nc.semaphore("s_pe")), inorder=True)
        ACT = Eng(nc.scalar, E(nc.semaphore("s_act")), kind="act")
        DVE = Eng(nc.vector, E(nc.semaphore("s_dve")))
        POOL = Eng(nc.gpsimd, E(nc.semaphore("s_pool")), kind="pool")
        SP = Eng(nc.sync, E(nc.semaphore("s_sp")), kind="sp")
        kb.engs = [PE, ACT, DVE, POOL, SP]
        QS = DmaQ(SP, [E(nc.semaphore(f"qs{i}")) for i in range(12)])
        QP = DmaQ(POOL, [E(nc.semaphore(f"qp{i}")) for i in range(6)])
        QA = DmaQ(ACT, [E(nc.semaphore(f"qa{i}")) for i in range(4)])
        kb.qs = [QS, QP, QA]

        nmc = [0]

        def sb(name, shape, dtype, scope=None):
            nmc[0] += 1
            return (scope or E)(nc.sbuf_tensor(f"sb{nmc[0]}_{name}", shape, dtype))

        PS = [E(nc.psum_tensor(f"ps{i}", [P, 512], F32)) for i in range(8)]
        PSB = [Buf(f"ps{i}") for i in range(8)]
        psrot = [5]

        def nextps():
            i = psrot[0]
            psrot[0] = 5 + (i - 5 + 1) % 3
            return PS[i], PSB[i]

        idb = sb("idb", [P, P], BF16)
        idf = sb("idf", [P, P], F32)
        ones = sb("ones", [P, P], BF16)
        bcs = sb("bcs", [P, 256], BF16)
        dftc = sb("dftc", [P, 2, 2, 256], BF16)
        fg = sb("fg", [P, D], F32)
        smalls = sb("smalls", [P, DEPTH, NSM], F32)
        cvec = sb("cvec", [P, 16], F32)
        csil = sb("csil", [P, 16], BF16)
        MOD = sb("mod", [P, DEPTH, 48, 2], F32)
        GS = sb("gs", [P, DEPTH, 2, 8, 2], F32)
        B_const = Buf("const")
        B_mod = Buf("mod")
        MODB = [[Buf(f"mod{l}_{g}") for g in range(12)] for l in range(DEPTH)]
        GSB = [[Buf(), Buf()] for l in range(DEPTH)]
        NSS = 16
        ss = [sb(f"ss{i}", [P, 4], F32) for i in range(NSS)]
        ssB = [Buf() for _ in range(NSS)]
        ssrot = [0]

        for dst, src in ((idb[:], idb_in), (idf[:], idf_in), (bcs[:], bcs_in), (fg[:], fg_in),
                         (cvec[:], cvec_in), (dftc[:].rearrange("p a b c -> p (a b c)"), dftc_in)):
            kb.dma(QS, dst, src, writes=[B_const], ww=False)
        for l in range(DEPTH):
            kb.dma(QS, smalls[:, l, :], smalls_in[l], writes=[B_const], ww=False)
        kb.op(POOL, lambda: nc.gpsimd.memset(ones[:], 1.0), writes=[B_const], ww=False)
        gfill = [sb(f"gfill{i}", [P, P], F32) for i in range(2)]
        gfillB = [Buf(), Buf()]
        epsc = sb("epsc", [P, 1], F32)
        kb.op(POOL, lambda: nc.gpsimd.memset(epsc[:], EPS), writes=[B_const], ww=False)
        kb.op(ACT, lambda: nc.scalar.activation(out=csil[:], in_=cvec[:], func=AF.Silu), reads=[B_const], writes=[B_mod])
        PADB = Buf("pads")
        zpad = sb("zpad", [P, 8, 2], F32)
        kb.op(POOL, lambda: nc.gpsimd.memset(zpad[:], 0.0), writes=[B_const], ww=False)
        for pc in (0, 257, 2306):
            kb.dma(QS, h2_s[:, :, pc:pc + 1], zpad[:].bitcast(BF16)[:, :, 0:1], reads=[B_const], writes=[PADB], slow=True)
            kb.dma(QS, v_s[:, :, pc:pc + 1], zpad[:, 0:2, 0:1], reads=[B_const], writes=[PADB], slow=True)
        RES = [Buf(f"res{i}") for i in range(18)]

        def res_ap(sti):
            if sti < 2:
                return xc_s[sti * 128:(sti + 1) * 128, :]
            return out[(sti - 2) * 128:(sti - 1) * 128, :]

        xt_t = [sb(f"xt{i}", [P, D], F32) for i in range(2)]
        xtB = [Buf() for _ in range(2)]
        xn_t = [sb(f"xn{i}", [P, D], BF16) for i in range(2)]
        xnB = [Buf() for _ in range(2)]
        junk = sb("junk", [P, D], BF16)
        junkB = Buf()
        hT_t = [sb(f"hT{i}", [P, 8, 512], BF16) for i in range(2)]
        hTB = [[Buf(), Buf()] for _ in range(2)]
        cnt = {"xt": 0, "xn": 0, "hT": 0, "alt": 0}

        def rstd_from(ssum_ap, ssumB, n_feat):
            i = ssrot[0]
            ssrot[0] = (i + 1) % NSS
            r, rB = ss[i], ssB[i]
            kb.op(ACT, lambda: nc.scalar.activation(out=r[:, 2:3], in_=ssum_ap, func=AF.Ln, scale=1.0 / n_feat, bias=epsc[:, 0:1]),
                  reads=[ssumB, B_const], writes=[rB], n=1)
            kb.op(ACT, lambda: nc.scalar.activation(out=r[:, 3:4], in_=r[:, 2:3], func=AF.Exp, scale=-0.5), reads=[rB], writes=[rB], n=1)
            return r, rB

        def tok_sumsq(xt, xB, junk_=None, junkB_=None):
            junk_, junkB_ = (junk, junkB) if junk_ is None else (junk_, junkB_)
            i = ssrot[0]
            ssrot[0] = (i + 1) % NSS
            r, rB = ss[i], ssB[i]
            kb.op(DVE, lambda: nc.vector.memset(r[:, 0:1], 0.0), writes=[rB], n=1)
            kb.op(ACT, lambda: nc.scalar.activation(out=junk_[:], in_=xt[:], func=AF.Square, accum_out=r[:, 0:1]),
                  reads=[xB, rB], writes=[junkB_, rB], n=1024)
            return r, rB

        def norm_to_hT(xt, xB, l, w, v, hT, hB, st, bank=None, bufs=None, evac=None):
            if bufs is None:
                r0, r0B = tok_sumsq(xt, xB)
            else:
                r0, r0B = tok_sumsq(xt, xB, bufs[2], bufs[3])
            yield
            r, rB = rstd_from(r0[:, 0:1], r0B, D)
            yield
            if bufs is None:
                i = cnt["xn"] % 2
                cnt["xn"] += 1
                xn, xnb = xn_t[i], xnB[i]
            else:
                xn, xnb = bufs[0], bufs[1]
            modb = [GSB[l][w]] + (MODB[l][0:2] if w == 0 else MODB[l][6:8])
            kb.op(DVE, lambda: nc.vector.tensor_scalar(out=xn[:], in0=xt[:], scalar1=r[:, 3:4], scalar2=0.0, op0=ALU.mult, op1=ALU.add),
                  reads=[xB, rB], writes=[xnb], n=1024)
            yield
            ps, psB = nextps() if bank is None else (PS[bank], PSB[bank])
            psb = ps[:].bitcast(BF16)
            for k in range(8):
                kb.op(PE, lambda k=k: nc.tensor.transpose(psb[:, k * 128:(k + 1) * 128], xn[:, k * 128:(k + 1) * 128], idb[:]),
                      reads=[xnb, B_const], writes=[psB], n=128)
            yield
            sh0 = 0 if w == 0 else 24
            use_act = cnt["alt"] % 3 == 0
            cnt["alt"] += 1
            if evac is not None:
                use_act = evac == "act"
            for k in range(8):
                o = hT[:, k, st * 128:(st + 1) * 128]
                i_ = psb[:, k * 128:(k + 1) * 128]
                g_ = GS[:, l, w, k, v:v + 1]
                s_ = MOD[:, l, sh0 + k, v:v + 1]
                if use_act:
                    kb.op(ACT, lambda o=o, i_=i_, g_=g_, s_=s_: nc.scalar.activation(out=o, in_=i_, func=AF.Identity, bias=s_, scale=g_),
                          reads=[psB] + modb, writes=[hB[0]], n=128)
                else:
                    kb.op(DVE, lambda o=o, i_=i_, g_=g_, s_=s_: nc.vector.tensor_scalar(out=o, in0=i_, scalar1=g_, scalar2=s_,
                                                                                     op0=ALU.mult, op1=ALU.add),
                          reads=[psB] + modb, writes=[hB[1]], n=128)
                if k % 2 == 1:
                    yield

        H1B = [Buf(f"h1_{i}") for i in range(5)]
        H2B = [Buf(f"h2_{i}") for i in range(5)]
        VSB = [Buf(f"vs_{i}") for i in range(5)]
        BGB = [Buf(f"bg_{i}") for i in range(5)]

        def get_xt():
            i = cnt["xt"] % len(xt_t)
            cnt["xt"] += 1
            return xt_t[i], xtB[i]

        def get_hT():
            i = cnt["hT"] % 2
            cnt["hT"] += 1
            return hT_t[i], hTB[i]

        def ada_gen(l, groups, aslot, aB, banks):
            ns = len(aslot)
            groups = list(groups)
            issued = 0
            for gi, g in enumerate(groups):
                while issued < len(groups) and issued < gi + ns:
                    gg = groups[issued]
                    kb.dma(QP, aslot[gg % ns][:].rearrange("p k n -> p (k n)"), ada_in[l, gg], writes=[aB[gg % ns]])
                    issued += 1
                    if issued > ns:
                        break
                if gi == 0:
                    yield
                sl, slB = aslot[g % ns], aB[g % ns]
                bk = banks[g % 2]
                psm = PS[bk][:, 0:8]
                for j in range(4):
                    for kc in range(8):
                        kb.op(PE, lambda j=j, kc=kc, sl=sl, psm=psm: nc.tensor.matmul(
                            psm[:, 2 * j:2 * j + 2], sl[:, kc, j * 128:(j + 1) * 128], csil[:, 2 * kc:2 * kc + 2],
                            start=(kc == 0), stop=(kc == 7)), reads=[slB, B_mod], writes=[PSB[bk]], n=32)
                for v in range(2):
                    kb.op(DVE, lambda v=v, g=g, psm=psm: nc.vector.tensor_tensor(
                        out=MOD[:, l, 4 * g:4 * g + 4, v], in0=psm.rearrange("p (j v) -> p j v", v=2)[:, :, v],
                        in1=smalls[:, l, O_AB + 4 * g:O_AB + 4 * g + 4], op=ALU.add),
                        reads=[PSB[bk], B_const], writes=[MODB[l][g]], n=8, ww=False)
                for w, (sc0, og, glast) in enumerate(((8, O_G1, 3), (32, O_G2, 9))):
                    if g == glast:
                        for v in range(2):
                            kb.op(DVE, lambda w=w, v=v, sc0=sc0, og=og: nc.vector.scalar_tensor_tensor(
                                out=GS[:, l, w, :, v], in0=MOD[:, l, sc0:sc0 + 8, v], scalar=1.0,
                                in1=smalls[:, l, og:og + 8], op0=ALU.add, op1=ALU.mult),
                                reads=[MODB[l][glast - 1], MODB[l][glast], B_const], writes=[GSB[l][w]], n=8, ww=False)
                yield

        marks.append(('pass0', PE.cnt))
        with contextlib.ExitStack() as s0:
            A0 = s0.enter_context
            aslot = [sb(f"aslot{i}", [P, 8, 512], BF16, A0) for i in range(2)]
            aB = [Buf(), Buf()]
            p0 = []
            for ti in range(5):
                p0.append(dict(xt=sb(f"p0xt{ti}", [P, D], F32, A0), xB=Buf(), xn=sb(f"p0xn{ti}", [P, D], BF16, A0), xnB=Buf(),
                               junk=sb(f"p0j{ti}", [P, D], BF16, A0), junkB=Buf(), hT=sb(f"p0hT{ti}", [P, 8, 512], BF16, A0), hB=[Buf(), Buf()]))

            def p0_task(ti):
                kind, c0, n = TILES[ti]
                v = 1 if kind == "c" else 0
                b = p0[ti]
                for st in range(n // 128):
                    sti = c0 // 128 + st
                    src = ctx_in[st * 128:(st + 1) * 128, :] if kind == "c" else x_in[(sti - 2) * 128:(sti - 1) * 128, :]
                    kb.dma(QS, b["xt"][:], src, writes=[b["xB"]])
                    yield from norm_to_hT(b["xt"], b["xB"], 0, 0, v, b["hT"], b["hB"], st, bank=2 + ti,
                                          bufs=(b["xn"], b["xnB"], b["junk"], b["junkB"]))
                kb.dma(QA, h1_s[:, :, c0:c0 + n], b["hT"][:, :, 0:n], reads=b["hB"], writes=[H1B[ti]])

            kb.run_tasks([ada_gen(0, range(12), aslot, aB, (0, 1))] + [p0_task(ti) for ti in range(5)])
            kb.barrier()

        for l in range(n_layers):
            last = l == DEPTH - 1

            def sm(o, n=1, l=l):
                return smalls[:, l, o:o + n]

            with contextlib.ExitStack() as s1:
                A = s1.enter_context
                KN = sb("KN", [P, 4, T], BF16, A)
                KR = sb("KR", [P, T], BF16, A)
                Vt = sb("Vt", [P, 18, 512], BF16, A)
                CQS = [Buf(f"cqs{i}") for i in range(5)]
                U = sb("U", [P, 18, 256], BF16, A)
                KVB = [Buf(f"kv{i}") for i in range(5)]
                B_rope = Buf()
                B_w = Buf("wmix")
                GA1 = sb("GA1", [P, 2, D], F32, A)
                WUQ = sb("WUQ", [P, 3, 1024], BF16, A)
                B_ga = Buf()

                def make_gate_bc(dst, ch0, dstB):
                    for v in range(2):
                        for j in range(8):
                            fill, fB = gfill[j % 2], gfillB[j % 2]
                            kb.op(ACT, lambda v=v, j=j, fill=fill: nc.scalar.activation(out=fill[:], in_=idf[:], func=AF.Identity,
                                                                                        bias=MOD[:, l, ch0 + j, v:v + 1], scale=0.0),
                                  reads=MODB[l][ch0 // 4:ch0 // 4 + 2] + [B_const], writes=[fB], n=128)
                            ps, psB = nextps()
                            kb.op(PE, lambda ps=ps, fill=fill: nc.tensor.matmul(ps[:, 0:128], fill[:], idf[:], start=True, stop=True),
                                  reads=[fB, B_const], writes=[psB], n=512)
                            kb.op(DVE, lambda ps=ps, v=v, j=j: nc.vector.tensor_copy(out=dst[:, v, j * 128:(j + 1) * 128], in_=ps[:, 0:128]),
                                  reads=[psB], writes=[dstB], n=128, ww=False)


                marks.append((f'L{l}pass1', PE.cnt))
                with contextlib.ExitStack() as s2:
                    A2 = s2.enter_context
                    WIN = sb("WIN", [P, 8, NWIN], BF16, A2)
                    WIN_SEGS = ((0, 640), (640, 1920))
                    B_wins = [Buf() for _ in WIN_SEGS]
                    for (a_, b_), wb in zip(WIN_SEGS, B_wins):
                        kb.dma(QP, WIN[:, :, a_:b_], win_in[l].rearrange("p (k n) -> p k n", n=NWIN)[:, :, a_:b_], writes=[wb])

                    def winB(col0, ncols):
                        return [wb for (a_, b_), wb in zip(WIN_SEGS, B_wins) if a_ < col0 + ncols and col0 < b_]
                    WUKV = sb("WUKV", [P, 2, 1024], BF16, A2)
                    kb.dma(QP, WUKV[:].rearrange("p k n -> p (k n)"), wukv_in[l], writes=[B_w])
                    ropeC = sb("ropeC", [P, 512], F32, A2)
                    ropeS = sb("ropeS", [P, 512], F32, A2)
                    kb.dma(QP, WUQ[:].rearrange("p k n -> p (k n)"), wuq_in[l], writes=[B_w])
                    make_gate_bc(GA1, 16, B_ga)
                    cqf = sb("cqf", [P, 3, 512], F32, A2)
                    cqf2 = sb("cqf2", [P, 2, 512], F32, A2)
                    cqf2B = Buf()
                    sq2 = [sb(f"sq2_{i}", [P, 512], BF16, A2) for i in range(2)]
                    sq2B = [Buf(), Buf()]
                    rsb2 = sb("rsb2", [P, 3, 512], F32, A2)
                    rsB2 = Buf()
                    cgs, cgB = cqf2, cqf2B

                    def mkrot(banks):
                        st_ = [0]

                        def r():
                            b = banks[st_[0] % len(banks)]
                            st_[0] += 1
                            return PS[b], PSB[b]
                        return r
                    rot1, rot2 = mkrot((0, 1, 2, 3)), mkrot((4, 5, 6))
                    cqo = [sb(f"cqo{i}", [P, 3, 512], BF16, A2) for i in range(2)]
                    cqoB = [Buf(), Buf()]
                    cqfB = Buf()
                    sq = [sb(f"sq{i}", [P, 512], BF16, A2) for i in range(3)]
                    sqB = [Buf() for _ in range(3)]
                    rsb = sb("rsb", [P, 3, 512], F32, A2)
                    rsB = Buf()
                    ckvn = sb("ckvn", [P, 2, 512], BF16, A2)
                    ckvnB = Buf()
                    tA = sb("tA", [P, 512], F32, A2)
                    tB_ = sb("tB", [P, 512], F32, A2)
                    tAB, tBB = Buf(), Buf()
                    vbuf = [sb(f"vbuf{i}", [P, 2, 512], F32, A2) for i in range(1)] * 2
                    vbB = [Buf()] * 2
                    bgbuf = [sb(f"bgbuf{i}", [P, 2, 512], F32, A2) for i in range(1)] * 2
                    bgbB = [Buf()] * 2
                    sqc = [0]

                    def feat_rstd(nchunks, nfeat, getsrc, n, ps_get=None, sqs=None, sqBs=None, rs=None, rsBuf=None):
                        sq_, sqB_ = (sq, sqB) if sqs is None else (sqs, sqBs)
                        rsb_, rsB_ = (rsb, rsB) if rs is None else (rs, rsBuf)
                        nsq = len(sq_)
                        pss, pssB = (ps_get or nextps)()
                        for j in range(nchunks):
                            src_ap, srcB = getsrc(j)
                            i = sqc[0] % nsq
                            sqc[0] += 1
                            kb.op(ACT, lambda src_ap=src_ap, i=i: nc.scalar.activation(out=sq_[i][:, :n], in_=src_ap, func=AF.Square),
                                  reads=[srcB], writes=[sqB_[i]])
                            kb.op(PE, lambda j=j, i=i: nc.tensor.matmul(pss[:, :n], ones[:], sq_[i][:, :n], start=(j == 0), stop=(j == nchunks - 1)),
                                  reads=[sqB_[i], B_const], writes=[pssB])
                        kb.op(ACT, lambda: nc.scalar.activation(out=rsb_[:, 1, :n], in_=pss[:, :n], func=AF.Ln, scale=1.0 / nfeat, bias=epsc[:, 0:1]),
                              reads=[pssB, B_const], writes=[rsB_])
                        kb.op(ACT, lambda: nc.scalar.activation(out=rsb_[:, 2, :n], in_=rsb_[:, 1, :n], func=AF.Exp, scale=-0.5), reads=[rsB_], writes=[rsB_])

                    def win_mm(ps, psB, col0, ncols, hT, hB, n, c_lo=0):
                        for kc in range(8):
                            kb.op(PE, lambda kc=kc: nc.tensor.matmul(ps[0:ncols, :n], WIN[:, kc, col0:col0 + ncols], hT[:, kc, c_lo:c_lo + n],
                                                                    start=(kc == 0), stop=(kc == 7)), reads=winB(col0, ncols) + hB, writes=[psB])

                    ada_next = None
                    if l + 1 < n_layers:
                        aslot1 = [sb(f"aslotn{i}", [P, 8, 512], BF16, A2) for i in range(2)]
                        ada_next = ada_gen(l + 1, range(12), aslot1, [Buf(), Buf()], (7, 7))
                        next(ada_next, None)
                    print('SBUF remaining pass1', nc.sbuf_bytes_remaining)
                    for ti, (kind, c0, n) in enumerate(TILES):
                        lat = kind == "l"
                        kvb = KVB[ti]
                        hT, hB = get_hT()
                        kb.dma(QS, hT[:, :, 0:n], h1_s[:, :, c0:c0 + n], reads=[H1B[ti]], writes=hB)
                        def T1():
                            nextps = rot1
                            for j in range(3):
                                yield
                                ps, psB = nextps()
                                win_mm(ps, psB, j * 128, 128, hT, hB, n)
                                kb.op(ACT, lambda ps=ps, j=j: nc.scalar.activation(out=cqf[:, j, :n], in_=ps[:, :n], func=AF.Copy),
                                      reads=[psB], writes=[cqfB])
                            feat_rstd(3, 384, lambda j: (cqf[:, j, :n], cqfB), n, ps_get=nextps)
                            for j in range(3):
                                kb.op(DVE, lambda j=j: nc.vector.scalar_tensor_tensor(out=cqo[ti % 2][:, j, :n], in0=cqf[:, j, :n], scalar=sm(O_QG + j),
                                                                                     in1=rsb[:, 2, :n], op0=ALU.mult, op1=ALU.mult),
                                      reads=[cqfB, rsB, B_const], writes=[cqoB[ti % 2]])
                            kb.dma(QP, cqn_s[:, :, c0:c0 + n], cqo[ti % 2][:, :, :n], reads=[cqoB[ti % 2]], writes=[CQS[ti]])
                            yield
                            psa, psaB = nextps()
                            win_mm(psa, psaB, 640, 128, hT, hB, n)
                            if lat:
                                psb_, psbB = nextps()
                                win_mm(psb_, psbB, 768, 128, hT, hB, n)
                                r0 = c0 - 256
                                kb.dma(QS, ropeC[:, :n], rope_in[0][:, r0:r0 + n], writes=[B_rope])
                                kb.dma(QS, ropeS[:, :n], rope_in[1][:, r0:r0 + n], writes=[B_rope])
                                kb.op(DVE, lambda: nc.vector.tensor_tensor(out=tA[:, :n], in0=psa[:, :n], in1=ropeC[:, :n], op=ALU.mult),
                                      reads=[psaB, B_rope], writes=[tAB])
                                kb.op(DVE, lambda: nc.vector.tensor_tensor(out=tB_[:, :n], in0=psb_[:, :n], in1=ropeS[:, :n], op=ALU.mult),
                                      reads=[psbB, B_rope], writes=[tBB])
                                kb.op(DVE, lambda: nc.vector.tensor_tensor(out=KR[:, c0:c0 + n], in0=tA[:, :n], in1=tB_[:, :n], op=ALU.add),
                                      reads=[tAB, tBB], writes=[kvb])
                            else:
                                kb.op(ACT, lambda: nc.scalar.activation(out=KR[:, c0:c0 + n], in_=psa[:, :n], func=AF.Copy), reads=[psaB], writes=[kvb])
                            for st in range(n // 128):
                                yield
                                ps, psB = nextps()
                                for kc in range(8):
                                    kb.op(PE, lambda kc=kc, st=st, ps=ps: nc.tensor.matmul(ps[:, 0:256], hT[:, kc, st * 128:(st + 1) * 128], WIN[:, kc, 896:1152],
                                                                                          start=(kc == 0), stop=(kc == 7)), reads=winB(896, 256) + hB, writes=[psB])
                                kt = c0 // 128 + st
                                if st % 2 == 1:
                                    kb.op(ACT, lambda kt=kt, ps=ps: nc.scalar.activation(out=U[:, kt, :], in_=ps[:, 0:256], func=AF.Copy), reads=[psB], writes=[kvb], ww=False, n=256)
                                else:
                                    kb.op(DVE, lambda kt=kt, ps=ps: nc.vector.tensor_copy(out=U[:, kt, :], in_=ps[:, 0:256]), reads=[psB], writes=[kvb], ww=False, n=256)

                        def T2():
                            nextps = rot2
                            for j in range(2):
                                yield
                                ps, psB = nextps()
                                win_mm(ps, psB, 384 + j * 128, 128, hT, hB, n)
                                kb.op(ACT, lambda ps=ps, j=j: nc.scalar.activation(out=cqf2[:, j, :n], in_=ps[:, :n], func=AF.Copy),
                                      reads=[psB], writes=[cqf2B])
                            feat_rstd(2, 256, lambda j: (cqf2[:, j, :n], cqf2B), n, ps_get=nextps, sqs=sq2, sqBs=sq2B, rs=rsb2, rsBuf=rsB2)
                            for j in range(2):
                                kb.op(DVE, lambda j=j: nc.vector.scalar_tensor_tensor(out=ckvn[:, j, :n], in0=cqf2[:, j, :n], scalar=sm(O_KVG + j),
                                                                                     in1=rsb2[:, 2, :n], op0=ALU.mult, op1=ALU.mult),
                                      reads=[cqf2B, rsB2, B_const], writes=[ckvnB])
                            for h in range(4):
                                yield
                                ps, psB = nextps()
                                for kc in range(2):
                                    kb.op(PE, lambda kc=kc, h=h, ps=ps: nc.tensor.matmul(ps[:, :n], WUKV[:, kc, h * 128:(h + 1) * 128], ckvn[:, kc, :n],
                                                                                        start=(kc == 0), stop=(kc == 1)), reads=[B_w, ckvnB], writes=[psB])
                                if h % 2 == 0:
                                    kb.op(ACT, lambda h=h, ps=ps: nc.scalar.activation(out=KN[:, h, c0:c0 + n], in_=ps[:, :n], func=AF.Copy),
                                          reads=[psB], writes=[kvb], ww=False)
                                else:
                                    kb.op(DVE, lambda h=h, ps=ps: nc.vector.tensor_copy(out=KN[:, h, c0:c0 + n], in_=ps[:, :n]), reads=[psB], writes=[kvb], ww=False)
                            for st in range(n // 128):
                                yield
                                ps, psB = nextps()
                                for kc in range(2):
                                    kb.op(PE, lambda kc=kc, st=st, ps=ps: nc.tensor.matmul(ps[:, :], ckvn[:, kc, st * 128:(st + 1) * 128], WUKV[:, kc, 512:1024],
                                                                                          start=(kc == 0), stop=(kc == 1)), reads=[B_w, ckvnB], writes=[psB])
                                kt = c0 // 128 + st
                                if st % 2 == 0:
                                    kb.op(ACT, lambda kt=kt, ps=ps: nc.scalar.activation(out=Vt[:, kt, :], in_=ps[:, :], func=AF.Copy), reads=[psB], writes=[kvb], ww=False)
                                else:
                                    kb.op(DVE, lambda kt=kt, ps=ps: nc.vector.tensor_copy(out=Vt[:, kt, :], in_=ps[:, :]), reads=[psB], writes=[kvb], ww=False)
                            vb, vbb = vbuf[ti % 2], vbB[ti % 2]
                            bgb, bgbb = bgbuf[ti % 2], bgbB[ti % 2]
                            for j in range(2):
                                yield
                                ps, psB = nextps()
                                win_mm(ps, psB, 1152 + j * 128, 128, hT, hB, n)
                                kb.op(ACT, lambda ps=ps, j=j: nc.scalar.activation(out=bgb[:, j, :n], in_=ps[:, :n], func=AF.Copy), reads=[psB], writes=[bgbb])
                            for j in range(2):
                                yield
                                ps, psB = nextps()
                                win_mm(ps, psB, 1408 + j * 128, 128, hT, hB, n)
                                kb.op(ACT, lambda ps=ps, j=j: nc.scalar.activation(out=cgs[:, j, :n], in_=ps[:, :n], func=AF.Copy), reads=[psB], writes=[cgB])
                            for j in range(2):
                                yield
                                ps, psB = nextps()
                                win_mm(ps, psB, 1664 + j * 128, 128, hT, hB, n)
                                kb.op(DVE, lambda ps=ps, j=j: nc.vector.tensor_tensor(out=vb[:, j, :n], in0=ps[:, :n], in1=cgs[:, j, :n], op=ALU.mult),
                                      reads=[psB, cgB], writes=[vbb])
                            kb.dma(QA, bg_s[:, :, c0:c0 + n], bgb[:, :, :n], reads=[bgbb], writes=[BGB[ti]])
                            kb.dma(QP, v_s[:, :, H2C[ti]:H2C[ti] + n], vb[:, :, :n], reads=[vbb], writes=[VSB[ti]])

                        kb.run_tasks([T1(), T2()] + ([itertools.islice(ada_next, 3)] if ada_next is not None else []))
                    kb.barrier()

                marks.append((f'L{l}pass2', PE.cnt))
                with contextlib.ExitStack() as s2:
                    A2 = s2.enter_context
                    WOUT = sb("WOUT", [P, 8, 1024], BF16, A2)
                    B_wout = Buf()
                    kb.dma(QP, WOUT[:].rearrange("p k n -> p (k n)"), wout_in[l], writes=[B_wout])
                    ropeC = sb("ropeC2", [P, 512], F32, A2)
                    ropeS = sb("ropeS2", [P, 512], F32, A2)
                    DT = [sb(f"DT{i}", [P, 2, 2, 256], BF16, A2) for i in range(3)]
                    DTB = [Buf() for _ in range(3)]
                    dtc = [0]
                    QNl = [sb(f"QN{i}", [P, 4, 512], BF16, A2) for i in range(2)]
                    QRl = [sb(f"QR{i}", [P, 2, 512], BF16, A2) for i in range(2)]
                    qBl = [Buf(), Buf()]
                    CQT = [sb(f"CQT{i}", [P, 3, 512], BF16, A2) for i in range(2)]
                    CQTB = [Buf(), Buf()]
                    tA = sb("tA2", [P, 512], F32, A2)
                    tB_ = sb("tB2", [P, 512], F32, A2)
                    tAB, tBB = Buf(), Buf()
                    rinv, rinvB = tA, tAB
                    PT = [sb(f"PT{i}", [P, 512], BF16, A2) for i in range(3)]
                    PTB = [Buf() for _ in range(3)]
                    attf = sb("attf", [P, 4, 512], F32, A2)
                    attB = Buf()
                    Yt = [sb(f"Y{i}", [P, 8, 512], BF16, A2) for i in range(2)]
                    YAB = [Buf(), Buf()]
                    YBB = [Buf(), Buf()]
                    P12 = sb("P12", [P, 4, 256], BF16, A2)
                    P12B = Buf()
                    yff = sb("yff", [P, 2, 256], F32, A2)
                    yffB = Buf()
                    vwj = sb("vwj", [P, 514], F32, A2)
                    vwB = Buf()
                    bgj = sb("bgj", [P, 512], F32, A2)
                    bgwB = Buf()
                    cacc = sb("cacc", [P, 2, 512], F32, A2)
                    caccB = Buf()
                    sqA = [sb(f"sqA{i}", [P, 512], BF16, A2) for i in range(2)]
                    sqAB = [Buf() for _ in range(2)]
                    sqBt = [sb(f"sqB{i}", [P, 512], BF16, A2) for i in range(2)]
                    sqBB = [Buf() for _ in range(2)]
                    rsA = sb("rsA", [P, 2, 512], F32, A2)
                    rsAB = Buf()
                    rsBt = sb("rsB", [P, 2, 512], F32, A2)
                    rsBB = Buf()
                    tmpx = sb("tmpx", [P, 512], F32, A2)
                    tmpxB = Buf()
                    print('SBUF remaining pass2', nc.sbuf_bytes_remaining)

                    def feat_rstd2(nchunks, nfeat, getsrc, n, bank, sqs, sqBs, rs, rsBuf, cnt_):
                        pss, pssB = PS[bank], PSB[bank]
                        for j in range(nchunks):
                            src_ap, srcB = getsrc(j)
                            i = cnt_[0] % 2
                            cnt_[0] += 1
                            kb.op(DVE, lambda src_ap=src_ap, i=i: nc.vector.tensor_tensor(out=sqs[i][:, :n], in0=src_ap, in1=src_ap, op=ALU.mult),
                                  reads=[srcB], writes=[sqBs[i]], n=n)
                            kb.op(PE, lambda j=j, i=i: nc.tensor.matmul(pss[:, :n], ones[:], sqs[i][:, :n], start=(j == 0), stop=(j == nchunks - 1)),
                                  reads=[sqBs[i], B_const], writes=[pssB], n=n)
                        kb.op(ACT, lambda: nc.scalar.activation(out=rs[:, 0, :n], in_=pss[:, :n], func=AF.Ln, scale=1.0 / nfeat, bias=epsc[:, 0:1]),
                              reads=[pssB, B_const], writes=[rsBuf])
                        kb.op(ACT, lambda: nc.scalar.activation(out=rs[:, 1, :n], in_=rs[:, 0, :n], func=AF.Exp, scale=-0.5), reads=[rsBuf], writes=[rsBuf])

                    def feat_rstd2g(nchunks, nfeat, getsrc, n, bank, sqs, sqBs, rs, rsBuf, cnt_):
                        pss, pssB = PS[bank], PSB[bank]
                        for j in range(nchunks):
                            src_ap, srcB = getsrc(j)
                            i = cnt_[0] % 2
                            cnt_[0] += 1
                            kb.op(DVE, lambda src_ap=src_ap, i=i: nc.vector.tensor_tensor(out=sqs[i][:, :n], in0=src_ap, in1=src_ap, op=ALU.mult),
                                  reads=[srcB], writes=[sqBs[i]], n=n)
                            kb.op(PE, lambda j=j, i=i: nc.tensor.matmul(pss[:, :n], ones[:], sqs[i][:, :n], start=(j == 0), stop=(j == nchunks - 1)),
                                  reads=[sqBs[i], B_const], writes=[pssB], n=n)
                        yield
                        kb.op(ACT, lambda: nc.scalar.activation(out=rs[:, 0, :n], in_=pss[:, :n], func=AF.Ln, scale=1.0 / nfeat, bias=epsc[:, 0:1]),
                              reads=[pssB, B_const], writes=[rsBuf])
                        kb.op(ACT, lambda: nc.scalar.activation(out=rs[:, 1, :n], in_=rs[:, 0, :n], func=AF.Exp, scale=-0.5), reads=[rsBuf], writes=[rsBuf])

                    cA, cB = [0], [0]

                    def qproj(ti, yi, b0, b1):
                        kind, c0, n = TILES[ti]
                        lat = kind == "l"
                        QN, QR, qB = QNl[yi], QRl[yi], qBl[yi]
                        cq, cqB = CQT[yi], CQTB[yi]
                        kb.dma(QS, cq[:, :, 0:n], cqn_s[:, :, c0:c0 + n], reads=[CQS[ti]], writes=[cqB])
                        if lat:
                            r0 = c0 - 256
                            kb.dma(QS, ropeC[:, :n], rope_in[0][:, r0:r0 + n], writes=[B_rope])
                            kb.dma(QS, ropeS[:, :n], rope_in[1][:, r0:r0 + n], writes=[B_rope])
                        for h in range(4):
                            ps, psB = PS[(b0, b1)[h % 2]], PSB[(b0, b1)[h % 2]]
                            for kc in range(3):
                                kb.op(PE, lambda kc=kc, h=h, ps=ps: nc.tensor.matmul(ps[:, :n], WUQ[:, kc, h * 128:(h + 1) * 128], cq[:, kc, :n],
                                                                                    start=(kc == 0), stop=(kc == 2)), reads=[B_w, cqB], writes=[psB])
                            kb.op(DVE, lambda h=h, ps=ps: nc.vector.tensor_copy(out=QN[:, h, :n], in_=ps[:, :n]), reads=[psB], writes=[qB], ww=False)
                            yield
                        for pr in range(2):
                            psa, psaB = PS[b0], PSB[b0]
                            for kc in range(3):
                                kb.op(PE, lambda kc=kc, pr=pr, psa=psa: nc.tensor.matmul(psa[:, :n], WUQ[:, kc, 512 + pr * 128:640 + pr * 128], cq[:, kc, :n],
                                                                                        start=(kc == 0), stop=(kc == 2)), reads=[B_w, cqB], writes=[psaB])
                            if lat:
                                psb_, psbB = PS[b1], PSB[b1]
                                for kc in range(3):
                                    kb.op(PE, lambda kc=kc, pr=pr, psb_=psb_: nc.tensor.matmul(psb_[:, :n], WUQ[:, kc, 768 + pr * 128:896 + pr * 128], cq[:, kc, :n],
                                                                                              start=(kc == 0), stop=(kc == 2)), reads=[B_w, cqB], writes=[psbB])
                                kb.op(DVE, lambda psa=psa: nc.vector.tensor_tensor(out=rsBt[:, 0, :n], in0=psa[:, :n], in1=ropeC[:, :n], op=ALU.mult),
                                      reads=[psaB, B_rope], writes=[rsBB], ww=False)
                                kb.op(DVE, lambda psb_=psb_: nc.vector.tensor_tensor(out=rsBt[:, 1, :n], in0=psb_[:, :n], in1=ropeS[:, :n], op=ALU.mult),
                                      reads=[psbB, B_rope], writes=[rsBB], ww=False)
                                kb.op(POOL, lambda pr=pr: nc.gpsimd.tensor_tensor(out=QR[:, pr, :n], in0=rsBt[:, 0, :n], in1=rsBt[:, 1, :n], op=ALU.add),
                                      reads=[rsBB], writes=[qB], ww=False)
                            else:
                                kb.op(DVE, lambda pr=pr, psa=psa: nc.vector.tensor_copy(out=QR[:, pr, :n], in_=psa[:, :n]), reads=[psaB], writes=[qB])
                            yield

                    def taskA(ti, yi):
                        kind, c0, n = TILES[ti]
                        lat = kind == "l"
                        nk = 18 if lat else 2
                        allkv = KVB if lat else KVB[0:1]
                        QN, QR, qB = QNl[yi], QRl[yi], qBl[yi]
                        for h in range(4):
                            kp = (h % 2) * 64

                            def scores(kt, h=h, kp=kp):
                                b = kt % 2
                                kb.op(PE, lambda: nc.tensor.matmul(PS[b][:, :n], KN[:, h, kt * 128:(kt + 1) * 128], QN[:, h, :n], start=True, stop=False),
                                      reads=allkv + [qB], writes=[PSB[b]])
                                kb.op(PE, lambda: nc.tensor.matmul(PS[b][:, :n], KR[kp:kp + 64, kt * 128:(kt + 1) * 128], QR[kp:kp + 64, h // 2, :n],
                                                                   start=False, stop=True), reads=allkv + [qB], writes=[PSB[b]])

                            def expo(kt):
                                b = kt % 2
                                kb.op(ACT, lambda: nc.scalar.activation(out=PT[kt % 3][:, :n], in_=PS[b][:, :n], func=AF.Exp, scale=SM_SCALE),
                                      reads=[PSB[b]], writes=[PTB[kt % 3]])

                            def pv(kt, h=h):
                                b = kt % 3
                                kb.op(PE, lambda: nc.tensor.matmul(PS[2][:, :n], Vt[:, kt, h * 128:(h + 1) * 128], PT[b][:, :n], start=(kt == 0), stop=(kt == nk - 1)),
                                      reads=allkv + [PTB[b]], writes=[PSB[2]])
                                kb.op(PE, lambda: nc.tensor.matmul(PS[3][:, :n], ones[:], PT[b][:, :n], start=(kt == 0), stop=(kt == nk - 1)),
                                      reads=[B_const, PTB[b]], writes=[PSB[3]])

                            scores(0)
                            for kt in range(nk):
                                expo(kt)
                                if kt + 1 < nk:
                                    scores(kt + 1)
                                pv(kt)
                                yield
                            kb.op(ACT, lambda: nc.scalar.activation(out=tB_[:, :n], in_=PS[3][:, :n], func=AF.Ln), reads=[PSB[3]], writes=[tBB])
                            kb.op(ACT, lambda: nc.scalar.activation(out=rinv[:, :n], in_=tB_[:, :n], func=AF.Exp, scale=-1.0), reads=[tBB], writes=[rinvB])
                            kb.op(DVE, lambda h=h: nc.vector.tensor_copy(out=attf[:, h, :n], in_=PS[2][:, :n]), reads=[PSB[2]], writes=[attB])
                            kb.op(DVE, lambda h=h: nc.vector.tensor_tensor(out=attf[:, h, :n], in0=attf[:, h, :n], in1=rinv[:, :n], op=ALU.mult),
                                  reads=[attB, rinvB], writes=[attB])

                    def taskB(ti, yi, nxt=None):
                        kind, c0, n = TILES[ti]
                        lat = kind == "l"
                        allkv = KVB if lat else KVB[0:1]
                        Y, YB = Yt[yi], YBB[yi]
                        nti = 16 if lat else 2
                        scale = (nti * 128 * 64) ** -0.5
                        for kh in range(n // 256):
                            k0 = kh * 256
                            for ti2 in range(nti):
                                if lat:
                                    if ti2 % 2 == 0:
                                        di = dtc[0] % 3
                                        dtc[0] += 1
                                        kb.dma(QS, DT[di][:].rearrange("p a b c -> p (a b c)"), dftl_in[(ti - 1) * 2 + kh, ti2 // 2], writes=[DTB[di]])
                                    tab = lambda di=di, ti2=ti2: DT[di][:, ti2 % 2, :, :].rearrange("p a b -> p (a b)")
                                    tabB = DTB[di]
                                    ug = 2 + ti2
                                else:
                                    tab = lambda ti2=ti2: dftc[:, ti2, :, :].rearrange("p a b -> p (a b)")
                                    tabB = B_const
                                    ug = ti2
                                for f in range(2):
                                    kb.op(PE, lambda f=f, tab=tab, ug=ug, ti2=ti2: nc.tensor.matmul(
                                        PS[4 + f][:, :], U[:, ug, f * 128:(f + 1) * 128], tab(), start=(ti2 == 0), stop=(ti2 == nti - 1)),
                                        reads=[tabB] + allkv, writes=[PSB[4 + f]])
                                yield
                            for f in range(2):
                                src = PS[4 + f][:, :]
                                dst = P12[:, 2 * f:2 * f + 2, :].rearrange("p a b -> p (a b)")
                                if False:
                                    pass
                                else:
                                    kb.op(DVE, lambda src=src, dst=dst: nc.vector.tensor_scalar(out=dst, in0=src, scalar1=scale, scalar2=0.0, op0=ALU.mult, op1=ALU.add),
                                          reads=[PSB[4 + f]], writes=[P12B])
                            for f in range(2):
                                ps, psB = PS[4 + f], PSB[4 + f]
                                kb.op(PE, lambda f=f, ps=ps: nc.tensor.matmul(ps[:, 0:256], bcs[:, 0:128], P12[:, 2 * f, :], start=True, stop=False),
                                      reads=[P12B, B_const], writes=[psB])
                                kb.op(PE, lambda f=f, ps=ps: nc.tensor.matmul(ps[:, 0:256], bcs[:, 128:256], P12[:, 2 * f + 1, :], start=False, stop=True),
                                      reads=[P12B, B_const], writes=[psB])
                                kb.op(DVE, lambda f=f, ps=ps: nc.vector.tensor_copy(out=yff[:, f, :], in_=ps[:, 0:256]), reads=[psB], writes=[yffB], n=256)
                            yield
                            yield from feat_rstd2g(2, 256, lambda j: (yff[:, j, :], yffB), 256, 4, sqBt, sqBB, rsBt, rsBB, cB)
                            for f in range(2):
                                kb.op(DVE, lambda f=f: nc.vector.scalar_tensor_tensor(out=Y[:, 4 + f, k0:k0 + 256], in0=yff[:, f, :], scalar=sm(O_OG + 4 + f),
                                                                                     in1=rsBt[:, 1, 0:256], op0=ALU.mult, op1=ALU.mult),
                                      reads=[yffB, rsBB, B_const], writes=[YB])
                            yield
                        first = ti in (0, 1)
                        lastt = ti in (0, 4)
                        nb = [VSB[ti], PADB] + ([] if first else [VSB[ti - 1]]) + ([] if lastt else [VSB[ti + 1]])
                        lo = 1 if first else 0
                        hi = n - 1 if lastt else n
                        for j in range(2):
                            kb.dma(QS, vwj[:, 0:n + 2], v_s[:, j, H2C[ti] - 1:H2C[ti] + n + 1], reads=nb, writes=[vwB])
                            kb.dma(QS, bgj[:, 0:n], bg_s[:, j, c0:c0 + n], reads=[BGB[ti]], writes=[bgwB])
                            kb.op(POOL, lambda j=j: nc.gpsimd.tensor_scalar(out=cacc[:, j, :n], in0=vwj[:, 1:n + 1], scalar1=sm(O_SW + 3 * j + 1),
                                                                            scalar2=sm(O_SB + j), op0=ALU.mult, op1=ALU.add),
                                  reads=[vwB, B_const], writes=[caccB])
                            kb.op(DVE, lambda j=j: nc.vector.scalar_tensor_tensor(out=cacc[:, j, lo:n], in0=vwj[:, lo:n], scalar=sm(O_SW + 3 * j),
                                                                                  in1=cacc[:, j, lo:n], op0=ALU.mult, op1=ALU.add),
                                  reads=[vwB, B_const, caccB], writes=[caccB])
                            kb.op(DVE, lambda j=j: nc.vector.scalar_tensor_tensor(out=cacc[:, j, 0:hi], in0=vwj[:, 2:hi + 2], scalar=sm(O_SW + 3 * j + 2),
                                                                                  in1=cacc[:, j, 0:hi], op0=ALU.mult, op1=ALU.add),
                                  reads=[vwB, B_const, caccB], writes=[caccB])
                            kb.op(POOL, lambda j=j: nc.gpsimd.tensor_tensor(out=cacc[:, j, :n], in0=cacc[:, j, :n], in1=bgj[:, :n], op=ALU.mult),
                                  reads=[caccB, bgwB], writes=[caccB])
                            yield
                        yield from feat_rstd2g(2, 256, lambda j: (cacc[:, j, :n], caccB), n, 5, sqBt, sqBB, rsBt, rsBB, cB)
                        yield
                        for j in range(2):
                            kb.op(DVE, lambda j=j: nc.vector.scalar_tensor_tensor(out=Y[:, 6 + j, :n], in0=cacc[:, j, :n], scalar=sm(O_OG + 6 + j),
                                                                                 in1=rsBt[:, 1, :n], op0=ALU.mult, op1=ALU.mult),
                                  reads=[caccB, rsBB, B_const], writes=[YB])
                        yield

                        if nxt is not None:
                            yield from qproj(nxt[0], nxt[1], 4, 5)

                    def taskC(ti, yi):
                        kind, c0, n = TILES[ti]
                        v = 0 if kind == "l" else 1
                        Y = Yt[yi]
                        yb = [YAB[yi], YBB[yi]]
                        hT, hB = get_hT()
                        feat_rstd2(4, 512, lambda j: (attf[:, j, :n], attB), n, 6, sqA, sqAB, rsA, rsAB, cA)
                        yield
                        for h in range(4):
                            kb.op(DVE, lambda h=h: nc.vector.scalar_tensor_tensor(out=Y[:, h, :n], in0=attf[:, h, :n], scalar=sm(O_OG + h),
                                                                                 in1=rsA[:, 1, :n], op0=ALU.mult, op1=ALU.mult),
                                  reads=[attB, rsAB, B_const], writes=[YAB[yi]])
                        yield

                        pendc = None
                        for st in range(n // 128):
                            sti = c0 // 128 + st
                            xt, xB = get_xt()
                            if l == 0:
                                src0 = ctx_in[sti * 128:(sti + 1) * 128, :] if sti < 2 else x_in[(sti - 2) * 128:(sti - 1) * 128, :]
                            else:
                                src0 = res_ap(sti)
                            kb.dma(QS, xt[:], src0, reads=[RES[sti]], writes=[xB])
                            for hf in range(2):
                                ps, psB = PS[6 + hf], PSB[6 + hf]
                                for kc in range(8):
                                    kb.op(PE, lambda kc=kc, st=st, hf=hf, ps=ps: nc.tensor.matmul(ps[:, :], Y[:, kc, st * 128:(st + 1) * 128], WOUT[:, kc, hf * 512:(hf + 1) * 512],
                                                                                                 start=(kc == 0), stop=(kc == 7)), reads=yb + [B_wout], writes=[psB])
                                kb.op(DVE, lambda hf=hf, ps=ps: nc.vector.tensor_tensor(out=tmpx[:], in0=ps[:, :], in1=GA1[:, v, hf * 512:(hf + 1) * 512], op=ALU.mult),
                                      reads=[psB, B_ga], writes=[tmpxB])
                                kb.op(POOL, lambda hf=hf, xt=xt: nc.gpsimd.tensor_tensor(out=xt[:, hf * 512:(hf + 1) * 512], in0=xt[:, hf * 512:(hf + 1) * 512], in1=tmpx[:], op=ALU.add),
                                      reads=[tmpxB, xB], writes=[xB])
                                yield
                            kb.dma(QP, res_ap(sti), xt[:], reads=[xB], writes=[RES[sti]])
                            newp = norm_to_hT(xt, xB, l, 1, v, hT, hB, st, bank=6 + (st % 2), evac="dve")
                            for _ in range(3):
                                next(newp)
                                yield
                            if pendc is not None:
                                yield from pendc
                            pendc = newp
                            yield
                        if pendc is not None:
                            yield from pendc
                        kb.dma(QA, h2_s[:, :, H2C[ti]:H2C[ti] + n], hT[:, :, 0:n], reads=hB, writes=[H2B[ti]])

                    p2tiles = [1, 2, 3, 4] + ([] if last else [0])
                    prev = None
                    kb.run_tasks([qproj(p2tiles[0], 0, 4, 5)])
                    for idx, ti in enumerate(p2tiles):
                        nxt = (p2tiles[idx + 1], (idx + 1) % 2) if idx + 1 < len(p2tiles) else None
                        tasks = [taskA(ti, idx % 2), taskB(ti, idx % 2, nxt)]
                        if prev is not None:
                            tasks.append(taskC(prev[0], prev[1]))
                        kb.run_tasks(tasks)
                        prev = (ti, idx % 2)
                    kb.run_tasks([taskC(prev[0], prev[1])])
                    kb.barrier()

            marks.append((f'L{l}ffn', PE.cnt))
            with contextlib.ExitStack() as s1:
                A = s1.enter_context
                GA2 = sb("GA2", [P, 2, D], F32, A)
                B_ga2 = Buf()
                WU = [sb(f"WU{i}", [P, 8, 512], BF16, A) for i in range(2)]
                WUB = [Buf(), Buf()]
                WD = sb("WD", [P, 2, NCF, 512], BF16, A)
                WDB = [Buf(), Buf()]
                G = sb("G", [P, NCF, 1280], BF16, A)
                GB_ = Buf()
                h2w = sb("h2w", [P, 8, 1280 + 6], BF16, A)
                h2wBs = {t: Buf() for t in range(5)}
                gb = [sb(f"gb{i}", [P, 512], F32, A) for i in range(2)]
                gbB = [Buf(), Buf()]
                sg = [sb(f"sg{i}", [P, 512], F32, A) for i in range(2)]
                sgB = [Buf(), Buf()]
                tmpx = sb("tmpx3", [P, 512], F32, A)
                tmpxB = Buf()
                xt_t.append(sb("xt_extra", [P, D], F32, A))
                xtB.append(Buf())
                print('SBUF remaining ffn', nc.sbuf_bytes_remaining)
                supers = [[t for t in (0, 1, 2) if not (last and t == 0)], [3, 4]]
                wuc = [0]
                upc = [0]
                up_defer = [None]
                def load_windows(sup, prev=()):
                    offs, goff = {}, {}
                    o_, g_ = 0, 0
                    for t in sup:
                        kind, c0, n = TILES[t]
                        offs[t], goff[t] = o_, g_
                        first = t in (0, 1)
                        lastt = t in (0, 4)
                        nb = [H2B[t], PADB] + ([] if first else [H2B[t - 1]]) + ([] if lastt else [H2B[t + 1]])
                        kb.dma(QS, h2w[:, :, o_:o_ + n + 2], h2_s[:, :, H2C[t] - 1:H2C[t] + n + 1], reads=nb, writes=[h2wBs[t]] + [h2wBs[u] for u in prev])
                        o_ += n + 2
                        g_ += n
                    return offs, goff

                win_next = load_windows(supers[0])
                for g in range(2):
                    kb.dma(QP, WU[g][:].rearrange("p k n -> p (k n)"), wup_in[l, g], writes=[WUB[g]])
                make_gate_bc(GA2, 40, B_ga2)
                for si, sup in enumerate(supers):
                    offs, goff = win_next
                    for g in range(11):
                        wi = wuc[0] % 2
                        wuc[0] += 1
                        if not (si == 0 and g < 2):
                            kb.dma(QP, WU[wi][:].rearrange("p k n -> p (k n)"), wup_in[l, g], writes=[WUB[wi]])
                        if si == 0 and g == 1:
                            for hf in range(2):
                                kb.dma(QP, WD[:, hf].rearrange("p c n -> p (c n)"), wdn_in[l, hf], writes=[WDB[hf]])
                        for cl in range(2):
                            c = 2 * g + cl
                            for t in sup:
                                kind, c0, n = TILES[t]
                                nh = n // 2
                                o_ = offs[t]
                                first = t in (0, 1)
                                lastt = t in (0, 4)
                                bi = upc[0] % 2
                                vbk = (2, 5, 6, 7)[upc[0] % 4]
                                upc[0] += 1
                                gA, gAB = PS[3 * bi], PSB[3 * bi]
                                gBk, gBB = PS[3 * bi + 1], PSB[3 * bi + 1]
                                vv, vvB = PS[vbk], PSB[vbk]
                                for kc in range(8):
                                    wg = WU[wi][:, kc, cl * 256:cl * 256 + 128]
                                    kb.op(PE, lambda kc=kc, wg=wg: nc.tensor.matmul(gA[:, 0:nh + 2], wg, h2w[:, kc, o_:o_ + nh + 2], start=(kc == 0), stop=(kc == 7)),
                                          reads=[WUB[wi], h2wBs[t]], writes=[gAB])
                                for kc in range(8):
                                    wg = WU[wi][:, kc, cl * 256:cl * 256 + 128]
                                    kb.op(PE, lambda kc=kc, wg=wg: nc.tensor.matmul(gBk[:, 0:nh + 2], wg, h2w[:, kc, o_ + nh:o_ + n + 2], start=(kc == 0), stop=(kc == 7)),
                                          reads=[WUB[wi], h2wBs[t]], writes=[gBB])
                                for kc in range(8):
                                    wv = WU[wi][:, kc, cl * 256 + 128:cl * 256 + 256]
                                    kb.op(PE, lambda kc=kc, wv=wv: nc.tensor.matmul(vv[:, 0:n], wv, h2w[:, kc, o_ + 1:o_ + n + 1], start=(kc == 0), stop=(kc == 7)),
                                          reads=[WUB[wi], h2wBs[t]], writes=[vvB])
                                gbuf, gbb = gb[bi], gbB[bi]
                                w0, w1, w2, bb = sm(O_FW + 3 * c), sm(O_FW + 3 * c + 1), sm(O_FW + 3 * c + 2), sm(O_FB + c)
                                kb.op(ACT, lambda: nc.scalar.activation(out=gbuf[:, 0:nh], in_=gA[:, 1:nh + 1], func=AF.Identity, bias=bb, scale=w1),
                                      reads=[gAB, B_const], writes=[gbb])
                                kb.op(ACT, lambda: nc.scalar.activation(out=gbuf[:, nh:n], in_=gBk[:, 1:nh + 1], func=AF.Identity, bias=bb, scale=w1),
                                      reads=[gBB, B_const], writes=[gbb])
                                lo = 1 if first else 0
                                hi = nh - 1 if lastt else nh
                                kb.op(DVE, lambda: nc.vector.scalar_tensor_tensor(out=gbuf[:, lo:nh], in0=gA[:, lo:nh], scalar=w0, in1=gbuf[:, lo:nh], op0=ALU.mult, op1=ALU.add),
                                      reads=[gAB, B_const, gbb], writes=[gbb])
                                kb.op(DVE, lambda: nc.vector.scalar_tensor_tensor(out=gbuf[:, nh:n], in0=gBk[:, 0:nh], scalar=w0, in1=gbuf[:, nh:n], op0=ALU.mult, op1=ALU.add),
                                      reads=[gBB, B_const, gbb], writes=[gbb])
                                kb.op(DVE, lambda: nc.vector.scalar_tensor_tensor(out=gbuf[:, 0:nh], in0=gA[:, 2:nh + 2], scalar=w2, in1=gbuf[:, 0:nh], op0=ALU.mult, op1=ALU.add),
                                      reads=[gAB, B_const, gbb], writes=[gbb])
                                kb.op(DVE, lambda: nc.vector.scalar_tensor_tensor(out=gbuf[:, nh:nh + hi], in0=gBk[:, 2:hi + 2], scalar=w2, in1=gbuf[:, nh:nh + hi], op0=ALU.mult, op1=ALU.add),
                                      reads=[gBB, B_const, gbb], writes=[gbb])
                                sgb, sgbb = sg[bi], sgB[bi]
                                go = goff[t]

                                def part2(gbuf=gbuf, gbb=gbb, sgb=sgb, sgbb=sgbb, vv=vv, vvB=vvB, c=c, go=go, n=n):
                                    kb.op(ACT, lambda: nc.scalar.activation(out=sgb[:, 0:n], in_=gbuf[:, 0:n], func=AF.Silu), reads=[gbb], writes=[sgbb])
                                    kb.op(DVE, lambda: nc.vector.tensor_tensor(out=G[:, c, go:go + n], in0=vv[:, 0:n], in1=sgb[:, 0:n], op=ALU.mult),
                                          reads=[vvB, sgbb], writes=[GB_], ww=False)
                                if up_defer[0] is not None:
                                    up_defer[0]()
                                up_defer[0] = part2
                    if up_defer[0] is not None:
                        up_defer[0]()
                        up_defer[0] = None
                    if si + 1 < len(supers):
                        win_next = load_windows(supers[si + 1], prev=sup)
                    marks.append((f'L{l}down', PE.cnt))
                    for t in sup:
                        kind, c0, n = TILES[t]
                        v = 1 if kind == "c" else 0
                        if not last:
                            hT, hB = get_hT()
                        pending = None
                        for st in range(n // 128):
                            sti = c0 // 128 + st
                            xt, xB = get_xt()
                            kb.dma(QS, xt[:], res_ap(sti), reads=[RES[sti]], writes=[xB])
                            go = goff[t] + st * 128
                            for hf in range(2):
                                ps, psB = nextps()
                                for c in range(NCF):
                                    kb.op(PE, lambda c=c, hf=hf, ps=ps, go=go: nc.tensor.matmul(ps[:, :], G[:, c, go:go + 128], WD[:, hf, c, :], start=(c == 0), stop=(c == NCF - 1)),
                                          reads=[GB_, WDB[hf]], writes=[psB])
                                kb.op(DVE, lambda hf=hf, ps=ps: nc.vector.tensor_tensor(out=tmpx[:], in0=ps[:, :], in1=GA2[:, v, hf * 512:(hf + 1) * 512], op=ALU.mult),
                                      reads=[psB, B_ga2], writes=[tmpxB])
                                kb.op(DVE, lambda hf=hf, xt=xt: nc.vector.tensor_tensor(out=xt[:, hf * 512:(hf + 1) * 512], in0=xt[:, hf * 512:(hf + 1) * 512], in1=tmpx[:], op=ALU.add),
                                      reads=[tmpxB, xB], writes=[xB])
                            if last:
                                r0, r0B = tok_sumsq(xt, xB)
                                r, rB = rstd_from(r0[:, 0:1], r0B, D)
                                kb.op(DVE, lambda xt=xt, r=r: nc.vector.scalar_tensor_tensor(out=xt[:], in0=xt[:], scalar=r[:, 3:4], in1=fg[:], op0=ALU.mult, op1=ALU.mult),
                                      reads=[xB, rB, B_const], writes=[xB])
                                kb.dma(QA, res_ap(sti), xt[:], reads=[xB], writes=[RES[sti]])
                            else:
                                kb.dma(QA, res_ap(sti), xt[:], reads=[xB], writes=[RES[sti]])
                                newp = norm_to_hT(xt, xB, l + 1, 0, v, hT, hB, st)
                                for _ in range(3):
                                    next(newp)
                                if pending is not None:
                                    for _ in pending:
                                        pass
                                pending = newp
                        if not last:
                            if pending is not None:
                                for _ in pending:
                                    pass
                                pending = None
                            kb.dma(QA, h1_s[:, :, c0:c0 + n], hT[:, :, 0:n], reads=hB, writes=[H1B[t]])
                xt_t.pop()
                xtB.pop()
                kb.barrier()
        kb.barrier()
        marks.append(('end', PE.cnt))
        build_nc.marks = marks
    return nc


def _wP(w):
    K = w.shape[0] // 128
    return np.ascontiguousarray(w.reshape(K, 128, -1).transpose(1, 0, 2).reshape(128, -1))


def _chunkP(v):
    return np.ascontiguousarray(v.reshape(-1, 128).T)


_CONST_CACHE = {}


def _constants():
    if _CONST_CACHE:
        return _CONST_CACHE
    bf = ml_dtypes.bfloat16
    t = np.arange(S)
    r, c = (t // 64).astype(np.float64), (t % 64).astype(np.float64)
    inv = 10000.0 ** (-np.arange(0, 32, 2, dtype=np.float64) / 32)
    d = np.arange(64)
    a, j, i = d // 32, (d % 32) // 16, d % 16
    pos = np.where(a[:, None] == 0, r[None, :], c[None, :])
    ang = (pos.astype(np.float32) * inv.astype(np.float32)[i][:, None]).astype(np.float32).astype(np.float64)
    cosT = np.cos(ang)
    sinT = np.sin(ang) * np.where(j == 0, -1.0, 1.0)[:, None]
    rope = np.stack([np.concatenate([cosT, cosT], 0), np.concatenate([sinT, sinT], 0)]).astype(np.float32)
    tt = (np.arange(8)[:, None, None] * 256 + np.arange(2)[None, :, None] * 128 + np.arange(128)[None, None, :])
    dftl = np.empty((8, 8, 128, 2, 2, 256), dtype=bf)
    for kt in range(8):
        kk = kt * 256 + np.arange(256)
        ph = (tt[..., None] * kk[None, None, None, :]) % S
        angl = ph.astype(np.float64) * (2 * np.pi / S)
        dftl[kt, :, :, :, 0, :] = np.cos(angl).transpose(0, 2, 1, 3).astype(bf)
        dftl[kt, :, :, :, 1, :] = np.sin(angl).transpose(0, 2, 1, 3).astype(bf)
    dftl = dftl.reshape(8, 8, 128, 2 * 2 * 256)
    tc_ = np.arange(2)[:, None] * 128 + np.arange(128)[None, :]
    phc = (tc_[..., None] * np.arange(256)[None, None, :]) % CT
    angc = phc.astype(np.float64) * (2 * np.pi / CT)
    dftc = np.stack([np.cos(angc), np.sin(angc)], axis=2).transpose(1, 0, 2, 3).astype(bf).reshape(128, 2 * 2 * 256)
    m = np.arange(64)
    a64 = ((m[:, None] * m[None, :]) % 64).astype(np.float64) * (2 * np.pi / 64)
    bc = np.zeros((128, 128))
    bs = np.zeros((128, 128))
    for g in range(2):
        bc[g * 64:(g + 1) * 64, g * 64:(g + 1) * 64] = np.cos(a64)
        bs[g * 64:(g + 1) * 64, g * 64:(g + 1) * 64] = -np.sin(a64)
    bcs = np.concatenate([bc, bs], 1).astype(bf)
    _CONST_CACHE.update(rope=rope, dftl=dftl, dftc=dftc, bcs=bcs, idb=np.eye(128).astype(bf), idf=np.eye(128, dtype=np.float32))
    return _CONST_CACHE


def _prep_shared(inp):
    f = lambda k: np.asarray(inp[k], dtype=np.float32)
    d64 = np.arange(64)
    sw = np.where(d64 % 32 < 16, d64 + 16, d64 - 16)
    kr = 640 + d64
    krs = 640 + sw
    win_idx = np.concatenate([np.arange(0, 640), kr, kr, krs, krs, np.arange(704, 1728)])
    up_idx = np.concatenate([np.concatenate([np.arange(c * 128, (c + 1) * 128), DFF + np.arange(c * 128, (c + 1) * 128)]) for c in range(NCF)])
    smalls = np.zeros((DEPTH, 128, NSM), np.float32)
    ada = np.empty((DEPTH, 12, 128, 8 * 512), np.float32)
    win = np.empty((DEPTH, 128, 8 * NWIN), np.float32)
    wuq = np.empty((DEPTH, 128, 3 * 1024), np.float32)
    wukv = np.empty((DEPTH, 128, 2 * 1024), np.float32)
    wout = np.empty((DEPTH, 128, 8 * 1024), np.float32)
    wup = np.empty((DEPTH, 11, 128, 8 * 512), np.float32)
    wdn = np.empty((DEPTH, 2, 128, NCF * 512), np.float32)
    for l in range(DEPTH):
        s = smalls[l]
        s[:, O_G1:O_G1 + 8] = _chunkP(f("norm1_g")[l])
        s[:, O_QG:O_QG + 3] = _chunkP(f("q_norm_g")[l])
        s[:, O_KVG:O_KVG + 2] = _chunkP(f("kv_norm_g")[l])
        sw_ = f("sconv_w")[l]
        for j in range(2):
            for tap in range(3):
                s[:, O_SW + 3 * j + tap] = sw_[tap, j * 128:(j + 1) * 128]
        s[:, O_SB:O_SB + 2] = _chunkP(f("sconv_b")[l])
        s[:, O_OG:O_OG + 8] = _chunkP(f("out_norm_g")[l])
        s[:, O_G2:O_G2 + 8] = _chunkP(f("norm2_g")[l])
        fw = f("ffconv_w")[l]
        for c in range(NCF):
            for tap in range(3):
                s[:, O_FW + 3 * c + tap] = fw[tap, c * 128:(c + 1) * 128]
        s[:, O_FB:O_FB + NCF] = _chunkP(f("ffconv_b")[l])
        s[:, O_AB:O_AB + 48] = _chunkP(f("ada_b")[l])
        aw = f("ada_w")[l]
        for g in range(12):
            ada[l, g] = _wP(aw[:, g * 512:(g + 1) * 512])
        win[l] = _wP(f("w_in")[l][:, win_idx])
        uq = f("w_uq")[l]
        cols = [uq[:, h, 0:128] for h in range(4)]
        cols += [np.concatenate([uq[:, 2 * pr, 128:192], uq[:, 2 * pr + 1, 128:192]], 1) for pr in range(2)]
        cols += [np.concatenate([uq[:, 2 * pr, 128 + sw], uq[:, 2 * pr + 1, 128 + sw]], 1) for pr in range(2)]
        wuq[l] = _wP(np.concatenate(cols, 1))
        ukv = f("w_ukv")[l]
        wukv[l] = _wP(np.concatenate([ukv[:, h, 0:128] for h in range(4)] + [ukv[:, h, 128:256] for h in range(4)], 1))
        wout[l] = _wP(f("w_out")[l])
        upp = f("w_up")[l][:, up_idx]
        for g in range(11):
            wup[l, g] = _wP(upp[:, g * 512:(g + 1) * 512])
        dn = f("w_down")[l]
        for hf in range(2):
            wdn[l, hf] = _wP(dn[:, hf * 512:(hf + 1) * 512])
    fg = np.ascontiguousarray(np.broadcast_to(f("final_g")[None, :], (128, D)))
    return dict(smalls=smalls, ada=ada, win=win, wuq=wuq, wukv=wukv, wout=wout, wup=wup, wdn=wdn, fg=fg)


def kernel(**inputs):
    consts = _constants()
    shared = _prep_shared(inputs)
    x = np.asarray(inputs["x"], np.float32)
    c = np.asarray(inputs["c"], np.float32)
    ctx = np.asarray(inputs["ctx"], np.float32)
    c_ctx = np.asarray(inputs["c_ctx"], np.float32)
    nc = build_nc()
    in_maps = []
    for b in range(8):
        cv = np.stack([_chunkP(c[b]), _chunkP(c_ctx)], axis=2).reshape(128, 16)
        m = dict(x=np.ascontiguousarray(x[b]), ctx=np.ascontiguousarray(ctx[b]), cvec=np.ascontiguousarray(cv))
        m.update(shared)
        m.update(consts)
        in_maps.append(m)
    res = run_bass_kernel_spmd(nc, in_maps, core_ids=list(range(8)))
    return np.stack([np.asarray(r["out"], dtype=np.float32) for r in res.results], axis=0)
```

```python
import contextlib
import itertools
import numpy as np
import ml_dtypes
import concourse.bass as bass
import concourse.mybir as mybir
from concourse.bass_utils import run_bass_kernel_spmd

F32 = mybir.dt.float32
BF16 = mybir.dt.bfloat16
AF = mybir.ActivationFunctionType
ALU = mybir.AluOpType

P = 128
D = 1024
S = 2048
CT = 256
T = S + CT
DFF = 2816
NCF = 22
DEPTH = 2
EPS = 1e-6
SM_SCALE = 192 ** -0.5
NWIN = 1920
O_G1, O_QG, O_KVG, O_SW, O_SB, O_OG, O_G2, O_FW, O_FB, O_AB = 0, 8, 11, 13, 19, 21, 29, 37, 103, 125
NSM = 173
TILES = [("c", 0, 256)] + [("l", 256 + 512 * i, 512) for i in range(4)]
H2C = {0: 1, 1: 258, 2: 258 + 512, 3: 258 + 1024, 4: 258 + 1536}
H2W = 2307


class Buf:
    __slots__ = ("lw", "rd", "name")

    def __init__(self, name=""):
        self.lw = {}
        self.rd = {}
        self.name = name


class Eng:
    def __init__(self, obj, sem, inorder=False, kind="dve"):
        self.obj = obj
        self.sem = sem
        self.cnt = 0
        self.waited = {}
        self.inorder = inorder
        self.kind = kind
        self.free = 0.0


class DmaQ:
    def __init__(self, issuer, sems):
        self.issuer = issuer
        self.sems = sems
        self.vals = [0] * len(sems)
        self.n = 0
        self.free = 0.0


class Task:
    def __init__(self, gen):
        self.gen = gen
        self.vt = 0.0


class KB:
    SEM_LAT = 120.0

    def __init__(self, nc):
        self.nc = nc
        self.engs = []
        self.qs = []
        self.fin = {}
        self.cur = None
        self.now = 0.0
        self.rr = 0
        self.round_robin = False
        self.force_ww = False

    def wait(self, eng, tok):
        sem, val = tok
        if eng.waited.get(sem.num, 0) >= val:
            return
        eng.obj.wait_ge(sem, val)
        eng.waited[sem.num] = val

    def _deps(self, eng, reads, writes, ww):
        t = 0.0
        toks = []
        for b in reads:
            toks.extend(b.lw.values())
        for b in writes:
            if ww:
                toks.extend(b.lw.values())
            toks.extend(b.rd.values())
        for tok in toks:
            same = tok[0] is eng.sem
            t = max(t, self.fin.get((tok[0].num, tok[1]), 0.0) + (0.0 if same else self.SEM_LAT))
            if not (eng.inorder and same):
                self.wait(eng, tok)
        return t

    def _mark(self, tok, reads, writes, ww=True):
        for b in reads:
            b.rd[tok[0].num] = tok
        for b in writes:
            if ww:
                b.lw = {tok[0].num: tok}
                b.rd = {}
            else:
                b.lw[tok[0].num] = tok

    def op(self, eng, fn, reads=(), writes=(), n=512, ww=True):
        ww = True if self.force_ww else ww
        tdep = self._deps(eng, reads, writes, ww)
        ins = fn()
        eng.cnt += 1
        ins.then_inc(eng.sem, 1)
        tok = (eng.sem, eng.cnt)
        if eng.inorder:
            cost, lat = n / 2.4 + 12.0, 120.0
        elif eng.kind == "act":
            cost, lat = 200.0 + n * 1.0, 60.0
        elif eng.kind == "pool":
            cost, lat = 150.0 + n * 2.0, 60.0
        else:
            cost, lat = 90.0 + n * 1.0, 60.0
        start = max(eng.free, tdep, self.cur.vt if self.cur is not None else 0.0)
        eng.free = start + cost
        self.fin[(tok[0].num, tok[1])] = start + cost + lat
        if self.cur is not None:
            self.cur.vt = start
        self._mark(tok, reads, writes, ww)

    def dma(self, q, out, in_, reads=(), writes=(), slow=False, nbytes=1 << 20, ww=True):
        eng = q.issuer
        tdep = self._deps(eng, reads, writes, ww)
        k = q.n % len(q.sems)
        q.n += 1
        if q.vals[k] > 0:
            self.wait(eng, (q.sems[k], q.vals[k]))
            tdep = max(tdep, self.fin.get((q.sems[k].num, q.vals[k]), 0.0))
        ins = eng.obj.dma_start(out=out, in_=in_, allow_slow_non_contiguous=True) if slow else eng.obj.dma_start(out=out, in_=in_)
        q.vals[k] += 16
        ins.then_inc(q.sems[k], 16)
        tok = (q.sems[k], q.vals[k])
        start = max(eng.free, tdep, self.cur.vt if self.cur is not None else 0.0)
        eng.free = start + 60.0
        q.free = max(q.free, start) + nbytes / 150.0
        self.fin[(tok[0].num, tok[1])] = max(q.free, start + 2200.0)
        self._mark(tok, reads, writes, ww)

    def barrier(self):
        toks = [(e.sem, e.cnt) for e in self.engs if e.cnt > 0]
        for q in self.qs:
            toks += [(s, v) for s, v in zip(q.sems, q.vals) if v > 0]
        tmax = max([e.free for e in self.engs] + [self.fin.get((t[0].num, t[1]), 0.0) for t in toks])
        for e in self.engs:
            e.free = tmax
            for tok in toks:
                if tok[0] is e.sem:
                    continue
                self.wait(e, tok)

    def run_tasks(self, gens):
        tasks = [Task(g) for g in gens if g is not None]
        t0 = max(e.free for e in self.engs) if False else min(e.free for e in self.engs)
        for t in tasks:
            t.vt = t0
        while tasks:
            t = tasks[self.rr % len(tasks)] if self.round_robin else min(tasks, key=lambda x: x.vt)
            self.rr += 1
            self.cur = t
            try:
                next(t.gen)
            except StopIteration:
                tasks.remove(t)
            self.cur = None


def run_tasks(tasks):
    act = [t for t in tasks if t is not None]
    while act:
        for t in list(act):
            try:
                next(t)
            except StopIteration:
                act.remove(t)


def build_nc(n_layers=DEPTH):
    nc = bass.Bass("TRN2", target_bir_lowering=False)
    dt = nc.dram_tensor
    x_in = dt("x", [S, D], F32, kind="ExternalInput").ap()
    ctx_in = dt("ctx", [CT, D], F32, kind="ExternalInput").ap()
    cvec_in = dt("cvec", [P, 16], F32, kind="ExternalInput").ap()
    smalls_in = dt("smalls", [DEPTH, P, NSM], F32, kind="ExternalInput").ap()
    ada_in = dt("ada", [DEPTH, 12, P, 8 * 512], F32, kind="ExternalInput").ap()
    win_in = dt("win", [DEPTH, P, 8 * NWIN], F32, kind="ExternalInput").ap()
    wuq_in = dt("wuq", [DEPTH, P, 3 * 1024], F32, kind="ExternalInput").ap()
    wukv_in = dt("wukv", [DEPTH, P, 2 * 1024], F32, kind="ExternalInput").ap()
    wout_in = dt("wout", [DEPTH, P, 8 * 1024], F32, kind="ExternalInput").ap()
    wup_in = dt("wup", [DEPTH, 11, P, 8 * 512], F32, kind="ExternalInput").ap()
    wdn_in = dt("wdn", [DEPTH, 2, P, NCF * 512], F32, kind="ExternalInput").ap()
    fg_in = dt("fg", [P, D], F32, kind="ExternalInput").ap()
    rope_in = dt("rope", [2, P, S], F32, kind="ExternalInput").ap()
    dftl_in = dt("dftl", [8, 8, P, 2 * 2 * 256], BF16, kind="ExternalInput").ap()
    dftc_in = dt("dftc", [P, 2 * 2 * 256], BF16, kind="ExternalInput").ap()
    bcs_in = dt("bcs", [P, 256], BF16, kind="ExternalInput").ap()
    idb_in = dt("idb", [P, P], BF16, kind="ExternalInput").ap()
    idf_in = dt("idf", [P, P], F32, kind="ExternalInput").ap()
    out = dt("out", [S, D], F32, kind="ExternalOutput").ap()
    xc_s = dt("xc_s", [CT, D], F32, kind="Internal").ap()
    h1_s = dt("h1_s", [P, 8, T], BF16, kind="Internal").ap()
    h2_s = dt("h2_s", [P, 8, H2W], BF16, kind="Internal").ap()
    v_s = dt("v_s", [P, 2, H2W], F32, kind="Internal").ap()
    bg_s = dt("bg_s", [P, 2, T], F32, kind="Internal").ap()
    cqn_s = dt("cqn_s", [P, 3, T], BF16, kind="Internal").ap()

    es = contextlib.ExitStack()
    with es:
        E = es.enter_context
        kb = KB(nc)
        marks = []
        nc_marks = marks
        PE = Eng(nc.tensor, E(nc.semaphore("s_pe")), inorder=True)
        ACT = Eng(nc.scalar, E(nc.semaphore("s_act")), kind="act")
        DVE = Eng(nc.vector, E(nc.semaphore("s_dve")))
        POOL = Eng(nc.gpsimd, E(nc.semaphore("s_pool")), kind="pool")
        SP = Eng(nc.sync, E(nc.semaphore("s_sp")), kind="sp")
        kb.engs = [PE, ACT, DVE, POOL, SP]
        QS = DmaQ(SP, [E(nc.semaphore(f"qs{i}")) for i in range(12)])
        QP = DmaQ(POOL, [E(nc.semaphore(f"qp{i}")) for i in range(6)])
        QA = DmaQ(ACT, [E(nc.semaphore(f"qa{i}")) for i in range(4)])
        kb.qs = [QS, QP, QA]

        nmc = [0]

        def sb(name, shape, dtype, scope=None):
            nmc[0] += 1
            return (scope or E)(nc.sbuf_tensor(f"sb{nmc[0]}_{name}", shape, dtype))

        PS = [E(nc.psum_tensor(f"ps{i}", [P, 512], F32)) for i in range(8)]
        PSB = [Buf(f"ps{i}") for i in range(8)]
        psrot = [5]

        def nextps():
            i = psrot[0]
            psrot[0] = 5 + (i - 5 + 1) % 3
            return PS[i], PSB[i]

        idb = sb("idb", [P, P], BF16)
        idf = sb("idf", [P, P], F32)
        ones = sb("ones", [P, P], BF16)
        bcs = sb("bcs", [P, 256], BF16)
        dftc = sb("dftc", [P, 2, 2, 256], BF16)
        fg = sb("fg", [P, D], F32)
        smalls = sb("smalls", [P, DEPTH, NSM], F32)
        cvec = sb("cvec", [P, 16], F32)
        csil = sb("csil", [P, 16], BF16)
        MOD = sb("mod", [P, DEPTH, 48, 2], F32)
        GS = sb("gs", [P, DEPTH, 2, 8, 2], F32)
        B_const = Buf("const")
        B_mod = Buf("mod")
        MODB = [[Buf(f"mod{l}_{g}") for g in range(12)] for l in range(DEPTH)]
        GSB = [[Buf(), Buf()] for l in range(DEPTH)]
        NSS = 16
        ss = [sb(f"ss{i}", [P, 4], F32) for i in range(NSS)]
        ssB = [Buf() for _ in range(NSS)]
        ssrot = [0]

        for dst, src in ((idb[:], idb_in), (idf[:], idf_in), (bcs[:], bcs_in), (fg[:], fg_in),
                         (cvec[:], cvec_in), (dftc[:].rearrange("p a b c -> p (a b c)"), dftc_in)):
            kb.dma(QS, dst, src, writes=[B_const], ww=False)
        for l in range(DEPTH):
            kb.dma(QS, smalls[:, l, :], smalls_in[l], writes=[B_const], ww=False)
        kb.op(POOL, lambda: nc.gpsimd.memset(ones[:], 1.0), writes=[B_const], ww=False)
        gfill = [sb(f"gfill{i}", [P, P], F32) for i in range(2)]
        gfillB = [Buf(), Buf()]
        epsc = sb("epsc", [P, 1], F32)
        kb.op(POOL, lambda: nc.gpsimd.memset(epsc[:], EPS), writes=[B_const], ww=False)
        kb.op(ACT, lambda: nc.scalar.activation(out=csil[:], in_=cvec[:], func=AF.Silu), reads=[B_const], writes=[B_mod])
        PADB = Buf("pads")
        zpad = sb("zpad", [P, 8, 2], F32)
        kb.op(POOL, lambda: nc.gpsimd.memset(zpad[:], 0.0), writes=[B_const], ww=False)
        for pc in (0, 257, 2306):
            kb.dma(QS, h2_s[:, :, pc:pc + 1], zpad[:].bitcast(BF16)[:, :, 0:1], reads=[B_const], writes=[PADB], slow=True)
            kb.dma(QS, v_s[:, :, pc:pc + 1], zpad[:, 0:2, 0:1], reads=[B_const], writes=[PADB], slow=True)
        RES = [Buf(f"res{i}") for i in range(18)]

        def res_ap(sti):
            if sti < 2:
                return xc_s[sti * 128:(sti + 1) * 128, :]
            return out[(sti - 2) * 128:(sti - 1) * 128, :]

        xt_t = [sb(f"xt{i}", [P, D], F32) for i in range(2)]
        xtB = [Buf() for _ in range(2)]
        xn_t = [sb(f"xn{i}", [P, D], BF16) for i in range(2)]
        xnB = [Buf() for _ in range(2)]
        junk = sb("junk", [P, D], BF16)
        junkB = Buf()
        hT_t = [sb(f"hT{i}", [P, 8, 512], BF16) for i in range(2)]
        hTB = [[Buf(), Buf()] for _ in range(2)]
        cnt = {"xt": 0, "xn": 0, "hT": 0, "alt": 0}

        def rstd_from(ssum_ap, ssumB, n_feat):
            i = ssrot[0]
            ssrot[0] = (i + 1) % NSS
            r, rB = ss[i], ssB[i]
            kb.op(ACT, lambda: nc.scalar.activation(out=r[:, 2:3], in_=ssum_ap, func=AF.Ln, scale=1.0 / n_feat, bias=epsc[:, 0:1]),
                  reads=[ssumB, B_const], writes=[rB], n=1)
            kb.op(ACT, lambda: nc.scalar.activation(out=r[:, 3:4], in_=r[:, 2:3], func=AF.Exp, scale=-0.5), reads=[rB], writes=[rB], n=1)
            return r, rB

        def tok_sumsq(xt, xB, junk_=None, junkB_=None):
            junk_, junkB_ = (junk, junkB) if junk_ is None else (junk_, junkB_)
            i = ssrot[0]
            ssrot[0] = (i + 1) % NSS
            r, rB = ss[i], ssB[i]
            kb.op(DVE, lambda: nc.vector.memset(r[:, 0:1], 0.0), writes=[rB], n=1)
            kb.op(ACT, lambda: nc.scalar.activation(out=junk_[:], in_=xt[:], func=AF.Square, accum_out=r[:, 0:1]),
                  reads=[xB, rB], writes=[junkB_, rB], n=1024)
            return r, rB

        def norm_to_hT(xt, xB, l, w, v, hT, hB, st, bank=None, bufs=None, evac=None):
            if bufs is None:
                r0, r0B = tok_sumsq(xt, xB)
            else:
                r0, r0B = tok_sumsq(xt, xB, bufs[2], bufs[3])
            yield
            r, rB = rstd_from(r0[:, 0:1], r0B, D)
            yield
            if bufs is None:
                i = cnt["xn"] % 2
                cnt["xn"] += 1
                xn, xnb = xn_t[i], xnB[i]
            else:
                xn, xnb = bufs[0], bufs[1]
            modb = [GSB[l][w]] + (MODB[l][0:2] if w == 0 else MODB[l][6:8])
            kb.op(DVE, lambda: nc.vector.tensor_scalar(out=xn[:], in0=xt[:], scalar1=r[:, 3:4], scalar2=0.0, op0=ALU.mult, op1=ALU.add),
                  reads=[xB, rB], writes=[xnb], n=1024)
            yield
            ps, psB = nextps() if bank is None else (PS[bank], PSB[bank])
            psb = ps[:].bitcast(BF16)
            for k in range(8):
                kb.op(PE, lambda k=k: nc.tensor.transpose(psb[:, k * 128:(k + 1) * 128], xn[:, k * 128:(k + 1) * 128], idb[:]),
                      reads=[xnb, B_const], writes=[psB], n=128)
            yield
            sh0 = 0 if w == 0 else 24
            use_act = cnt["alt"] % 3 == 0
            cnt["alt"] += 1
            if evac is not None:
                use_act = evac == "act"
            for k in range(8):
                o = hT[:, k, st * 128:(st + 1) * 128]
                i_ = psb[:, k * 128:(k + 1) * 128]
                g_ = GS[:, l, w, k, v:v + 1]
                s_ = MOD[:, l, sh0 + k, v:v + 1]
                if use_act:
                    kb.op(ACT, lambda o=o, i_=i_, g_=g_, s_=s_: nc.scalar.activation(out=o, in_=i_, func=AF.Identity, bias=s_, scale=g_),
                          reads=[psB] + modb, writes=[hB[0]], n=128)
                else:
                    kb.op(DVE, lambda o=o, i_=i_, g_=g_, s_=s_: nc.vector.tensor_scalar(out=o, in0=i_, scalar1=g_, scalar2=s_,
                                                                                     op0=ALU.mult, op1=ALU.add),
                          reads=[psB] + modb, writes=[hB[1]], n=128)
                if k % 2 == 1:
                    yield

        H1B = [Buf(f"h1_{i}") for i in range(5)]
        H2B = [Buf(f"h2_{i}") for i in range(5)]
        VSB = [Buf(f"vs_{i}") for i in range(5)]
        BGB = [Buf(f"bg_{i}") for i in range(5)]

        def get_xt():
            i = cnt["xt"] % len(xt_t)
            cnt["xt"] += 1
            return xt_t[i], xtB[i]

        def get_hT():
            i = cnt["hT"] % 2
            cnt["hT"] += 1
            return hT_t[i], hTB[i]

        def ada_gen(l, groups, aslot, aB, banks):
            ns = len(aslot)
            groups = list(groups)
            issued = 0
            for gi, g in enumerate(groups):
                while issued < len(groups) and issued < gi + ns:
                    gg = groups[issued]
                    kb.dma(QP, aslot[gg % ns][:].rearrange("p k n -> p (k n)"), ada_in[l, gg], writes=[aB[gg % ns]])
                    issued += 1
                    if issued > ns:
                        break
                if gi == 0:
                    yield
                sl, slB = aslot[g % ns], aB[g % ns]
                bk = banks[g % 2]
                psm = PS[bk][:, 0:8]
                for j in range(4):
                    for kc in range(8):
                        kb.op(PE, lambda j=j, kc=kc, sl=sl, psm=psm: nc.tensor.matmul(
                            psm[:, 2 * j:2 * j + 2], sl[:, kc, j * 128:(j + 1) * 128], csil[:, 2 * kc:2 * kc + 2],
                            start=(kc == 0), stop=(kc == 7)), reads=[slB, B_mod], writes=[PSB[bk]], n=32)
                for v in range(2):
                    kb.op(DVE, lambda v=v, g=g, psm=psm: nc.vector.tensor_tensor(
                        out=MOD[:, l, 4 * g:4 * g + 4, v], in0=psm.rearrange("p (j v) -> p j v", v=2)[:, :, v],
                        in1=smalls[:, l, O_AB + 4 * g:O_AB + 4 * g + 4], op=ALU.add),
                        reads=[PSB[bk], B_const], writes=[MODB[l][g]], n=8, ww=False)
                for w, (sc0, og, glast) in enumerate(((8, O_G1, 3), (32, O_G2, 9))):
                    if g == glast:
                        for v in range(2):
                            kb.op(DVE, lambda w=w, v=v, sc0=sc0, og=og: nc.vector.scalar_tensor_tensor(
                                out=GS[:, l, w, :, v], in0=MOD[:, l, sc0:sc0 + 8, v], scalar=1.0,
                                in1=smalls[:, l, og:og + 8], op0=ALU.add, op1=ALU.mult),
                                reads=[MODB[l][glast - 1], MODB[l][glast], B_const], writes=[GSB[l][w]], n=8, ww=False)
                yield

        marks.append(('pass0', PE.cnt))
        with contextlib.ExitStack() as s0:
            A0 = s0.enter_context
            aslot = [sb(f"aslot{i}", [P, 8, 512], BF16, A0) for i in range(2)]
            aB = [Buf(), Buf()]
            p0 = []
            for ti in range(5):
                p0.append(dict(xt=sb(f"p0xt{ti}", [P, D], F32, A0), xB=Buf(), xn=sb(f"p0xn{ti}", [P, D], BF16, A0), xnB=Buf(),
                               junk=sb(f"p0j{ti}", [P, D], BF16, A0), junkB=Buf(), hT=sb(f"p0hT{ti}", [P, 8, 512], BF16, A0), hB=[Buf(), Buf()]))

            def p0_task(ti):
                kind, c0, n = TILES[ti]
                v = 1 if kind == "c" else 0
                b = p0[ti]
                for st in range(n // 128):
                    sti = c0 // 128 + st
                    src = ctx_in[st * 128:(st + 1) * 128, :] if kind == "c" else x_in[(sti - 2) * 128:(sti - 1) * 128, :]
                    kb.dma(QS, b["xt"][:], src, writes=[b["xB"]])
                    yield from norm_to_hT(b["xt"], b["xB"], 0, 0, v, b["hT"], b["hB"], st, bank=2 + ti,
                                          bufs=(b["xn"], b["xnB"], b["junk"], b["junkB"]))
                kb.dma(QA, h1_s[:, :, c0:c0 + n], b["hT"][:, :, 0:n], reads=b["hB"], writes=[H1B[ti]])

            kb.run_tasks([ada_gen(0, range(12), aslot, aB, (0, 1))] + [p0_task(ti) for ti in range(5)])
            kb.barrier()

        for l in range(n_layers):
            last = l == DEPTH - 1

            def sm(o, n=1, l=l):
                return smalls[:, l, o:o + n]

            with contextlib.ExitStack() as s1:
                A = s1.enter_context
                KN = sb("KN", [P, 4, T], BF16, A)
                KR = sb("KR", [P, T], BF16, A)
                Vt = sb("Vt", [P, 18, 512], BF16, A)
                CQS = [Buf(f"cqs{i}") for i in range(5)]
                U = sb("U", [P, 18, 256], BF16, A)
                KVB = [Buf(f"kv{i}") for i in range(5)]
                B_rope = Buf()
                B_w = Buf("wmix")
                GA1 = sb("GA1", [P, 2, D], F32, A)
                WUQ = sb("WUQ", [P, 3, 1024], BF16, A)
                B_ga = Buf()

                def make_gate_bc(dst, ch0, dstB):
                    for v in range(2):
                        for j in range(8):
                            fill, fB = gfill[j % 2], gfillB[j % 2]
                            kb.op(ACT, lambda v=v, j=j, fill=fill: nc.scalar.activation(out=fill[:], in_=idf[:], func=AF.Identity,
                                                                                        bias=MOD[:, l, ch0 + j, v:v + 1], scale=0.0),
                                  reads=MODB[l][ch0 // 4:ch0 // 4 + 2] + [B_const], writes=[fB], n=128)
                            ps, psB = nextps()
                            kb.op(PE, lambda ps=ps, fill=fill: nc.tensor.matmul(ps[:, 0:128], fill[:], idf[:], start=True, stop=True),
                                  reads=[fB, B_const], writes=[psB], n=512)
                            kb.op(DVE, lambda ps=ps, v=v, j=j: nc.vector.tensor_copy(out=dst[:, v, j * 128:(j + 1) * 128], in_=ps[:, 0:128]),
                                  reads=[psB], writes=[dstB], n=128, ww=False)


                marks.append((f'L{l}pass1', PE.cnt))
                with contextlib.ExitStack() as s2:
                    A2 = s2.enter_context
                    WIN = sb("WIN", [P, 8, NWIN], BF16, A2)
                    WIN_SEGS = ((0, 640), (640, 1920))
                    B_wins = [Buf() for _ in WIN_SEGS]
                    for (a_, b_), wb in zip(WIN_SEGS, B_wins):
                        kb.dma(QP, WIN[:, :, a_:b_], win_in[l].rearrange("p (k n) -> p k n", n=NWIN)[:, :, a_:b_], writes=[wb])

                    def winB(col0, ncols):
                        return [wb for (a_, b_), wb in zip(WIN_SEGS, B_wins) if a_ < col0 + ncols and col0 < b_]
                    WUKV = sb("WUKV", [P, 2, 1024], BF16, A2)
                    kb.dma(QP, WUKV[:].rearrange("p k n -> p (k n)"), wukv_in[l], writes=[B_w])
                    ropeC = sb("ropeC", [P, 512], F32, A2)
                    ropeS = sb("ropeS", [P, 512], F32, A2)
                    kb.dma(QP, WUQ[:].rearrange("p k n -> p (k n)"), wuq_in[l], writes=[B_w])
                    make_gate_bc(GA1, 16, B_ga)
                    cqf = sb("cqf", [P, 3, 512], F32, A2)
                    cqf2 = sb("cqf2", [P, 2, 512], F32, A2)
                    cqf2B = Buf()
                    sq2 = [sb(f"sq2_{i}", [P, 512], BF16, A2) for i in range(2)]
                    sq2B = [Buf(), Buf()]
                    rsb2 = sb("rsb2", [P, 3, 512], F32, A2)
                    rsB2 = Buf()
                    cgs, cgB = cqf2, cqf2B

                    def mkrot(banks):
                        st_ = [0]

                        def r():
                            b = banks[st_[0] % len(banks)]
                            st_[0] += 1
                            return PS[b], PSB[b]
                        return r
                    rot1, rot2 = mkrot((0, 1, 2, 3)), mkrot((4, 5, 6))
                    cqo = [sb(f"cqo{i}", [P, 3, 512], BF16, A2) for i in range(2)]
                    cqoB = [Buf(), Buf()]
                    cqfB = Buf()
                    sq = [sb(f"sq{i}", [P, 512], BF16, A2) for i in range(3)]
                    sqB = [Buf() for _ in range(3)]
                    rsb = sb("rsb", [P, 3, 512], F32, A2)
                    rsB = Buf()
                    ckvn = sb("ckvn", [P, 2, 512], BF16, A2)
                    ckvnB = Buf()
                    tA = sb("tA", [P, 512], F32, A2)
                    tB_ = sb("tB", [P, 512], F32, A2)
                    tAB, tBB = Buf(), Buf()
                    vbuf = [sb(f"vbuf{i}", [P, 2, 512], F32, A2) for i in range(1)] * 2
                    vbB = [Buf()] * 2
                    bgbuf = [sb(f"bgbuf{i}", [P, 2, 512], F32, A2) for i in range(1)] * 2
                    bgbB = [Buf()] * 2
                    sqc = [0]

                    def feat_rstd(nchunks, nfeat, getsrc, n, ps_get=None, sqs=None, sqBs=None, rs=None, rsBuf=None):
                        sq_, sqB_ = (sq, sqB) if sqs is None else (sqs, sqBs)
                        rsb_, rsB_ = (rsb, rsB) if rs is None else (rs, rsBuf)
                        nsq = len(sq_)
                        pss, pssB = (ps_get or nextps)()
                        for j in range(nchunks):
                            src_ap, srcB = getsrc(j)
                            i = sqc[0] % nsq
                            sqc[0] += 1
                            kb.op(ACT, lambda src_ap=src_ap, i=i: nc.scalar.activation(out=sq_[i][:, :n], in_=src_ap, func=AF.Square),
                                  reads=[srcB], writes=[sqB_[i]])
                            kb.op(PE, lambda j=j, i=i: nc.tensor.matmul(pss[:, :n], ones[:], sq_[i][:, :n], start=(j == 0), stop=(j == nchunks - 1)),
                                  reads=[sqB_[i], B_const], writes=[pssB])
                        kb.op(ACT, lambda: nc.scalar.activation(out=rsb_[:, 1, :n], in_=pss[:, :n], func=AF.Ln, scale=1.0 / nfeat, bias=epsc[:, 0:1]),
                              reads=[pssB, B_const], writes=[rsB_])
                        kb.op(ACT, lambda: nc.scalar.activation(out=rsb_[:, 2, :n], in_=rsb_[:, 1, :n], func=AF.Exp, scale=-0.5), reads=[rsB_], writes=[rsB_])

                    def win_mm(ps, psB, col0, ncols, hT, hB, n, c_lo=0):
                        for kc in range(8):
                            kb.op(PE, lambda kc=kc: nc.tensor.matmul(ps[0:ncols, :n], WIN[:, kc, col0:col0 + ncols], hT[:, kc, c_lo:c_lo + n],
                                                                    start=(kc == 0), stop=(kc == 7)), reads=winB(col0, ncols) + hB, writes=[psB])

                    ada_next = None
                    if l + 1 < n_layers:
                        aslot1 = [sb(f"aslotn{i}", [P, 8, 512], BF16, A2) for i in range(2)]
                        ada_next = ada_gen(l + 1, range(12), aslot1, [Buf(), Buf()], (7, 7))
                        next(ada_next, None)
                    print('SBUF remaining pass1', nc.sbuf_bytes_remaining)
                    for ti, (kind, c0, n) in enumerate(TILES):
                        lat = kind == "l"
                        kvb = KVB[ti]
                        hT, hB = get_hT()
                        kb.dma(QS, hT[:, :, 0:n], h1_s[:, :, c0:c0 + n], reads=[H1B[ti]], writes=hB)
                        def T1():
                            nextps = rot1
                            for j in range(3):
                                yield
                                ps, psB = nextps()
                                win_mm(ps, psB, j * 128, 128, hT, hB, n)
                                kb.op(ACT, lambda ps=ps, j=j: nc.scalar.activation(out=cqf[:, j, :n], in_=ps[:, :n], func=AF.Copy),
                                      reads=[psB], writes=[cqfB])
                            feat_rstd(3, 384, lambda j: (cqf[:, j, :n], cqfB), n, ps_get=nextps)
                            for j in range(3):
                                kb.op(DVE, lambda j=j: nc.vector.scalar_tensor_tensor(out=cqo[ti % 2][:, j, :n], in0=cqf[:, j, :n], scalar=sm(O_QG + j),
                                                                                     in1=rsb[:, 2, :n], op0=ALU.mult, op1=ALU.mult),
                                      reads=[cqfB, rsB, B_const], writes=[cqoB[ti % 2]])
                            kb.dma(QP, cqn_s[:, :, c0:c0 + n], cqo[ti % 2][:, :, :n], reads=[cqoB[ti % 2]], writes=[CQS[ti]])
                            yield
                            psa, psaB = nextps()
                            win_mm(psa, psaB, 640, 128, hT, hB, n)
                            if lat:
                                psb_, psbB = nextps()
                                win_mm(psb_, psbB, 768, 128, hT, hB, n)
                                r0 = c0 - 256
                                kb.dma(QS, ropeC[:, :n], rope_in[0][:, r0:r0 + n], writes=[B_rope])
                                kb.dma(QS, ropeS[:, :n], rope_in[1][:, r0:r0 + n], writes=[B_rope])
                                kb.op(DVE, lambda: nc.vector.tensor_tensor(out=tA[:, :n], in0=psa[:, :n], in1=ropeC[:, :n], op=ALU.mult),
                                      reads=[psaB, B_rope], writes=[tAB])
                                kb.op(DVE, lambda: nc.vector.tensor_tensor(out=tB_[:, :n], in0=psb_[:, :n], in1=ropeS[:, :n], op=ALU.mult),
                                      reads=[psbB, B_rope], writes=[tBB])
                                kb.op(DVE, lambda: nc.vector.tensor_tensor(out=KR[:, c0:c0 + n], in0=tA[:, :n], in1=tB_[:, :n], op=ALU.add),
                                      reads=[tAB, tBB], writes=[kvb])
                            else:
                                kb.op(ACT, lambda: nc.scalar.activation(out=KR[:, c0:c0 + n], in_=psa[:, :n], func=AF.Copy), reads=[psaB], writes=[kvb])
                            for st in range(n // 128):
                                yield
                                ps, psB = nextps()
                                for kc in range(8):
                                    kb.op(PE, lambda kc=kc, st=st, ps=ps: nc.tensor.matmul(ps[:, 0:256], hT[:, kc, st * 128:(st + 1) * 128], WIN[:, kc, 896:1152],
                                                                                          start=(kc == 0), stop=(kc == 7)), reads=winB(896, 256) + hB, writes=[psB])
                                kt = c0 // 128 + st
                                if st % 2 == 1:
                                    kb.op(ACT, lambda kt=kt, ps=ps: nc.scalar.activation(out=U[:, kt, :], in_=ps[:, 0:256], func=AF.Copy), reads=[psB], writes=[kvb], ww=False, n=256)
                                else:
                                    kb.op(DVE, lambda kt=kt, ps=ps: nc.vector.tensor_copy(out=U[:, kt, :], in_=ps[:, 0:256]), reads=[psB], writes=[kvb], ww=False, n=256)

                        def T2():
                            nextps = rot2
                            for j in range(2):
                                yield
                                ps, psB = nextps()
                                win_mm(ps, psB, 384 + j * 128, 128, hT, hB, n)
                                kb.op(ACT, lambda ps=ps, j=j: nc.scalar.activation(out=cqf2[:, j, :n], in_=ps[:, :n], func=AF.Copy),
                                      reads=[psB], writes=[cqf2B])
                            feat_rstd(2, 256, lambda j: (cqf2[:, j, :n], cqf2B), n, ps_get=nextps, sqs=sq2, sqBs=sq2B, rs=rsb2, rsBuf=rsB2)
                            for j in range(2):
                                kb.op(DVE, lambda j=j: nc.vector.scalar_tensor_tensor(out=ckvn[:, j, :n], in0=cqf2[:, j, :n], scalar=sm(O_KVG + j),
                                                                                     in1=rsb2[:, 2, :n], op0=ALU.mult, op1=ALU.mult),
                                      reads=[cqf2B, rsB2, B_const], writes=[ckvnB])
                            for h in range(4):
                                yield
                                ps, psB = nextps()
                                for kc in range(2):
                                    kb.op(PE, lambda kc=kc, h=h, ps=ps: nc.tensor.matmul(ps[:, :n], WUKV[:, kc, h * 128:(h + 1) * 128], ckvn[:, kc, :n],
                                                                                        start=(kc == 0), stop=(kc == 1)), reads=[B_w, ckvnB], writes=[psB])
                                if h % 2 == 0:
                                    kb.op(ACT, lambda h=h, ps=ps: nc.scalar.activation(out=KN[:, h, c0:c0 + n], in_=ps[:, :n], func=AF.Copy),
                                          reads=[psB], writes=[kvb], ww=False)
                                else:
                                    kb.op(DVE, lambda h=h, ps=ps: nc.vector.tensor_copy(out=KN[:, h, c0:c0 + n], in_=ps[:, :n]), reads=[psB], writes=[kvb], ww=False)
                            for st in range(n // 128):
                                yield
                                ps, psB = nextps()
                                for kc in range(2):
                                    kb.op(PE, lambda kc=kc, st=st, ps=ps: nc.tensor.matmul(ps[:, :], ckvn[:, kc, st * 128:(st + 1) * 128], WUKV[:, kc, 512:1024],
                                                                                          start=(kc == 0), stop=(kc == 1)), reads=[B_w, ckvnB], writes=[psB])
                                kt = c0 // 128 + st
                                if st % 2 == 0:
                                    kb.op(ACT, lambda kt=kt, ps=ps: nc.scalar.activation(out=Vt[:, kt, :], in_=ps[:, :], func=AF.Copy), reads=[psB], writes=[kvb], ww=False)
                                else:
                                    kb.op(DVE, lambda kt=kt, ps=ps: nc.vector.tensor_copy(out=Vt[:, kt, :], in_=ps[:, :]), reads=[psB], writes=[kvb], ww=False)
                            vb, vbb = vbuf[ti % 2], vbB[ti % 2]
                            bgb, bgbb = bgbuf[ti % 2], bgbB[ti % 2]
                            for j in range(2):
                                yield
                                ps, psB = nextps()
                                win_mm(ps, psB, 1152 + j * 128, 128, hT, hB, n)
                                kb.op(ACT, lambda ps=ps, j=j: nc.scalar.activation(out=bgb[:, j, :n], in_=ps[:, :n], func=AF.Copy), reads=[psB], writes=[bgbb])
                            for j in range(2):
                                yield
                                ps, psB = nextps()
                                win_mm(ps, psB, 1408 + j * 128, 128, hT, hB, n)
                                kb.op(ACT, lambda ps=ps, j=j: nc.scalar.activation(out=cgs[:, j, :n], in_=ps[:, :n], func=AF.Copy), reads=[psB], writes=[cgB])
                            for j in range(2):
                                yield
                                ps, psB = nextps()
                                win_mm(ps, psB, 1664 + j * 128, 128, hT, hB, n)
                                kb.op(DVE, lambda ps=ps, j=j: nc.vector.tensor_tensor(out=vb[:, j, :n], in0=ps[:, :n], in1=cgs[:, j, :n], op=ALU.mult),
                                      reads=[psB, cgB], writes=[vbb])
                            kb.dma(QA, bg_s[:, :, c0:c0 + n], bgb[:, :, :n], reads=[bgbb], writes=[BGB[ti]])
                            kb.dma(QP, v_s[:, :, H2C[ti]:H2C[ti] + n], vb[:, :, :n], reads=[vbb], writes=[VSB[ti]])

                        kb.run_tasks([T1(), T2()] + ([itertools.islice(ada_next, 3)] if ada_next is not None else []))
                    kb.barrier()

                marks.append((f'L{l}pass2', PE.cnt))
                with contextlib.ExitStack() as s2:
                    A2 = s2.enter_context
                    WOUT = sb("WOUT", [P, 8, 1024], BF16, A2)
                    B_wout = Buf()
                    kb.dma(QP, WOUT[:].rearrange("p k n -> p (k n)"), wout_in[l], writes=[B_wout])
                    ropeC = sb("ropeC2", [P, 512], F32, A2)
                    ropeS = sb("ropeS2", [P, 512], F32, A2)
                    DT = [sb(f"DT{i}", [P, 2, 2, 256], BF16, A2) for i in range(3)]
                    DTB = [Buf() for _ in range(3)]
                    dtc = [0]
                    QNl = [sb(f"QN{i}", [P, 4, 512], BF16, A2) for i in range(2)]
                    QRl = [sb(f"QR{i}", [P, 2, 512], BF16, A2) for i in range(2)]
                    qBl = [Buf(), Buf()]
                    CQT = [sb(f"CQT{i}", [P, 3, 512], BF16, A2) for i in range(2)]
                    CQTB = [Buf(), Buf()]
                    tA = sb("tA2", [P, 512], F32, A2)
                    tB_ = sb("tB2", [P, 512], F32, A2)
                    tAB, tBB = Buf(), Buf()
                    rinv, rinvB = tA, tAB
                    PT = [sb(f"PT{i}", [P, 512], BF16, A2) for i in range(3)]
                    PTB = [Buf() for _ in range(3)]
                    attf = sb("attf", [P, 4, 512], F32, A2)
                    attB = Buf()
                    Yt = [sb(f"Y{i}", [P, 8, 512], BF16, A2) for i in range(2)]
                    YAB = [Buf(), Buf()]
                    YBB = [Buf(), Buf()]
                    P12 = sb("P12", [P, 4, 256], BF16, A2)
                    P12B = Buf()
                    yff = sb("yff", [P, 2, 256], F32, A2)
                    yffB = Buf()
                    vwj = sb("vwj", [P, 514], F32, A2)
                    vwB = Buf()
                    bgj = sb("bgj", [P, 512], F32, A2)
                    bgwB = Buf()
                    cacc = sb("cacc", [P, 2, 512], F32, A2)
                    caccB = Buf()
                    sqA = [sb(f"sqA{i}", [P, 512], BF16, A2) for i in range(2)]
                    sqAB = [Buf() for _ in range(2)]
                    sqBt = [sb(f"sqB{i}", [P, 512], BF16, A2) for i in range(2)]
                    sqBB = [Buf() for _ in range(2)]
                    rsA = sb("rsA", [P, 2, 512], F32, A2)
                    rsAB = Buf()
                    rsBt = sb("rsB", [P, 2, 512], F32, A2)
                    rsBB = Buf()
                    tmpx = sb("tmpx", [P, 512], F32, A2)
                    tmpxB = Buf()
                    print('SBUF remaining pass2', nc.sbuf_bytes_remaining)

                    def feat_rstd2(nchunks, nfeat, getsrc, n, bank, sqs, sqBs, rs, rsBuf, cnt_):
                        pss, pssB = PS[bank], PSB[bank]
                        for j in range(nchunks):
                            src_ap, srcB = getsrc(j)
                            i = cnt_[0] % 2
                            cnt_[0] += 1
                            kb.op(DVE, lambda src_ap=src_ap, i=i: nc.vector.tensor_tensor(out=sqs[i][:, :n], in0=src_ap, in1=src_ap, op=ALU.mult),
                                  reads=[srcB], writes=[sqBs[i]], n=n)
                            kb.op(PE, lambda j=j, i=i: nc.tensor.matmul(pss[:, :n], ones[:], sqs[i][:, :n], start=(j == 0), stop=(j == nchunks - 1)),
                                  reads=[sqBs[i], B_const], writes=[pssB], n=n)
                        kb.op(ACT, lambda: nc.scalar.activation(out=rs[:, 0, :n], in_=pss[:, :n], func=AF.Ln, scale=1.0 / nfeat, bias=epsc[:, 0:1]),
                              reads=[pssB, B_const], writes=[rsBuf])
                        kb.op(ACT, lambda: nc.scalar.activation(out=rs[:, 1, :n], in_=rs[:, 0, :n], func=AF.Exp, scale=-0.5), reads=[rsBuf], writes=[rsBuf])

                    cA, cB = [0], [0]

                    def qproj(ti, yi, b0, b1):
                        kind, c0, n = TILES[ti]
                        lat = kind == "l"
                        QN, QR, qB = QNl[yi], QRl[yi], qBl[yi]
                        cq, cqB = CQT[yi], CQTB[yi]
                        kb.dma(QS, cq[:, :, 0:n], cqn_s[:, :, c0:c0 + n], reads=[CQS[ti]], writes=[cqB])
                        if lat:
                            r0 = c0 - 256
                            kb.dma(QS, ropeC[:, :n], rope_in[0][:, r0:r0 + n], writes=[B_rope])
                            kb.dma(QS, ropeS[:, :n], rope_in[1][:, r0:r0 + n], writes=[B_rope])
                        for h in range(4):
                            ps, psB = PS[(b0, b1)[h % 2]], PSB[(b0, b1)[h % 2]]
                            for kc in range(3):
                                kb.op(PE, lambda kc=kc, h=h, ps=ps: nc.tensor.matmul(ps[:, :n], WUQ[:, kc, h * 128:(h + 1) * 128], cq[:, kc, :n],
                                                                                    start=(kc == 0), stop=(kc == 2)), reads=[B_w, cqB], writes=[psB])
                            kb.op(DVE, lambda h=h, ps=ps: nc.vector.tensor_copy(out=QN[:, h, :n], in_=ps[:, :n]), reads=[psB], writes=[qB], ww=False)
                            yield
                        for pr in range(2):
                            psa, psaB = PS[b0], PSB[b0]
                            for kc in range(3):
                                kb.op(PE, lambda kc=kc, pr=pr, psa=psa: nc.tensor.matmul(psa[:, :n], WUQ[:, kc, 512 + pr * 128:640 + pr * 128], cq[:, kc, :n],
                                                                                        start=(kc == 0), stop=(kc == 2)), reads=[B_w, cqB], writes=[psaB])
                            if lat:
                                psb_, psbB = PS[b1], PSB[b1]
                                for kc in range(3):
                                    kb.op(PE, lambda kc=kc, pr=pr, psb_=psb_: nc.tensor.matmul(psb_[:, :n], WUQ[:, kc, 768 + pr * 128:896 + pr * 128], cq[:, kc, :n],
                                                                                              start=(kc == 0), stop=(kc == 2)), reads=[B_w, cqB], writes=[psbB])
                                kb.op(DVE, lambda psa=psa: nc.vector.tensor_tensor(out=rsBt[:, 0, :n], in0=psa[:, :n], in1=ropeC[:, :n], op=ALU.mult),
                                      reads=[psaB, B_rope], writes=[rsBB], ww=False)
                                kb.op(DVE, lambda psb_=psb_: nc.vector.tensor_tensor(out=rsBt[:, 1, :n], in0=psb_[:, :n], in1=ropeS[:, :n], op=ALU.mult),
                                      reads=[psbB, B_rope], writes=[rsBB], ww=False)
                                kb.op(POOL, lambda pr=pr: nc.gpsimd.tensor_tensor(out=QR[:, pr, :n], in0=rsBt[:, 0, :n], in1=rsBt[:, 1, :n], op=ALU.add),
                                      reads=[rsBB], writes=[qB], ww=False)
                            else:
                                kb.op(DVE, lambda pr=pr, psa=psa: nc.vector.tensor_copy(out=QR[:, pr, :n], in_=psa[:, :n]), reads=[psaB], writes=[qB])
                            yield

                    def taskA(ti, yi):
                        kind, c0, n = TILES[ti]
                        lat = kind == "l"
                        nk = 18 if lat else 2
                        allkv = KVB if lat else KVB[0:1]
                        QN, QR, qB = QNl[yi], QRl[yi], qBl[yi]
                        for h in range(4):
                            kp = (h % 2) * 64

                            def scores(kt, h=h, kp=kp):
                                b = kt % 2
                                kb.op(PE, lambda: nc.tensor.matmul(PS[b][:, :n], KN[:, h, kt * 128:(kt + 1) * 128], QN[:, h, :n], start=True, stop=False),
                                      reads=allkv + [qB], writes=[PSB[b]])
                                kb.op(PE, lambda: nc.tensor.matmul(PS[b][:, :n], KR[kp:kp + 64, kt * 128:(kt + 1) * 128], QR[kp:kp + 64, h // 2, :n],
                                                                   start=False, stop=True), reads=allkv + [qB], writes=[PSB[b]])

                            def expo(kt):
                                b = kt % 2
                                kb.op(ACT, lambda: nc.scalar.activation(out=PT[kt % 3][:, :n], in_=PS[b][:, :n], func=AF.Exp, scale=SM_SCALE),
                                      reads=[PSB[b]], writes=[PTB[kt % 3]])

                            def pv(kt, h=h):
                                b = kt % 3
                                kb.op(PE, lambda: nc.tensor.matmul(PS[2][:, :n], Vt[:, kt, h * 128:(h + 1) * 128], PT[b][:, :n], start=(kt == 0), stop=(kt == nk - 1)),
                                      reads=allkv + [PTB[b]], writes=[PSB[2]])
                                kb.op(PE, lambda: nc.tensor.matmul(PS[3][:, :n], ones[:], PT[b][:, :n], start=(kt == 0), stop=(kt == nk - 1)),
                                      reads=[B_const, PTB[b]], writes=[PSB[3]])

                            scores(0)
                            for kt in range(nk):
                                expo(kt)
                                if kt + 1 < nk:
                                    scores(kt + 1)
                                pv(kt)
                                yield
                            kb.op(ACT, lambda: nc.scalar.activation(out=tB_[:, :n], in_=PS[3][:, :n], func=AF.Ln), reads=[PSB[3]], writes=[tBB])
                            kb.op(ACT, lambda: nc.scalar.activation(out=rinv[:, :n], in_=tB_[:, :n], func=AF.Exp, scale=-1.0), reads=[tBB], writes=[rinvB])
                            kb.op(DVE, lambda h=h: nc.vector.tensor_copy(out=attf[:, h, :n], in_=PS[2][:, :n]), reads=[PSB[2]], writes=[attB])
                            kb.op(DVE, lambda h=h: nc.vector.tensor_tensor(out=attf[:, h, :n], in0=attf[:, h, :n], in1=rinv[:, :n], op=ALU.mult),
                                  reads=[attB, rinvB], writes=[attB])

                    def taskB(ti, yi, nxt=None):
                        kind, c0, n = TILES[ti]
                        lat = kind == "l"
                        allkv = KVB if lat else KVB[0:1]
                        Y, YB = Yt[yi], YBB[yi]
                        nti = 16 if lat else 2
                        scale = (nti * 128 * 64) ** -0.5
                        for kh in range(n // 256):
                            k0 = kh * 256
                            for ti2 in range(nti):
                                if lat:
                                    if ti2 % 2 == 0:
                                        di = dtc[0] % 3
                                        dtc[0] += 1
                                        kb.dma(QS, DT[di][:].rearrange("p a b c -> p (a b c)"), dftl_in[(ti - 1) * 2 + kh, ti2 // 2], writes=[DTB[di]])
                                    tab = lambda di=di, ti2=ti2: DT[di][:, ti2 % 2, :, :].rearrange("p a b -> p (a b)")
                                    tabB = DTB[di]
                                    ug = 2 + ti2
                                else:
                                    tab = lambda ti2=ti2: dftc[:, ti2, :, :].rearrange("p a b -> p (a b)")
                                    tabB = B_const
                                    ug = ti2
                                for f in range(2):
                                    kb.op(PE, lambda f=f, tab=tab, ug=ug, ti2=ti2: nc.tensor.matmul(
                                        PS[4 + f][:, :], U[:, ug, f * 128:(f + 1) * 128], tab(), start=(ti2 == 0), stop=(ti2 == nti - 1)),
                                        reads=[tabB] + allkv, writes=[PSB[4 + f]])
                                yield
                            for f in range(2):
                                src = PS[4 + f][:, :]
                                dst = P12[:, 2 * f:2 * f + 2, :].rearrange("p a b -> p (a b)")
                                if False:
                                    pass
                                else:
                                    kb.op(DVE, lambda src=src, dst=dst: nc.vector.tensor_scalar(out=dst, in0=src, scalar1=scale, scalar2=0.0, op0=ALU.mult, op1=ALU.add),
                                          reads=[PSB[4 + f]], writes=[P12B])
                            for f in range(2):
                                ps, psB = PS[4 + f], PSB[4 + f]
                                kb.op(PE, lambda f=f, ps=ps: nc.tensor.matmul(ps[:, 0:256], bcs[:, 0:128], P12[:, 2 * f, :], start=True, stop=False),
                                      reads=[P12B, B_const], writes=[psB])
                                kb.op(PE, lambda f=f, ps=ps: nc.tensor.matmul(ps[:, 0:256], bcs[:, 128:256], P12[:, 2 * f + 1, :], start=False, stop=True),
                                      reads=[P12B, B_const], writes=[psB])
                                kb.op(DVE, lambda f=f, ps=ps: nc.vector.tensor_copy(out=yff[:, f, :], in_=ps[:, 0:256]), reads=[psB], writes=[yffB], n=256)
                            yield
                            feat_rstd2(2, 256, lambda j: (yff[:, j, :], yffB), 256, 4, sqBt, sqBB, rsBt, rsBB, cB)
                            for f in range(2):
                                kb.op(DVE, lambda f=f: nc.vector.scalar_tensor_tensor(out=Y[:, 4 + f, k0:k0 + 256], in0=yff[:, f, :], scalar=sm(O_OG + 4 + f),
                                                                                     in1=rsBt[:, 1, 0:256], op0=ALU.mult, op1=ALU.mult),
                                      reads=[yffB, rsBB, B_const], writes=[YB])
                            yield
                        first = ti in (0, 1)
                        lastt = ti in (0, 4)
                        nb = [VSB[ti], PADB] + ([] if first else [VSB[ti - 1]]) + ([] if lastt else [VSB[ti + 1]])
                        lo = 1 if first else 0
                        hi = n - 1 if lastt else n
                        for j in range(2):
                            kb.dma(QS, vwj[:, 0:n + 2], v_s[:, j, H2C[ti] - 1:H2C[ti] + n + 1], reads=nb, writes=[vwB])
                            kb.dma(QS, bgj[:, 0:n], bg_s[:, j, c0:c0 + n], reads=[BGB[ti]], writes=[bgwB])
                            kb.op(POOL, lambda j=j: nc.gpsimd.tensor_scalar(out=cacc[:, j, :n], in0=vwj[:, 1:n + 1], scalar1=sm(O_SW + 3 * j + 1),
                                                                            scalar2=sm(O_SB + j), op0=ALU.mult, op1=ALU.add),
                                  reads=[vwB, B_const], writes=[caccB])
                            kb.op(DVE, lambda j=j: nc.vector.scalar_tensor_tensor(out=cacc[:, j, lo:n], in0=vwj[:, lo:n], scalar=sm(O_SW + 3 * j),
                                                                                  in1=cacc[:, j, lo:n], op0=ALU.mult, op1=ALU.add),
                                  reads=[vwB, B_const, caccB], writes=[caccB])
                            kb.op(DVE, lambda j=j: nc.vector.scalar_tensor_tensor(out=cacc[:, j, 0:hi], in0=vwj[:, 2:hi + 2], scalar=sm(O_SW + 3 * j + 2),
                                                                                  in1=cacc[:, j, 0:hi], op0=ALU.mult, op1=ALU.add),
                                  reads=[vwB, B_const, caccB], writes=[caccB])
                            kb.op(POOL, lambda j=j: nc.gpsimd.tensor_tensor(out=cacc[:, j, :n], in0=cacc[:, j, :n], in1=bgj[:, :n], op=ALU.mult),
                                  reads=[caccB, bgwB], writes=[caccB])
                            yield
                        feat_rstd2(2, 256, lambda j: (cacc[:, j, :n], caccB), n, 5, sqBt, sqBB, rsBt, rsBB, cB)
                        yield
                        for j in range(2):
                            kb.op(DVE, lambda j=j: nc.vector.scalar_tensor_tensor(out=Y[:, 6 + j, :n], in0=cacc[:, j, :n], scalar=sm(O_OG + 6 + j),
                                                                                 in1=rsBt[:, 1, :n], op0=ALU.mult, op1=ALU.mult),
                                  reads=[caccB, rsBB, B_const], writes=[YB])
                        yield

                        if nxt is not None:
                            yield from qproj(nxt[0], nxt[1], 4, 5)

                    def taskC(ti, yi):
                        kind, c0, n = TILES[ti]
                        v = 0 if kind == "l" else 1
                        Y = Yt[yi]
                        yb = [YAB[yi], YBB[yi]]
                        hT, hB = get_hT()
                        feat_rstd2(4, 512, lambda j: (attf[:, j, :n], attB), n, 6, sqA, sqAB, rsA, rsAB, cA)
                        yield
                        for h in range(4):
                            kb.op(DVE, lambda h=h: nc.vector.scalar_tensor_tensor(out=Y[:, h, :n], in0=attf[:, h, :n], scalar=sm(O_OG + h),
                                                                                 in1=rsA[:, 1, :n], op0=ALU.mult, op1=ALU.mult),
                                  reads=[attB, rsAB, B_const], writes=[YAB[yi]])
                        yield

                        pendc = None
                        for st in range(n // 128):
                            sti = c0 // 128 + st
                            xt, xB = get_xt()
                            if l == 0:
                                src0 = ctx_in[sti * 128:(sti + 1) * 128, :] if sti < 2 else x_in[(sti - 2) * 128:(sti - 1) * 128, :]
                            else:
                                src0 = res_ap(sti)
                            kb.dma(QS, xt[:], src0, reads=[RES[sti]], writes=[xB])
                            for hf in range(2):
                                ps, psB = PS[6 + hf], PSB[6 + hf]
                                for kc in range(8):
                                    kb.op(PE, lambda kc=kc, st=st, hf=hf, ps=ps: nc.tensor.matmul(ps[:, :], Y[:, kc, st * 128:(st + 1) * 128], WOUT[:, kc, hf * 512:(hf + 1) * 512],
                                                                                                 start=(kc == 0), stop=(kc == 7)), reads=yb + [B_wout], writes=[psB])
                                kb.op(DVE, lambda hf=hf, ps=ps: nc.vector.tensor_tensor(out=tmpx[:], in0=ps[:, :], in1=GA1[:, v, hf * 512:(hf + 1) * 512], op=ALU.mult),
                                      reads=[psB, B_ga], writes=[tmpxB])
                                kb.op(POOL, lambda hf=hf, xt=xt: nc.gpsimd.tensor_tensor(out=xt[:, hf * 512:(hf + 1) * 512], in0=xt[:, hf * 512:(hf + 1) * 512], in1=tmpx[:], op=ALU.add),
                                      reads=[tmpxB, xB], writes=[xB])
                                yield
                            kb.dma(QP, res_ap(sti), xt[:], reads=[xB], writes=[RES[sti]])
                            newp = norm_to_hT(xt, xB, l, 1, v, hT, hB, st, bank=6 + (st % 2), evac="dve")
                            for _ in range(3):
                                next(newp)
                                yield
                            if pendc is not None:
                                yield from pendc
                            pendc = newp
                            yield
                        if pendc is not None:
                            yield from pendc
                        kb.dma(QA, h2_s[:, :, H2C[ti]:H2C[ti] + n], hT[:, :, 0:n], reads=hB, writes=[H2B[ti]])

                    p2tiles = [1, 2, 3, 4] + ([] if last else [0])
                    prev = None
                    kb.run_tasks([qproj(p2tiles[0], 0, 4, 5)])
                    for idx, ti in enumerate(p2tiles):
                        nxt = (p2tiles[idx + 1], (idx + 1) % 2) if idx + 1 < len(p2tiles) else None
                        tasks = [taskA(ti, idx % 2), taskB(ti, idx % 2, nxt)]
                        if prev is not None:
                            tasks.append(taskC(prev[0], prev[1]))
                        kb.run_tasks(tasks)
                        prev = (ti, idx % 2)
                    kb.run_tasks([taskC(prev[0], prev[1])])
                    kb.barrier()

            marks.append((f'L{l}ffn', PE.cnt))
            with contextlib.ExitStack() as s1:
                A = s1.enter_context
                GA2 = sb("GA2", [P, 2, D], F32, A)
                B_ga2 = Buf()
                WU = [sb(f"WU{i}", [P, 8, 512], BF16, A) for i in range(2)]
                WUB = [Buf(), Buf()]
                WD = sb("WD", [P, 2, NCF, 512], BF16, A)
                WDB = [Buf(), Buf()]
                G = sb("G", [P, NCF, 1280], BF16, A)
                GB_ = Buf()
                h2w = sb("h2w", [P, 8, 1280 + 6], BF16, A)
                h2wBs = {t: Buf() for t in range(5)}
                gb = [sb(f"gb{i}", [P, 512], F32, A) for i in range(2)]
                gbB = [Buf(), Buf()]
                sg = [sb(f"sg{i}", [P, 512], F32, A) for i in range(2)]
                sgB = [Buf(), Buf()]
                tmpx = sb("tmpx3", [P, 512], F32, A)
                tmpxB = Buf()
                xt_t.append(sb("xt_extra", [P, D], F32, A))
                xtB.append(Buf())
                print('SBUF remaining ffn', nc.sbuf_bytes_remaining)
                supers = [[t for t in (0, 1, 2) if not (last and t == 0)], [3, 4]]
                wuc = [0]
                upc = [0]
                up_defer = [None]
                def load_windows(sup, prev=()):
                    offs, goff = {}, {}
                    o_, g_ = 0, 0
                    for t in sup:
                        kind, c0, n = TILES[t]
                        offs[t], goff[t] = o_, g_
                        first = t in (0, 1)
                        lastt = t in (0, 4)
                        nb = [H2B[t], PADB] + ([] if first else [H2B[t - 1]]) + ([] if lastt else [H2B[t + 1]])
                        kb.dma(QS, h2w[:, :, o_:o_ + n + 2], h2_s[:, :, H2C[t] - 1:H2C[t] + n + 1], reads=nb, writes=[h2wBs[t]] + [h2wBs[u] for u in prev])
                        o_ += n + 2
                        g_ += n
                    return offs, goff

                win_next = load_windows(supers[0])
                for g in range(2):
                    kb.dma(QP, WU[g][:].rearrange("p k n -> p (k n)"), wup_in[l, g], writes=[WUB[g]])
                make_gate_bc(GA2, 40, B_ga2)
                for si, sup in enumerate(supers):
                    offs, goff = win_next
                    for g in range(11):
                        wi = wuc[0] % 2
                        wuc[0] += 1
                        if not (si == 0 and g < 2):
                            kb.dma(QP, WU[wi][:].rearrange("p k n -> p (k n)"), wup_in[l, g], writes=[WUB[wi]])
                        if si == 0 and g == 1:
                            for hf in range(2):
                                kb.dma(QP, WD[:, hf].rearrange("p c n -> p (c n)"), wdn_in[l, hf], writes=[WDB[hf]])
                        for cl in range(2):
                            c = 2 * g + cl
                            for t in sup:
                                kind, c0, n = TILES[t]
                                nh = n // 2
                                o_ = offs[t]
                                first = t in (0, 1)
                                lastt = t in (0, 4)
                                bi = upc[0] % 2
                                vbk = (2, 5, 6, 7)[upc[0] % 4]
                                upc[0] += 1
                                gA, gAB = PS[3 * bi], PSB[3 * bi]
                                gBk, gBB = PS[3 * bi + 1], PSB[3 * bi + 1]
                                vv, vvB = PS[vbk], PSB[vbk]
                                for kc in range(8):
                                    wg = WU[wi][:, kc, cl * 256:cl * 256 + 128]
                                    kb.op(PE, lambda kc=kc, wg=wg: nc.tensor.matmul(gA[:, 0:nh + 2], wg, h2w[:, kc, o_:o_ + nh + 2], start=(kc == 0), stop=(kc == 7)),
                                          reads=[WUB[wi], h2wBs[t]], writes=[gAB])
                                for kc in range(8):
                                    wg = WU[wi][:, kc, cl * 256:cl * 256 + 128]
                                    kb.op(PE, lambda kc=kc, wg=wg: nc.tensor.matmul(gBk[:, 0:nh + 2], wg, h2w[:, kc, o_ + nh:o_ + n + 2], start=(kc == 0), stop=(kc == 7)),
                                          reads=[WUB[wi], h2wBs[t]], writes=[gBB])
                                for kc in range(8):
                                    wv = WU[wi][:, kc, cl * 256 + 128:cl * 256 + 256]
                                    kb.op(PE, lambda kc=kc, wv=wv: nc.tensor.matmul(vv[:, 0:n], wv, h2w[:, kc, o_ + 1:o_ + n + 1], start=(kc == 0), stop=(kc == 7)),
                                          reads=[WUB[wi], h2wBs[t]], writes=[vvB])
                                gbuf, gbb = gb[bi], gbB[bi]
                                w0, w1, w2, bb = sm(O_FW + 3 * c), sm(O_FW + 3 * c + 1), sm(O_FW + 3 * c + 2), sm(O_FB + c)
                                kb.op(ACT, lambda: nc.scalar.activation(out=gbuf[:, 0:nh], in_=gA[:, 1:nh + 1], func=AF.Identity, bias=bb, scale=w1),
                                      reads=[gAB, B_const], writes=[gbb])
                                kb.op(ACT, lambda: nc.scalar.activation(out=gbuf[:, nh:n], in_=gBk[:, 1:nh + 1], func=AF.Identity, bias=bb, scale=w1),
                                      reads=[gBB, B_const], writes=[gbb])
                                lo = 1 if first else 0
                                hi = nh - 1 if lastt else nh
                                kb.op(DVE, lambda: nc.vector.scalar_tensor_tensor(out=gbuf[:, lo:nh], in0=gA[:, lo:nh], scalar=w0, in1=gbuf[:, lo:nh], op0=ALU.mult, op1=ALU.add),
                                      reads=[gAB, B_const, gbb], writes=[gbb])
                                kb.op(DVE, lambda: nc.vector.scalar_tensor_tensor(out=gbuf[:, nh:n], in0=gBk[:, 0:nh], scalar=w0, in1=gbuf[:, nh:n], op0=ALU.mult, op1=ALU.add),
                                      reads=[gBB, B_const, gbb], writes=[gbb])
                                kb.op(DVE, lambda: nc.vector.scalar_tensor_tensor(out=gbuf[:, 0:nh], in0=gA[:, 2:nh + 2], scalar=w2, in1=gbuf[:, 0:nh], op0=ALU.mult, op1=ALU.add),
                                      reads=[gAB, B_const, gbb], writes=[gbb])
                                kb.op(DVE, lambda: nc.vector.scalar_tensor_tensor(out=gbuf[:, nh:nh + hi], in0=gBk[:, 2:hi + 2], scalar=w2, in1=gbuf[:, nh:nh + hi], op0=ALU.mult, op1=ALU.add),
                                      reads=[gBB, B_const, gbb], writes=[gbb])
                                sgb, sgbb = sg[bi], sgB[bi]
                                go = goff[t]

                                def part2(gbuf=gbuf, gbb=gbb, sgb=sgb, sgbb=sgbb, vv=vv, vvB=vvB, c=c, go=go, n=n):
                                    kb.op(ACT, lambda: nc.scalar.activation(out=sgb[:, 0:n], in_=gbuf[:, 0:n], func=AF.Silu), reads=[gbb], writes=[sgbb])
                                    kb.op(DVE, lambda: nc.vector.tensor_tensor(out=G[:, c, go:go + n], in0=vv[:, 0:n], in1=sgb[:, 0:n], op=ALU.mult),
                                          reads=[vvB, sgbb], writes=[GB_], ww=False)
                                if up_defer[0] is not None:
                                    up_defer[0]()
                                up_defer[0] = part2
                    if up_defer[0] is not None:
                        up_defer[0]()
                        up_defer[0] = None
                    if si + 1 < len(supers):
                        win_next = load_windows(supers[si + 1], prev=sup)
                    marks.append((f'L{l}down', PE.cnt))
                    for t in sup:
                        kind, c0, n = TILES[t]
                        v = 1 if kind == "c" else 0
                        if not last:
                            hT, hB = get_hT()
                        pending = None
                        for st in range(n // 128):
                            sti = c0 // 128 + st
                            xt, xB = get_xt()
                            kb.dma(QS, xt[:], res_ap(sti), reads=[RES[sti]], writes=[xB])
                            go = goff[t] + st * 128
                            for hf in range(2):
                                ps, psB = nextps()
                                for c in range(NCF):
                                    kb.op(PE, lambda c=c, hf=hf, ps=ps, go=go: nc.tensor.matmul(ps[:, :], G[:, c, go:go + 128], WD[:, hf, c, :], start=(c == 0), stop=(c == NCF - 1)),
                                          reads=[GB_, WDB[hf]], writes=[psB])
                                kb.op(DVE, lambda hf=hf, ps=ps: nc.vector.tensor_tensor(out=tmpx[:], in0=ps[:, :], in1=GA2[:, v, hf * 512:(hf + 1) * 512], op=ALU.mult),
                                      reads=[psB, B_ga2], writes=[tmpxB])
                                kb.op(DVE, lambda hf=hf, xt=xt: nc.vector.tensor_tensor(out=xt[:, hf * 512:(hf + 1) * 512], in0=xt[:, hf * 512:(hf + 1) * 512], in1=tmpx[:], op=ALU.add),
                                      reads=[tmpxB, xB], writes=[xB])
                            if last:
                                r0, r0B = tok_sumsq(xt, xB)
                                r, rB = rstd_from(r0[:, 0:1], r0B, D)
                                kb.op(DVE, lambda xt=xt, r=r: nc.vector.scalar_tensor_tensor(out=xt[:], in0=xt[:], scalar=r[:, 3:4], in1=fg[:], op0=ALU.mult, op1=ALU.mult),
                                      reads=[xB, rB, B_const], writes=[xB])
                                kb.dma(QA, res_ap(sti), xt[:], reads=[xB], writes=[RES[sti]])
                            else:
                                kb.dma(QA, res_ap(sti), xt[:], reads=[xB], writes=[RES[sti]])
                                newp = norm_to_hT(xt, xB, l + 1, 0, v, hT, hB, st, evac="act")
                                for _ in range(3):
                                    next(newp)
                                if pending is not None:
                                    for _ in pending:
                                        pass
                                pending = newp
                        if not last:
                            if pending is not None:
                                for _ in pending:
                                    pass
                                pending = None
                            kb.dma(QA, h1_s[:, :, c0:c0 + n], hT[:, :, 0:n], reads=hB, writes=[H1B[t]])
                xt_t.pop()
                xtB.pop()
                kb.barrier()
        kb.barrier()
        marks.append(('end', PE.cnt))
        build_nc.marks = marks
    return nc


def _wP(w):
    K = w.shape[0] // 128
    return np.ascontiguousarray(w.reshape(K, 128, -1).transpose(1, 0, 2).reshape(128, -1))


def _chunkP(v):
    return np.ascontiguousarray(v.reshape(-1, 128).T)


_CONST_CACHE = {}


def _constants():
    if _CONST_CACHE:
        return _CONST_CACHE
    bf = ml_dtypes.bfloat16
    t = np.arange(S)
    r, c = (t // 64).astype(np.float64), (t % 64).astype(np.float64)
    inv = 10000.0 ** (-np.arange(0, 32, 2, dtype=np.float64) / 32)
    d = np.arange(64)
    a, j, i = d // 32, (d % 32) // 16, d % 16
    pos = np.where(a[:, None] == 0, r[None, :], c[None, :])
    ang = (pos.astype(np.float32) * inv.astype(np.float32)[i][:, None]).astype(np.float32).astype(np.float64)
    cosT = np.cos(ang)
    sinT = np.sin(ang) * np.where(j == 0, -1.0, 1.0)[:, None]
    rope = np.stack([np.concatenate([cosT, cosT], 0), np.concatenate([sinT, sinT], 0)]).astype(np.float32)
    tt = (np.arange(8)[:, None, None] * 256 + np.arange(2)[None, :, None] * 128 + np.arange(128)[None, None, :])
    dftl = np.empty((8, 8, 128, 2, 2, 256), dtype=bf)
    for kt in range(8):
        kk = kt * 256 + np.arange(256)
        ph = (tt[..., None] * kk[None, None, None, :]) % S
        angl = ph.astype(np.float64) * (2 * np.pi / S)
        dftl[kt, :, :, :, 0, :] = np.cos(angl).transpose(0, 2, 1, 3).astype(bf)
        dftl[kt, :, :, :, 1, :] = np.sin(angl).transpose(0, 2, 1, 3).astype(bf)
    dftl = dftl.reshape(8, 8, 128, 2 * 2 * 256)
    tc_ = np.arange(2)[:, None] * 128 + np.arange(128)[None, :]
    phc = (tc_[..., None] * np.arange(256)[None, None, :]) % CT
    angc = phc.astype(np.float64) * (2 * np.pi / CT)
    dftc = np.stack([np.cos(angc), np.sin(angc)], axis=2).transpose(1, 0, 2, 3).astype(bf).reshape(128, 2 * 2 * 256)
    m = np.arange(64)
    a64 = ((m[:, None] * m[None, :]) % 64).astype(np.float64) * (2 * np.pi / 64)
    bc = np.zeros((128, 128))
    bs = np.zeros((128, 128))
    for g in range(2):
        bc[g * 64:(g + 1) * 64, g * 64:(g + 1) * 64] = np.cos(a64)
        bs[g * 64:(g + 1) * 64, g * 64:(g + 1) * 64] = -np.sin(a64)
    bcs = np.concatenate([bc, bs], 1).astype(bf)
    _CONST_CACHE.update(rope=rope, dftl=dftl, dftc=dftc, bcs=bcs, idb=np.eye(128).astype(bf), idf=np.eye(128, dtype=np.float32))
    return _CONST_CACHE


def _prep_shared(inp):
    f = lambda k: np.asarray(inp[k], dtype=np.float32)
    d64 = np.arange(64)
    sw = np.where(d64 % 32 < 16, d64 + 16, d64 - 16)
    kr = 640 + d64
    krs = 640 + sw
    win_idx = np.concatenate([np.arange(0, 640), kr, kr, krs, krs, np.arange(704, 1728)])
    up_idx = np.concatenate([np.concatenate([np.arange(c * 128, (c + 1) * 128), DFF + np.arange(c * 128, (c + 1) * 128)]) for c in range(NCF)])
    smalls = np.zeros((DEPTH, 128, NSM), np.float32)
    ada = np.empty((DEPTH, 12, 128, 8 * 512), np.float32)
    win = np.empty((DEPTH, 128, 8 * NWIN), np.float32)
    wuq = np.empty((DEPTH, 128, 3 * 1024), np.float32)
    wukv = np.empty((DEPTH, 128, 2 * 1024), np.float32)
    wout = np.empty((DEPTH, 128, 8 * 1024), np.float32)
    wup = np.empty((DEPTH, 11, 128, 8 * 512), np.float32)
    wdn = np.empty((DEPTH, 2, 128, NCF * 512), np.float32)
    for l in range(DEPTH):
        s = smalls[l]
        s[:, O_G1:O_G1 + 8] = _chunkP(f("norm1_g")[l])
        s[:, O_QG:O_QG + 3] = _chunkP(f("q_norm_g")[l])
        s[:, O_KVG:O_KVG + 2] = _chunkP(f("kv_norm_g")[l])
        sw_ = f("sconv_w")[l]
        for j in range(2):
            for tap in range(3):
                s[:, O_SW + 3 * j + tap] = sw_[tap, j * 128:(j + 1) * 128]
        s[:, O_SB:O_SB + 2] = _chunkP(f("sconv_b")[l])
        s[:, O_OG:O_OG + 8] = _chunkP(f("out_norm_g")[l])
        s[:, O_G2:O_G2 + 8] = _chunkP(f("norm2_g")[l])
        fw = f("ffconv_w")[l]
        for c in range(NCF):
            for tap in range(3):
                s[:, O_FW + 3 * c + tap] = fw[tap, c * 128:(c + 1) * 128]
        s[:, O_FB:O_FB + NCF] = _chunkP(f("ffconv_b")[l])
        s[:, O_AB:O_AB + 48] = _chunkP(f("ada_b")[l])
        aw = f("ada_w")[l]
        for g in range(12):
            ada[l, g] = _wP(aw[:, g * 512:(g + 1) * 512])
        win[l] = _wP(f("w_in")[l][:, win_idx])
        uq = f("w_uq")[l]
        cols = [uq[:, h, 0:128] for h in range(4)]
        cols += [np.concatenate([uq[:, 2 * pr, 128:192], uq[:, 2 * pr + 1, 128:192]], 1) for pr in range(2)]
        cols += [np.concatenate([uq[:, 2 * pr, 128 + sw], uq[:, 2 * pr + 1, 128 + sw]], 1) for pr in range(2)]
        wuq[l] = _wP(np.concatenate(cols, 1))
        ukv = f("w_ukv")[l]
        wukv[l] = _wP(np.concatenate([ukv[:, h, 0:128] for h in range(4)] + [ukv[:, h, 128:256] for h in range(4)], 1))
        wout[l] = _wP(f("w_out")[l])
        upp = f("w_up")[l][:, up_idx]
        for g in range(11):
            wup[l, g] = _wP(upp[:, g * 512:(g + 1) * 512])
        dn = f("w_down")[l]
        for hf in range(2):
            wdn[l, hf] = _wP(dn[:, hf * 512:(hf + 1) * 512])
    fg = np.ascontiguousarray(np.broadcast_to(f("final_g")[None, :], (128, D)))
    return dict(smalls=smalls, ada=ada, win=win, wuq=wuq, wukv=wukv, wout=wout, wup=wup, wdn=wdn, fg=fg)


def kernel(**inputs):
    consts = _constants()
    shared = _prep_shared(inputs)
    x = np.asarray(inputs["x"], np.float32)
    c = np.asarray(inputs["c"], np.float32)
    ctx = np.asarray(inputs["ctx"], np.float32)
    c_ctx = np.asarray(inputs["c_ctx"], np.float32)
    nc = build_nc()
    in_maps = []
    for b in range(8):
        cv = np.stack([_chunkP(c[b]), _chunkP(c_ctx)], axis=2).reshape(128, 16)
        m = dict(x=np.ascontiguousarray(x[b]), ctx=np.ascontiguousarray(ctx[b]), cvec=np.ascontiguousarray(cv))
        m.update(shared)
        m.update(consts)
        in_maps.append(m)
    res = run_bass_kernel_spmd(nc, in_maps, core_ids=list(range(8)))
    return np.stack([np.asarray(r["out"], dtype=np.float32) for r in res.results], axis=0)
```
